# Optimizing a Trainium2 kernel written in Bass

```python
import jax, jax.numpy as jnp
from jax import lax
import numpy as np

D_MODEL = 2048
BATCH = 8
SEQ = 2048
DEPTH = 4

CHUNK = 64
QBLK = 128
EPS = 1e-6
A_HEADS = 8
A_NOPE = 128
A_ROPE = 64
A_VDIM = 128
A_QLORA = 512
A_KVLORA = 256
ROPE_THETA = 10000.0
B_HEADS = 8
B_HDIM = 128
B_PREV = 8
B_BAND = (B_PREV + 1) * CHUNK
REL_CLIP = 128
SG_WIDTH = D_MODEL
SG_GROUPS = 8
SG_LEN = 128
A_WIDTH = A_HEADS * A_VDIM
B_WIDTH = B_HEADS * B_HDIM
AB_WIDTH = A_WIDTH + B_WIDTH
AB_IN = A_QLORA + A_KVLORA + A_ROPE + 3 * B_WIDTH + AB_WIDTH
N_EVEN = (DEPTH + 1) // 2
N_ODD = DEPTH // 2

kernel_name = "hybrid_mla_bandattn_gmlp_sandwich_adaln"


def rmsnorm(x, g):
    xf = x.astype(jnp.float32)
    y = xf * lax.rsqrt(jnp.mean(xf * xf, axis=-1, keepdims=True) + EPS)
    return y.astype(x.dtype) * g


def layernorm(x, g, b):
    xf = x.astype(jnp.float32)
    mu = jnp.mean(xf, axis=-1, keepdims=True)
    var = jnp.mean(jnp.square(xf - mu), axis=-1, keepdims=True)
    return ((xf - mu) * lax.rsqrt(var + EPS)).astype(x.dtype) * g + b


def rope(x, pos):
    half = x.shape[-1] // 2
    freqs = ROPE_THETA ** (-jnp.arange(half, dtype=jnp.float32) / half)
    ang = pos[:, None] * freqs[None, :]
    cos = jnp.cos(ang)[:, None, :].astype(x.dtype)
    sin = jnp.sin(ang)[:, None, :].astype(x.dtype)
    x1, x2 = x[..., :half], x[..., half:]
    return jnp.concatenate([x1 * cos - x2 * sin, x1 * sin + x2 * cos], axis=-1)


def chunk_causal_attention(q, k, v):
    bsz, s_len, h, dk = q.shape
    nb = s_len // QBLK
    scale = dk ** -0.5
    qb = q.reshape(bsz, nb, QBLK, h, dk).swapaxes(0, 1)
    kchunk = jnp.arange(s_len) // CHUNK

    def one(args):
        qi, i = args
        s = jnp.einsum('bqhd,bkhd->bhqk', qi, k, preferred_element_type=jnp.float32) * scale
        qchunk = (i * QBLK + jnp.arange(QBLK)) // CHUNK
        mask = kchunk[None, :] <= qchunk[:, None]
        p = jax.nn.softmax(jnp.where(mask, s, -jnp.inf), axis=-1)
        return jnp.einsum('bhqk,bkhd->bqhd', p.astype(v.dtype), v)

    o = lax.map(one, (qb, jnp.arange(nb)))
    return o.swapaxes(0, 1).reshape(bsz, s_len, h, v.shape[-1])


def chunk_band_attention(q, k, v, rel_table):
    bsz, s_len, h, d = q.shape
    nc = s_len // CHUNK
    scale = d ** -0.5
    pad = ((0, 0), (B_PREV * CHUNK, 0), (0, 0), (0, 0))
    kp = jnp.pad(k, pad)
    vp = jnp.pad(v, pad)
    rel = (B_PREV * CHUNK + jnp.arange(CHUNK))[:, None] - jnp.arange(B_BAND)[None, :]
    bias = rel_table[:, jnp.clip(rel, -REL_CLIP, REL_CLIP) + REL_CLIP].astype(jnp.float32)

    def one(i):
        qi = lax.dynamic_slice_in_dim(q, i * CHUNK, CHUNK, axis=1)
        kb = lax.dynamic_slice_in_dim(kp, i * CHUNK, B_BAND, axis=1)
        vb = lax.dynamic_slice_in_dim(vp, i * CHUNK, B_BAND, axis=1)
        s = jnp.einsum('bqhd,bkhd->bhqk', qi, kb, preferred_element_type=jnp.float32) * scale + bias[None]
        valid = (i - B_PREV) * CHUNK + jnp.arange(B_BAND) >= 0
        p = jax.nn.softmax(jnp.where(valid[None, None, None, :], s, -jnp.inf), axis=-1)
        return jnp.einsum('bhqk,bkhd->bqhd', p.astype(vb.dtype), vb)

    o = lax.map(one, jnp.arange(nc))
    return o.swapaxes(0, 1).reshape(bsz, s_len, h, d)


def mla_band_mixer(h, w_in, g_q, w_uq, g_kv, w_ukv, rel_table, w_out, pos):
    bsz, s_len, _ = h.shape
    z = h @ w_in
    o1 = A_QLORA
    o2 = o1 + A_KVLORA
    o3 = o2 + A_ROPE
    o4 = o3 + B_WIDTH
    o5 = o4 + B_WIDTH
    o6 = o5 + B_WIDTH
    cq, ckv, kr, bq, bk, bv, gate = jnp.split(z, [o1, o2, o3, o4, o5, o6], axis=-1)
    q = (rmsnorm(cq, g_q) @ w_uq).reshape(bsz, s_len, A_HEADS, A_NOPE + A_ROPE)
    q = jnp.concatenate([q[..., :A_NOPE], rope(q[..., A_NOPE:], pos)], axis=-1)
    kv = (rmsnorm(ckv, g_kv) @ w_ukv).reshape(bsz, s_len, A_HEADS, A_NOPE + A_VDIM)
    kr = rope(kr[:, :, None, :], pos)
    k = jnp.concatenate([kv[..., :A_NOPE], jnp.broadcast_to(kr, (bsz, s_len, A_HEADS, A_ROPE))], axis=-1)
    oa = chunk_causal_attention(q, k, kv[..., A_NOPE:]).reshape(bsz, s_len, A_WIDTH)
    shp = (bsz, s_len, B_HEADS, B_HDIM)
    ob = chunk_band_attention(bq.reshape(shp), bk.reshape(shp), bv.reshape(shp), rel_table)
    ob = ob.reshape(bsz, s_len, B_WIDTH)
    y = jnp.concatenate([oa, ob], axis=-1) * jax.nn.silu(gate)
    return y @ w_out


def spatial_gating_mixer(h, w_in, ln_g, ln_b, w_s, b_s, w_out):
    bsz, s_len, _ = h.shape
    u, v, gate = jnp.split(h @ w_in, 3, axis=-1)
    v = layernorm(v, ln_g, ln_b)
    n = s_len // SG_LEN
    dg = SG_WIDTH // SG_GROUPS
    v = v.reshape(bsz, n, SG_LEN, SG_GROUPS, dg)
    cpos = jnp.arange(SG_LEN) // CHUNK
    mask = cpos[None, :] <= cpos[:, None]
    ws = jnp.where(mask[None], w_s, jnp.zeros((), w_s.dtype))
    sv = jnp.einsum('gij,bnjgd->bnigd', ws, v) + b_s.T[:, :, None]
    y = u * sv.reshape(bsz, s_len, SG_WIDTH) * jax.nn.silu(gate)
    return y @ w_out


def setup_inputs(seed: int = 0) -> dict:
    key = jax.random.key(seed)
    ks = jax.random.split(key, 20)
    f32 = jnp.float32
    nrm = lambda k, shp, s: jax.random.normal(k, shp, f32) * s
    return {
        "x": nrm(ks[0], (BATCH, SEQ, D_MODEL), 1.0),
        "c": nrm(ks[1], (BATCH, D_MODEL), 1.0),
        "w_mod": nrm(ks[2], (DEPTH, D_MODEL, 3 * D_MODEL), 0.5 * D_MODEL ** -0.5),
        "b_mod": nrm(ks[3], (DEPTH, 3 * D_MODEL), 0.01),
        "g_pre": 1.0 + nrm(ks[4], (DEPTH, D_MODEL), 0.05),
        "g_post": 1.0 + nrm(ks[5], (DEPTH, D_MODEL), 0.05),
        "ab_w_in": nrm(ks[6], (N_EVEN, D_MODEL, AB_IN), D_MODEL ** -0.5),
        "a_g_q": 1.0 + nrm(ks[7], (N_EVEN, A_QLORA), 0.05),
        "a_w_uq": nrm(ks[8], (N_EVEN, A_QLORA, A_HEADS * (A_NOPE + A_ROPE)), A_QLORA ** -0.5),
        "a_g_kv": 1.0 + nrm(ks[9], (N_EVEN, A_KVLORA), 0.05),
        "a_w_ukv": nrm(ks[10], (N_EVEN, A_KVLORA, A_HEADS * (A_NOPE + A_VDIM)), A_KVLORA ** -0.5),
        "b_rel_bias": nrm(ks[11], (N_EVEN, B_HEADS, 2 * REL_CLIP + 1), 0.5),
        "ab_w_out": nrm(ks[12], (N_EVEN, AB_WIDTH, D_MODEL), AB_WIDTH ** -0.5),
        "sg_w_in": nrm(ks[13], (N_ODD, D_MODEL, 3 * SG_WIDTH), D_MODEL ** -0.5),
        "sg_ln_g": 1.0 + nrm(ks[14], (N_ODD, SG_WIDTH), 0.05),
        "sg_ln_b": nrm(ks[15], (N_ODD, SG_WIDTH), 0.01),
        "sg_w_s": nrm(ks[16], (N_ODD, SG_GROUPS, SG_LEN, SG_LEN), 0.5 * SG_LEN ** -0.5),
        "sg_b_s": 1.0 + nrm(ks[17], (N_ODD, SG_GROUPS, SG_LEN), 0.05),
        "sg_w_out": nrm(ks[18], (N_ODD, SG_WIDTH, D_MODEL), SG_WIDTH ** -0.5),
    }


def reference(x, c, w_mod, b_mod, g_pre, g_post, ab_w_in, a_g_q, a_w_uq, a_g_kv, a_w_ukv,
              b_rel_bias, ab_w_out, sg_w_in, sg_ln_g, sg_ln_b, sg_w_s, sg_b_s, sg_w_out):
    pos = jnp.arange(x.shape[1], dtype=jnp.float32)
    cs = jax.nn.silu(c)
    for l in range(DEPTH):
        mod = cs @ w_mod[l] + b_mod[l]
        shift, scale, gate = jnp.split(mod[:, None, :], 3, axis=-1)
        h = rmsnorm(x, g_pre[l]) * (1 + scale) + shift
        i = l // 2
        if l % 2 == 0:
            y = mla_band_mixer(h, ab_w_in[i], a_g_q[i], a_w_uq[i], a_g_kv[i], a_w_ukv[i],
                               b_rel_bias[i], ab_w_out[i], pos)
        else:
            y = spatial_gating_mixer(h, sg_w_in[i], sg_ln_g[i], sg_ln_b[i], sg_w_s[i],
                                     sg_b_s[i], sg_w_out[i])
        x = x + gate * rmsnorm(y, g_post[l])
    return x
```

```python
import numpy as np
from contextlib import ExitStack
import concourse.bass as bass
import concourse.mybir as mybir
from concourse.bass_utils import run_bass_kernel_spmd

F32 = mybir.dt.float32
BF16 = mybir.dt.bfloat16
AF = mybir.ActivationFunctionType
ALU = mybir.AluOpType

S = 2048
D = 2048
DEPTH = 4
EPS = 1e-6
NEG = -30000.0
CE = ("pe", "act", "dve", "pool")


class Tok:
    __slots__ = ("name", "w", "r", "closed", "pr", "psum")

    ALL = []

    def __init__(self, name, psum=False):
        self.name = name
        self.psum = psum
        self.reset()
        Tok.ALL.append(self)

    def reset(self):
        self.w = {}
        self.r = {}
        self.pr = {}
        self.closed = False


def _merge(d, s):
    for k, v in s.items():
        if d.get(k, 0) < v:
            d[k] = v


class Prog:
    NS = 8

    def __init__(self, nc, es):
        self.nc = nc
        self.sem = {e: es.enter_context(nc.semaphore("s_" + e)) for e in CE}
        self.cnt = {e: 0 for e in CE}
        self.dq = ("sp", "pool", "act")
        self.dsem = {q: [es.enter_context(nc.semaphore("d_%s%d" % (q, i))) for i in range(self.NS)] for q in self.dq}
        self.reset()

    def reset(self):
        self.cnt = {e: 0 for e in CE}
        self.dval = {q: [0] * self.NS for q in self.dq}
        self.dnext = {q: 0 for q in self.dq}
        self.stream = {e: [] for e in ("pe", "act", "dve", "pool", "sp")}
        self.pend = {e: ([], [], []) for e in CE}
        self.pendset = {e: set() for e in CE}
        self.nops = 0

    def semh(self, key):
        if isinstance(key, tuple):
            return self.dsem[key[0]][key[1]]
        return self.sem[key]

    def _publish(self, r, w, wp, key, val):
        for t in r:
            if t.r.get(key, 0) < val:
                t.r[key] = val
            t.closed = True
        for t in w:
            t.w = {key: val}
            t.r = {}
            t.pr = {key: val}
            t.closed = False
        for t in wp:
            if t.closed:
                t.pr = t.r
                t.w = {key: val}
                t.r = {}
                t.closed = False
            else:
                if t.w.get(key, 0) < val:
                    t.w[key] = val

    def op(self, eng, fn, r=(), w=(), wp=(), sig=True, dma=False):
        self.nops += 1
        waits = {}
        if eng != "pe":
            xs = [t for t in (*r, *w, *wp) if t.psum]
            if xs:
                r = [t for t in r if not t.psum]
                wp = [t for t in wp if not t.psum]
                w = [t for t in w if not t.psum] + xs
                for t in xs:
                    for d_ in (t.w, t.r, t.pr):
                        for k, v in d_.items():
                            if k != eng and waits.get(k, 0) < v:
                                waits[k] = v
                own = waits.get(eng)
            else:
                own = None
        else:
            xs = ()
            own = None
        for t in (*r, *w, *wp):
            for e2, ps in self.pendset.items():
                if e2 != eng and t in ps:
                    raise RuntimeError("token %s pending on %s touched by %s" % (t.name, e2, eng))
        for t in r:
            _merge(waits, t.w)
        for t in w:
            _merge(waits, t.w)
            _merge(waits, t.r)
        for t in wp:
            _merge(waits, t.r)
            if not t.closed:
                _merge(waits, t.pr)
        if eng == "pe":
            waits.pop("pe", None)
        elif xs:
            ownv = 0
            for t in r:
                ownv = max(ownv, t.w.get(eng, 0))
            for t in w:
                if not t.psum:
                    ownv = max(ownv, t.w.get(eng, 0), t.r.get(eng, 0))
            for t in wp:
                ownv = max(ownv, t.r.get(eng, 0))
                if not t.closed:
                    ownv = max(ownv, t.pr.get(eng, 0))
            if ownv:
                waits[eng] = ownv
            else:
                waits.pop(eng, None)
        if dma:
            q = eng
            slot = self.dnext[q] % self.NS
            self.dnext[q] += 1
            key = (q, slot)
            if self.dval[q][slot] > 0:
                waits[key] = max(waits.get(key, 0), self.dval[q][slot])
            self.dval[q][slot] += 16
            self._publish(r, w, wp, key, self.dval[q][slot])
            inc = (self.dsem[q][slot], 16)
        else:
            pr, pw, pwp = self.pend[eng]
            pr.extend(r)
            pw.extend(w)
            pwp.extend(wp)
            if sig:
                self.cnt[eng] += 1
                self._publish(pr, pw, pwp, eng, self.cnt[eng])
                self.pend[eng] = ([], [], [])
                self.pendset[eng] = set()
                inc = (self.sem[eng], 1)
            else:
                assert eng == "pe"
                self.pendset[eng].update(r)
                self.pendset[eng].update(w)
                self.pendset[eng].update(wp)
                inc = None
        self.stream[eng].append((waits, fn, inc))

    def emit(self, eng, e):
        seen = {}
        for waits, fn, inc in self.stream[eng]:
            for k, v in waits.items():
                if seen.get(k, 0) >= v:
                    continue
                seen[k] = v
                e.wait_ge(self.semh(k), v)
            if fn is not None:
                ins = fn(e)
                if inc is not None:
                    ins.then_inc(inc[0], inc[1])


class _Stop(Exception):
    pass


def build(depth=DEPTH, stop=None):
    Tok.ALL = []
    nc = bass.Bass("TRN2", target_bir_lowering=False)

    def din(name, shape, dt=F32):
        return nc.dram_tensor(name, list(shape), dt, kind="ExternalInput").ap()

    def dscr(name, shape, dt):
        return nc.dram_tensor(name, list(shape), dt).ap()

    x_in = din("x", [S, D])
    ccol = din("ccol", [128, 16])
    w_mod = din("w_mod", [4, D, 6144])
    b_mod = din("b_mod", [4, 6144])
    gpre_col = din("gpre_col", [128, 64])
    g_post = din("g_post", [4, D])
    ab_w_in = din("ab_w_in", [2, D, 5952])
    gq_col = din("gq_col", [128, 8])
    a_w_uq = din("a_w_uq", [2, 512, 1536])
    gkv_col = din("gkv_col", [128, 4])
    a_w_ukv = din("a_w_ukv", [2, 256, 2048])
    tb4 = din("tb4", [2, 8, 128, 256])
    tb3 = din("tb3", [2, 8, 128, 128])
    crep = din("crep", [128, 16])
    ab_w_out = din("ab_w_out", [2, D, D])
    sg_w_in = din("sg_w_in", [2, D, 6144])
    lng_col = din("lng_col", [128, 32])
    lnb_col = din("lnb_col", [128, 32])
    sg_w_s = din("sg_w_s", [2, 8, 128, 128])
    sg_b_s = din("sg_b_s", [2, 1024])
    sg_w_out = din("sg_w_out", [2, D, D])
    ident_d = din("ident", [128, 128])
    ropecs = din("ropecs", [128, S])
    out = nc.dram_tensor("out", [S, D], F32, kind="ExternalOutput").ap()

    XS = [dscr("XA", [S, D], F32), dscr("XB", [S, D], F32)]
    Y2 = dscr("Y2", [S, D], F32)
    G2ROW = dscr("G2ROW", [4, D], F32)
    QN = dscr("QN", [8, 128, S], BF16)
    QR = dscr("QR", [8, 64, S], BF16)
    KN = dscr("KN", [8, 128, S], BF16)
    VA = dscr("VA", [S, 1024], BF16)
    BQ = dscr("BQ", [8, 128, S], BF16)
    BK = dscr("BK", [8, 128, S], BF16)
    BV = dscr("BV", [S, 1024], BF16)
    GT = dscr("GT", [16, 128, S], BF16)
    UG = dscr("UG", [16, 128, S], BF16)
    VR = dscr("VR", [S, D], F32)

    es = ExitStack()
    with es:
        def sb(name, shape, dt):
            return es.enter_context(nc.sbuf_tensor(name, list(shape), dt))

        A = sb("A", [128, 16 * 2048], BF16)
        A3 = A[:].rearrange("p (k t) -> p k t", t=2048)
        WB = [sb("WB%d" % i, [128, 8192], BF16) for i in range(3)]
        SL = [sb("SL%d" % i, [128, 4096], BF16) for i in range(8)]
        PS = es.enter_context(nc.psum_tensor("PS", [128, 4096], F32))
        identb = sb("identb", [128, 128], BF16)
        onesb = sb("onesb", [128, 128], BF16)
        onesf = sb("onesf", [128, 128], F32)
        csT = sb("csT", [128, 16], BF16)
        ccol_t = sb("ccol_t", [128, 16], F32)
        gpre_t = sb("gpre_t", [128, 64], F32)
        gq_t = sb("gq_t", [128, 8], F32)
        gkv_t = sb("gkv_t", [128, 4], F32)
        crep_t = sb("crep_t", [128, 16], F32)
        lng_t = sb("lng_t", [128, 32], F32)
        lnb_t = sb("lnb_t", [128, 32], F32)
        Acol = sb("Acol", [128, 64], F32)
        Bcol = sb("Bcol", [128, 64], F32)
        st = sb("st", [128, 64], F32)
        rows = [sb("row%d" % i, [1, 512], F32) for i in range(2)]
        foldb = sb("foldb", [128, 64], BF16)
        brow = [sb("brow%d" % i, [1, 512], F32) for i in range(2)]
        grow = [sb("grow%d" % i, [1, 512], F32) for i in range(2)]
        identf = sb("identf", [128, 128], F32)
        diag = [sb("diag%d" % i, [128, 128], F32) for i in range(2)]
        junk = sb("junk", [128, 2048], BF16)
        bt4 = [sb("bt4_%d" % i, [128, 256], F32) for i in range(2)]
        bt3 = [sb("bt3_%d" % i, [128, 128], F32) for i in range(2)]
        bsrow = sb("bsrow", [1, 1024], F32)

        P = Prog(nc, es)

        A_t = [[Tok("A%d_%d" % (k, g)) for g in range(4)] for k in range(16)]
        WB_t = [Tok("WB%d" % i) for i in range(3)]
        H_t = [Tok("H%d" % i) for i in range(16)]
        PS_t = [Tok("PS%d" % i, psum=True) for i in range(8)]
        c_t = Tok("consts")
        st_t = [Tok("st%d" % i) for i in range(8)]
        row_t = [Tok("row%d" % i) for i in range(2)]
        brow_t = [Tok("brow%d" % i) for i in range(2)]
        grow_t = [Tok("grow%d" % i) for i in range(2)]
        diag_t = [Tok("diag%d" % i) for i in range(2)]
        bt_t = [Tok("bt%d" % i) for i in range(2)]
        AB_t = [Tok("AB%d" % l) for l in range(4)]
        X_t = {id(XS[0]): [Tok("XA%d" % i) for i in range(16)], id(XS[1]): [Tok("XB%d" % i) for i in range(16)],
               id(out): [Tok("out%d" % i) for i in range(16)], id(x_in): [Tok("xin%d" % i) for i in range(16)]}
        Y2_t = [Tok("Y2_%d" % i) for i in range(16)]
        G2_t = [Tok("G2_%d" % i) for i in range(4)]
        QN_t = [Tok("QN%d" % i) for i in range(8)]
        QR_t = [Tok("QR%d" % i) for i in range(8)]
        KN_t = [Tok("KN%d" % i) for i in range(8)]
        VA_t = Tok("VA")
        BQ_t = [Tok("BQ%d" % i) for i in range(8)]
        BK_t = [Tok("BK%d" % i) for i in range(8)]
        BV_t = Tok("BV")
        GT_t = [Tok("GT%d" % i) for i in range(16)]
        UG_t = [Tok("UG%d" % i) for i in range(16)]
        VR_t = [Tok("VR%d" % i) for i in range(16)]
        bs_t = Tok("bsrow")

        def H(i):
            return SL[i // 2][:, (i % 2) * 2048:(i % 2) * 2048 + 2048]

        def H32(i):
            return H(i).bitcast(F32)

        def Fs(j):
            return SL[j][:].bitcast(F32)

        def F_t(j):
            return [H_t[2 * j], H_t[2 * j + 1]]

        def bank(b):
            return PS[:, b * 512:(b + 1) * 512]

        def mm(o, lhsT, rhs, start, stop, r, w, sig):
            P.op("pe", lambda e: e.matmul(o, lhsT, rhs, start=start, stop=stop), r=r, w=w, sig=sig)

        def act(o, i, func, r, w=(), wp=(), bias=None, scale=None, accum=None):
            kw = {}
            if bias is not None:
                kw["bias"] = bias
            if scale is not None:
                kw["scale"] = scale
            if accum is not None:
                kw["accum_out"] = accum
            P.op("act", lambda e: e.activation(out=o, in_=i, func=func, **kw), r=r, w=w, wp=wp)

        def ts(eng, o, i, s1, s2, op0, op1, r, w=(), wp=()):
            if s2 is None:
                P.op(eng, lambda e: e.tensor_scalar(o, i, s1, None, op0), r=r, w=w, wp=wp)
            else:
                P.op(eng, lambda e: e.tensor_scalar(o, i, s1, s2, op0, op1), r=r, w=w, wp=wp)

        def stt(eng, o, i0, sc, i1, op0, op1, r, w=(), wp=()):
            P.op(eng, lambda e: e.scalar_tensor_tensor(o, i0, sc, i1, op0, op1), r=r, w=w, wp=wp)

        def tt(eng, o, i0, i1, op, r, w=(), wp=()):
            P.op(eng, lambda e: e.tensor_tensor(o, i0, i1, op), r=r, w=w, wp=wp)

        def cp(eng, o, i, r, w=(), wp=()):
            P.op(eng, lambda e: e.tensor_copy(o, i), r=r, w=w, wp=wp)

        def recip(o, i, r, w=(), wp=()):
            P.op("dve", lambda e: e.reciprocal(o, i), r=r, w=w, wp=wp)

        def mset(eng, o, val, w=(), wp=()):
            P.op(eng, lambda e: e.memset(o, val), w=w, wp=wp)

        def dma(q, o, i, r=(), w=(), wp=(), slow=False):
            if slow:
                P.op(q, lambda e: e.dma_start(out=o, in_=i, allow_slow_non_contiguous=True), r=r, w=w, wp=wp, dma=True)
            else:
                P.op(q, lambda e: e.dma_start(out=o, in_=i), r=r, w=w, wp=wp, dma=True)

        def chk(name):
            if stop == name:
                raise _Stop()

        rr = {"bank": 0, "fm": 0, "w": 0, "row": 0, "st": 0}

        def nb():
            b = rr["bank"] % 8
            rr["bank"] += 1
            return b

        def nfm():
            b = 4 * (rr["fm"] % 2)
            rr["fm"] += 1
            return b

        plan = {"specs": [], "replay": None, "k": 0, "issued": 0}

        def wload_generic(issue_fn):
            k = plan["k"]
            plan["k"] += 1
            if plan["replay"] is None:
                plan["specs"].append(issue_fn)
                issue_fn(k % 3)
            else:
                specs = plan["replay"]
                while plan["issued"] <= min(k + 1, len(specs) - 1):
                    j = plan["issued"]
                    tb_ = WB_t[j % 3]
                    assert tb_.closed or not tb_.w, "weight buffer %d reloaded before its consumers were recorded (load %d)" % (j % 3, j)
                    specs[j](j % 3)
                    plan["issued"] += 1
            return k % 3

        def wload(src, nkc, ncols, bufcols=None, col_off=0):
            bufcols = bufcols or ncols

            def view(i):
                return WB[i][:, 0:nkc * bufcols].rearrange("p (k c) -> p k c", c=bufcols)

            def issue(i):
                dma("pool", view(i)[:, :, col_off:col_off + ncols], src.rearrange("(k p) c -> p k c", p=128), w=[WB_t[i]])

            i = wload_generic(issue)
            return view(i), WB_t[i]

        def consts():
            dma("pool", identb[:], ident_d[:, :], w=[c_t])
            dma("sp", identf[:], ident_d[:, :], wp=[c_t])
            mset("dve", onesb[:], 1.0, wp=[c_t])
            mset("dve", onesf[:], 1.0, wp=[c_t])
            dma("sp", ccol_t[:], ccol[:, :], wp=[c_t])
            dma("sp", gpre_t[:], gpre_col[:, :], wp=[c_t])
            dma("sp", gq_t[:], gq_col[:, :], wp=[c_t])
            dma("sp", gkv_t[:], gkv_col[:, :], wp=[c_t])
            dma("sp", crep_t[:], crep[:, :], wp=[c_t])
            dma("sp", lng_t[:], lng_col[:, :], wp=[c_t])
            dma("sp", lnb_t[:], lnb_col[:, :], wp=[c_t])
            act(csT[:], ccol_t[:], AF.Silu, r=[c_t], wp=[c_t])
            tt("dve", foldb[:], identb[:, 0:64], identb[:, 64:128], ALU.add, r=[c_t], wp=[c_t])

        modq = []

        def mod_rows_prefetch(l, cg):
            bi = cg % 2
            dma("sp", brow[bi][:], b_mod[l:l + 1, cg * 512:(cg + 1) * 512], w=[brow_t[bi]])
            if cg >= 8:
                g0 = (cg - 8) * 512
                dma("sp", grow[bi][:], g_post[l:l + 1, g0:g0 + 512], w=[grow_t[bi]])

        def mod_flush():
            while modq:
                modq.pop(0)()

        def mod_group(l, cg):
            wv, wt = wload(w_mod[l, :, cg * 512:(cg + 1) * 512], 16, 512)
            mod_flush()
            if cg == 0:
                mod_rows_prefetch(l, 0)
            if cg + 1 < 12:
                mod_rows_prefetch(l, cg + 1)
            b = nb()
            for kc in range(16):
                mm(bank(b)[0:1, :], csT[:, kc:kc + 1], wv[:, kc, :], kc == 0, kc == 15, r=[wt, c_t], w=[PS_t[b]], sig=(kc == 15))
            bi = cg % 2
            ri = cg % 2
            tt("dve", rows[ri][:], bank(b)[0:1, :], brow[bi][:], ALU.add, r=[PS_t[b], brow_t[bi]], w=[row_t[ri]])
            if cg < 8:
                def fin():
                    b2 = nb()
                    for j in range(4):
                        mm(bank(b2)[:, j:j + 1], rows[ri][0:1, j * 128:(j + 1) * 128], onesf[0:1, 0:1], True, True,
                           r=[row_t[ri], c_t], w=[PS_t[b2]], sig=(j == 3))
                    if cg < 4:
                        c0 = l * 16 + cg * 4
                        cp("dve", Bcol[:, c0:c0 + 4], bank(b2)[:, 0:4], r=[PS_t[b2]], wp=[AB_t[l]])
                    else:
                        c0 = l * 16 + (cg - 4) * 4
                        stt("dve", Acol[:, c0:c0 + 4], bank(b2)[:, 0:4], 1.0, gpre_t[:, c0:c0 + 4], ALU.add, ALU.mult,
                            r=[PS_t[b2], c_t], wp=[AB_t[l]])
                modq.append(fin)
            else:
                g0 = (cg - 8) * 512
                tt("dve", rows[ri][:], rows[ri][:], grow[bi][:], ALU.mult, r=[row_t[ri], grow_t[bi]], w=[row_t[ri]])
                dma("sp", G2ROW[l:l + 1, g0:g0 + 512], rows[ri][:], r=[row_t[ri]], wp=[G2_t[l]])

        def norm_phase(l, Xprev, Xnext, has_y2, make_h):
            G2rep = Fs(5)
            if has_y2:
                dma("sp", G2rep, G2ROW[l - 1:l, :].broadcast_to([128, D]), r=[G2_t[l - 1]], w=F_t(5))

            def ysl(t_):
                return t_ % 2

            def xsl(t_):
                return 2 + t_ % 3

            def loads(t_):
                if has_y2:
                    dma("sp", Fs(ysl(t_)), Y2[t_ * 128:(t_ + 1) * 128, :], r=[Y2_t[t_]], w=F_t(ysl(t_)))
                dma("sp", Fs(xsl(t_)), Xprev[t_ * 128:(t_ + 1) * 128, :], r=[X_t[id(Xprev)][t_]], w=F_t(xsl(t_)))

            cols = {}

            def stage_a(t_):
                y2 = Fs(ysl(t_))
                xt = Fs(xsl(t_))
                yt_, xt_ = F_t(ysl(t_)), F_t(xsl(t_))
                si = rr["st"] % 8
                rr["st"] += 1
                c = si * 8
                cols[t_] = (si, c)
                stt_ = [st_t[si]]
                if has_y2:
                    act(junk[:], y2, AF.Square, r=yt_, w=stt_, accum=st[:, c:c + 1])
                    act(st[:, c + 1:c + 2], st[:, c:c + 1], AF.Sqrt, r=stt_, w=stt_, bias=EPS, scale=1.0 / D)
                    recip(st[:, c + 2:c + 3], st[:, c + 1:c + 2], r=stt_, w=stt_)
                    tt("dve", y2, y2, G2rep, ALU.mult, r=yt_ + F_t(5), w=yt_)
                    stt("dve", xt, y2, st[:, c + 2:c + 3], xt, ALU.mult, ALU.add, r=yt_ + xt_ + stt_, w=xt_)
                    dma("sp", Xnext[t_ * 128:(t_ + 1) * 128, :], xt, r=xt_, w=[X_t[id(Xnext)][t_]])

            def stage_b(t_):
                xt = Fs(xsl(t_))
                xt_ = F_t(xsl(t_))
                si, c = cols[t_]
                stt_ = [st_t[si]]
                act(junk[:], xt, AF.Square, r=xt_, w=stt_, accum=st[:, c + 3:c + 4])
                act(st[:, c + 4:c + 5], st[:, c + 3:c + 4], AF.Sqrt, r=stt_, w=stt_, bias=EPS, scale=1.0 / D)
                recip(st[:, c + 5:c + 6], st[:, c + 4:c + 5], r=stt_, w=stt_)
                dg = diag[t_ % 2]
                ts("dve", dg[:], identf[:], st[:, c + 5:c + 6], None, ALU.mult, None, r=stt_ + [c_t], w=[diag_t[t_ % 2]])
                for q in range(4):
                    b = nb()
                    for j in range(4):
                        fc = 4 * q + j
                        mm(bank(b)[:, j * 128:(j + 1) * 128], xt[:, fc * 128:(fc + 1) * 128], dg[:], True, True,
                           r=xt_ + [diag_t[t_ % 2]], w=[PS_t[b]], sig=(j == 3))
                    for j in range(4):
                        fc = 4 * q + j
                        o = A3[:, fc, t_ * 128:(t_ + 1) * 128]
                        i_ = bank(b)[:, j * 128:(j + 1) * 128]
                        ac = Acol[:, l * 16 + fc:l * 16 + fc + 1]
                        bc = Bcol[:, l * 16 + fc:l * 16 + fc + 1]
                        if q % 2 == 0:
                            act(o, i_, AF.Identity, r=[PS_t[b], AB_t[l]], wp=[A_t[fc][t_ // 4]], bias=bc, scale=ac)
                        else:
                            ts("dve", o, i_, ac, bc, ALU.mult, ALU.add, r=[PS_t[b], AB_t[l]], wp=[A_t[fc][t_ // 4]])

            loads(0)
            loads(1)
            stage_a(0)
            for t_ in range(16):
                if t_ + 2 < 16:
                    loads(t_ + 2)
                if t_ + 1 < 16:
                    stage_a(t_ + 1)
                if make_h:
                    stage_b(t_)

        def fm_block(wv, wt, cols, nkc, rhs_fn, b0, M=128, banks=None):
            for kc in range(nkc):
                for tg in range(4):
                    rhs, rt = rhs_fn(kc, tg)
                    b = b0 + tg
                    mm(bank(b)[0:M, :], wv[:, kc, cols], rhs, kc == 0, kc == nkc - 1, r=[wt] + rt, w=[PS_t[b]], sig=(kc == nkc - 1))

        def rhs_A(kc, tg):
            return A3[:, kc, tg * 512:(tg + 1) * 512], [A_t[kc][tg]]

        def tm_tile(t_, rhs_fn, nkc, lhs_fn):
            b = nb()
            for kc in range(nkc):
                lhsT, lt = lhs_fn(kc, t_)
                rhs, rt = rhs_fn(kc)
                mm(bank(b), lhsT, rhs, kc == 0, kc == nkc - 1, r=lt + rt, w=[PS_t[b]], sig=(kc == nkc - 1))
            return b

        def lhs_A(kc, t_):
            return A3[:, kc, t_ * 128:(t_ + 1) * 128], [A_t[kc][t_ // 4]]

        ev = {"i": 0}

        def evac_copy(o, i_, r, w=(), wp=()):
            ev["i"] += 1
            if ev["i"] % 2 == 0:
                act(o, i_, AF.Copy, r=r, w=w, wp=wp)
            else:
                cp("dve", o, i_, r=r, w=w, wp=wp)

        def fm_to_dram(wv, wt, cols, dst, dst_t, hslot, silu=False):
            b0 = nfm()
            fm_block(wv, wt, cols, 16, rhs_A, b0)
            o = H(hslot)
            src = PS[:, b0 * 512:(b0 + 4) * 512]
            rt = [PS_t[b0 + i] for i in range(4)]
            if silu:
                act(o, src, AF.Silu, r=rt, w=[H_t[hslot]])
            else:
                for tg in range(4):
                    evac_copy(o[:, tg * 512:(tg + 1) * 512], bank(b0 + tg), r=[PS_t[b0 + tg]], wp=[H_t[hslot]])
            dma("sp", dst, o, r=[H_t[hslot]], w=[dst_t])

        def out_phase(w_out_l, mods):
            hs = [12, 13, 14, 15]
            k = 0
            for cg in range(4):
                if mods:
                    mods.pop(0)()
                wv, wt = wload(w_out_l[:, cg * 512:(cg + 1) * 512], 16, 512)
                for t_ in range(16):
                    b = tm_tile(t_, lambda kc: (wv[:, kc, :], [wt]), 16, lhs_A)
                    hi = hs[k % 4]
                    k += 1
                    o = H32(hi)[:, 0:512]
                    evac_copy(o, bank(b), r=[PS_t[b]], w=[H_t[hi]])
                    dma("sp", Y2[t_ * 128:(t_ + 1) * 128, cg * 512:(cg + 1) * 512], o, r=[H_t[hi]], wp=[Y2_t[t_]])

        def even_layer(i, mods):
            w_in = ab_w_in[i]
            CS = Fs(3)
            dma("sp", CS, ropecs[:, :], w=F_t(3))
            cqn = [H(0), H(1), H(2), H(3)]
            ckvn = [H(4), H(5)]
            KRs = 10
            KR = H(KRs)
            sq_h = [H32(8), H32(9)]
            rst_h = H32(11)
            Tt = H(11)[:, 1024:1536]

            def rope_fold(src_bank, tg, dst, dst_tok):
                tt("dve", Tt, bank(src_bank), CS[:, tg * 512:(tg + 1) * 512], ALU.mult, r=[PS_t[src_bank]] + F_t(3), w=[H_t[11]])
                bf = nb()
                mm(bank(bf)[0:64, :], foldb[:], Tt, True, True, r=[H_t[11], c_t], w=[PS_t[bf]], sig=True)
                evac_copy(dst[0:64, tg * 512:(tg + 1) * 512], bank(bf)[0:64, :], r=[PS_t[bf]], wp=[dst_tok])

            def lowrank_group(wv, wt, ncb, g_t, gcol0, outs, with_kr):
                for tg in range(4):
                    for cb in range(ncb):
                        for kc in range(16):
                            mm(bank(cb), wv[:, kc, cb * 128:(cb + 1) * 128], A3[:, kc, tg * 512:(tg + 1) * 512],
                               kc == 0, kc == 15, r=[wt, A_t[kc][tg]], w=[PS_t[cb]], sig=(kc == 15))
                    if with_kr:
                        for kc in range(16):
                            mm(bank(2), wv[:, kc, 256:384], A3[:, kc, tg * 512:(tg + 1) * 512],
                               kc == 0, kc == 15, r=[wt, A_t[kc][tg]], w=[PS_t[2]], sig=(kc == 15))
                    sbk = 4 + (tg % 2)
                    for cb in range(ncb):
                        sq = sq_h[cb % 2][:, (cb // 2 % 2) * 512:(cb // 2 % 2) * 512 + 512]
                        sq_tok = H_t[8 + cb % 2]
                        act(sq, bank(cb), AF.Square, r=[PS_t[cb]], w=[sq_tok])
                        mm(bank(sbk), onesf[:], sq, cb == 0, cb == ncb - 1, r=[sq_tok, c_t], w=[PS_t[sbk]], sig=True)
                    rst = rst_h[:, 0:512]
                    act(rst, bank(sbk), AF.Sqrt, r=[PS_t[sbk]], w=[H_t[11]], bias=EPS, scale=1.0 / (ncb * 128))
                    recip(rst, rst, r=[H_t[11]], w=[H_t[11]])
                    for cb in range(ncb):
                        o, ot = outs[cb]
                        stt("dve", o[:, tg * 512:(tg + 1) * 512], bank(cb), g_t[:, gcol0 + cb:gcol0 + cb + 1], rst, ALU.mult, ALU.mult,
                            r=[PS_t[cb], H_t[11], c_t], wp=[ot])
                    if with_kr:
                        rope_fold(2, tg, KR, H_t[KRs])

            wv, wt = wload(w_in[:, 0:512], 16, 512)
            wv1, wt1 = wload(w_in[:, 512:832], 16, 320, bufcols=384)
            lowrank_group(wv, wt, 4, gq_t, i * 4, [(cqn[k], H_t[k]) for k in range(4)], False)
            chk("g0")
            ts("dve", wv1[:, :, 320:352], wv1[:, :, 288:320], -1.0, None, ALU.mult, None, r=[wt1], wp=[wt1])
            cp("dve", wv1[:, :, 352:384], wv1[:, :, 256:288], r=[wt1], wp=[wt1])
            lowrank_group(wv1, wt1, 2, gkv_t, i * 2, [(ckvn[k], H_t[4 + k]) for k in range(2)], True)

            chk("g1")
            if mods:
                mods.pop(0)()
            wkv, wkt = wload(a_w_ukv[i], 2, 2048)
            wkv4 = wkv.rearrange("p k (h c) -> p k h c", c=256)
            def wq_view(i_):
                return WB[i_][:, 0:8192].rearrange("p (k h c) -> p k h c", k=4, c=256)

            def wq_issue(i_):
                for kc in range(4):
                    dma("pool", wq_view(i_)[:, kc, :, 0:192], a_w_uq[i, kc * 128:(kc + 1) * 128, :].rearrange("p (h c) -> p h c", c=192),
                        w=[WB_t[i_]] if kc == 0 else [], wp=[] if kc == 0 else [WB_t[i_]])

            iq = wload_generic(wq_issue)
            wqt = WB_t[iq]
            wq4 = wq_view(iq)
            ts("dve", wq4[:, :, :, 192:224], wq4[:, :, :, 160:192], -1.0, None, ALU.mult, None, r=[wqt], wp=[wqt])
            cp("dve", wq4[:, :, :, 224:256], wq4[:, :, :, 128:160], r=[wqt], wp=[wqt])

            def rhs_ckv(kc, tg):
                return ckvn[kc][:, tg * 512:(tg + 1) * 512], [H_t[4 + kc]]

            eh = [12, 13, 14, 15]
            ek = 0
            for h in range(8):
                b0 = nfm()
                fm_block(wkv, wkt, slice(h * 256, h * 256 + 128), 2, rhs_ckv, b0)
                hi = eh[ek % 4]
                ek += 1
                for tg in range(4):
                    evac_copy(H(hi)[:, tg * 512:(tg + 1) * 512], bank(b0 + tg), r=[PS_t[b0 + tg]], wp=[H_t[hi]])
                dma("sp", KN[h], H(hi), r=[H_t[hi]], w=[KN_t[h]])
            for t_ in range(16):
                for half in range(2):
                    b = tm_tile(t_, lambda kc: (wkv4[:, kc, 4 * half:4 * half + 4, 128:256], [wkt]), 2,
                                lambda kc, t2: (ckvn[kc][:, t2 * 128:(t2 + 1) * 128], [H_t[4 + kc]]))
                    hi = eh[ek % 4]
                    ek += 1
                    o = H(hi)[:, 0:512]
                    evac_copy(o, bank(b), r=[PS_t[b]], w=[H_t[hi]])
                    dma("sp", VA[t_ * 128:(t_ + 1) * 128, half * 512:(half + 1) * 512], o, r=[H_t[hi]], wp=[VA_t])
            chk("kv")
            for h in range(8):
                b0 = nfm()
                for kc in range(4):
                    for tg in range(4):
                        mm(bank(b0 + tg), wq4[:, kc, h, 0:128], cqn[kc][:, tg * 512:(tg + 1) * 512], kc == 0, kc == 3,
                           r=[wqt, H_t[kc]], w=[PS_t[b0 + tg]], sig=(kc == 3))
                hi = eh[ek % 4]
                ek += 1
                for tg in range(4):
                    evac_copy(H(hi)[:, tg * 512:(tg + 1) * 512], bank(b0 + tg), r=[PS_t[b0 + tg]], wp=[H_t[hi]])
                dma("sp", QN[h], H(hi), r=[H_t[hi]], w=[QN_t[h]])
                hi = eh[ek % 4]
                ek += 1
                for tg in range(4):
                    br = nb()
                    for kc in range(4):
                        mm(bank(br), wq4[:, kc, h, 128:256], cqn[kc][:, tg * 512:(tg + 1) * 512],
                           kc == 0, kc == 3, r=[wqt, H_t[kc]], w=[PS_t[br]], sig=(kc == 3))
                    rope_fold(br, tg, H(hi), H_t[hi])
                dma("sp", QR[h], H(hi)[0:64, :], r=[H_t[hi]], w=[QR_t[h]])

            chk("q")
            def fm_cols(c0, nblk, dst, dst_t, blk0, silu):
                nonlocal ek
                done = 0
                while done < nblk:
                    n = min(4, nblk - done)
                    if mods:
                        mods.pop(0)()
                    wv_, wt_ = wload(w_in[:, c0 + done * 128:c0 + (done + n) * 128], 16, n * 128)
                    for cb in range(n):
                        hi = eh[ek % 4]
                        ek += 1
                        blk = blk0 + done + cb
                        fm_to_dram(wv_, wt_, slice(cb * 128, (cb + 1) * 128), dst[blk], dst_t[blk], hi, silu=silu)
                    done += n

            fm_cols(832, 8, BQ, BQ_t, 0, False)
            fm_cols(1856, 8, BK, BK_t, 0, False)
            for cg in range(2):
                if mods:
                    mods.pop(0)()
                wv_, wt_ = wload(w_in[:, 2880 + cg * 512:2880 + (cg + 1) * 512], 16, 512)
                for t_ in range(16):
                    b = tm_tile(t_, lambda kc: (wv_[:, kc, :], [wt_]), 16, lhs_A)
                    hi = eh[ek % 4]
                    ek += 1
                    o = H(hi)[:, 0:512]
                    evac_copy(o, bank(b), r=[PS_t[b]], w=[H_t[hi]])
                    dma("sp", BV[t_ * 128:(t_ + 1) * 128, cg * 512:(cg + 1) * 512], o, r=[H_t[hi]], wp=[BV_t])
            fm_cols(3904, 16, GT, GT_t, 0, True)

            chk("inproj")
            attn(i)

        def attn_head(kind, i, h, par, loads_only=False, compute_only=False):
            base = par * 5
            qs, qrs, ks, vs, gs = base, base + 1, base + 2, base + 3, base + 4
            chunk = h if kind == "a" else 8 + h
            if not compute_only:
                if kind == "a":
                    dma("sp", H(qs), QN[h], r=[QN_t[h]], w=[H_t[qs]])
                    dma("sp", H(qrs)[0:64, :], QR[h], r=[QR_t[h]], w=[H_t[qrs]])
                    dma("sp", H(ks), KN[h], r=[KN_t[h]], w=[H_t[ks]])
                    dma("sp", H(vs).rearrange("p (t d) -> p t d", d=128), VA[:, h * 128:(h + 1) * 128].rearrange("(t p) d -> p t d", p=128),
                        r=[VA_t], w=[H_t[vs]])
                else:
                    dma("sp", H(qs), BQ[h], r=[BQ_t[h]], w=[H_t[qs]])
                    dma("sp", H(ks), BK[h], r=[BK_t[h]], w=[H_t[ks]])
                    dma("sp", H(vs).rearrange("p (t d) -> p t d", d=128), BV[:, h * 128:(h + 1) * 128].rearrange("(t p) d -> p t d", p=128),
                        r=[BV_t], w=[H_t[vs]])
                    dma("sp", bt4[par][:], tb4[i, h], w=[bt_t[par]])
                    dma("sp", bt3[par][:], tb3[i, h], wp=[bt_t[par]])
                dma("sp", H(gs), GT[chunk], r=[GT_t[chunk]], w=[H_t[gs]])
            if loads_only:
                return
            q_, k_, v_, g_ = H(qs), H(ks), H(vs).rearrange("p (t d) -> p t d", d=128), H(gs)
            qr_ = H(qrs)
            KR = H(10)
            PT_s = [11, 12]
            tmp_s = [13, 14]
            scale = (192.0 ** -0.5) if kind == "a" else (128.0 ** -0.5)
            cb_ = crep_t[:, i * 8 + h:i * 8 + h + 1]
            for g in range(4):
                ob = g % 2
                sb_ = 2 + g % 2
                tiles = []
                if kind == "a":
                    for kt in range(4 * g + 4):
                        c0 = 0 if kt < 4 * g else (kt - 4 * g) * 128
                        tiles.append((kt, c0, 512 - c0, "diag" if kt >= 4 * g else "plain"))
                else:
                    for u in [4, 5, 6, 7, 0, 1, 2, 3]:
                        t_ = 4 * g - 4 + u
                        if t_ < 0:
                            continue
                        if u < 4:
                            tiles.append((t_, 0, 128 * (u + 1), "u%d" % u))
                        else:
                            tiles.append((t_, 128 * (u - 4), 512 - 128 * (u - 4), "u%d" % u))
                nt = len(tiles)
                ptl = []

                def s_stage(idx):
                    kt, c0, N, kindt = tiles[idx]
                    k_i = att["pt"] % 8
                    att["pt"] += 1
                    sbk = 4 + att["s"] % 4
                    att["s"] += 1
                    pt_tok = PT_t[k_i]
                    pt = H(PT_s[k_i // 4])[:, (k_i % 4) * 512:(k_i % 4) * 512 + 512]
                    q0 = g * 512 + c0
                    if kind == "a":
                        mm(bank(sbk)[:, 0:N], k_[:, kt * 128:(kt + 1) * 128], q_[:, q0:q0 + N], True, False,
                           r=[H_t[ks], H_t[qs]], w=[PS_t[sbk]], sig=False)
                        mm(bank(sbk)[:, 0:N], KR[0:64, kt * 128:(kt + 1) * 128], qr_[0:64, q0:q0 + N], False, True,
                           r=[H_t[10], H_t[qrs]], w=[PS_t[sbk]], sig=True)
                        act(pt[:, 0:N], bank(sbk)[:, 0:N], AF.Exp, r=[PS_t[sbk]], w=[pt_tok], scale=scale)
                        if kindt == "diag":
                            mset("dve", pt[64:128, 0:64], 0.0, w=[pt_tok])
                    else:
                        mm(bank(sbk)[:, 0:N], k_[:, kt * 128:(kt + 1) * 128], q_[:, q0:q0 + N], True, True,
                           r=[H_t[ks], H_t[qs]], w=[PS_t[sbk]], sig=True)
                        u = int(kindt[1:])
                        if u >= 3:
                            nbias = min(256, N) if u >= 4 else 128
                            btile = bt4[par] if u >= 4 else bt3[par]
                            ti = att["tmp"] % 2
                            att["tmp"] += 1
                            tmp = H32(tmp_s[ti])[:, 0:nbias]
                            stt("dve", tmp, bank(sbk)[:, 0:nbias], scale, btile[:, 0:nbias], ALU.mult, ALU.add,
                                r=[PS_t[sbk], bt_t[par]], w=[H_t[tmp_s[ti]]])
                            act(pt[:, 0:nbias], tmp, AF.Exp, r=[H_t[tmp_s[ti]]], w=[pt_tok])
                            if N > nbias:
                                act(pt[:, nbias:N], bank(sbk)[:, nbias:N], AF.Exp, r=[PS_t[sbk], c_t], wp=[pt_tok], scale=scale, bias=cb_)
                        else:
                            act(pt[:, 0:N], bank(sbk)[:, 0:N], AF.Exp, r=[PS_t[sbk], c_t], w=[pt_tok], scale=scale, bias=cb_)
                        if u < 4:
                            mset("dve", pt[0:64, N - 64:N], 0.0, w=[pt_tok])
                    ptl.append((pt, pt_tok))

                def pv_stage(idx):
                    kt, c0, N, kindt = tiles[idx]
                    pt, pt_tok = ptl[idx]
                    mm(bank(ob)[:, c0:c0 + N], v_[:, kt, :], pt[:, 0:N], idx == 0, idx == nt - 1,
                       r=[H_t[vs], pt_tok], w=[PS_t[ob]], sig=False)
                    mm(bank(sb_)[:, c0:c0 + N], onesb[:], pt[:, 0:N], idx == 0, idx == nt - 1,
                       r=[c_t, pt_tok], w=[PS_t[sb_]], sig=True)

                LA = 3
                for idx in range(nt + LA):
                    if idx < nt:
                        s_stage(idx)
                    if idx >= LA:
                        pv_stage(idx - LA)
                ti = att["tmp"] % 2
                att["tmp"] += 1
                rec = H32(tmp_s[ti])[:, 0:512]
                o32 = H32(tmp_s[ti])[:, 512:1024]
                act(rec, bank(sb_), AF.Ln, r=[PS_t[sb_]], w=[H_t[tmp_s[ti]]])
                act(rec, rec, AF.Exp, r=[H_t[tmp_s[ti]]], w=[H_t[tmp_s[ti]]], scale=-1.0)
                tt("dve", o32, bank(ob), rec, ALU.mult, r=[PS_t[ob], H_t[tmp_s[ti]]], w=[H_t[tmp_s[ti]]])
                tt("pool", A3[:, chunk, g * 512:(g + 1) * 512], o32, g_[:, g * 512:(g + 1) * 512], ALU.mult,
                   r=[H_t[tmp_s[ti]], H_t[gs]], w=[A_t[chunk][g]])

        att = {"pt": 0, "s": 0, "tmp": 0}
        PT_t = [Tok("PT%d" % k) for k in range(8)]

        def tok_split(parent, children):
            for ch in children:
                ch.w = dict(parent.w)
                ch.r = dict(parent.r)
                ch.pr = dict(parent.pr)
                ch.closed = parent.closed

        def tok_join(parent, children):
            w, r, pr = {}, {}, {}
            for ch in children:
                _merge(w, ch.w)
                _merge(r, ch.r)
                _merge(r, ch.w)
                _merge(pr, ch.pr)
            parent.w, parent.r, parent.pr, parent.closed = w, r, pr, True

        def attn(i):
            tok_split(H_t[11], PT_t[0:4])
            tok_split(H_t[12], PT_t[4:8])
            attn_inner(i)
            tok_join(H_t[11], PT_t[0:4])
            tok_join(H_t[12], PT_t[4:8])

        def attn_inner(i):
            heads = [("a", h) for h in range(8)] + [("b", h) for h in range(8)]
            attn_head(heads[0][0], i, heads[0][1], 0, loads_only=True)
            for n, (kind, h) in enumerate(heads):
                if n + 1 < len(heads):
                    attn_head(heads[n + 1][0], i, heads[n + 1][1], (n + 1) % 2, loads_only=True)
                attn_head(kind, i, h, n % 2, compute_only=True)

        def odd_layer(i, mods):
            w_in = sg_w_in[i]
            eh = [12, 13, 14, 15]
            ek = 0
            for cg in range(4):
                if mods:
                    mods.pop(0)()
                wu, wut = wload(w_in[:, cg * 512:(cg + 1) * 512], 16, 512)
                wg, wgt = wload(w_in[:, 4096 + cg * 512:4096 + (cg + 1) * 512], 16, 512)
                for cb in range(4):
                    fc = cg * 4 + cb
                    hu = eh[ek % 4]
                    hg = eh[(ek + 1) % 4]
                    ek += 2
                    b0 = nfm()
                    fm_block(wu, wut, slice(cb * 128, (cb + 1) * 128), 16, rhs_A, b0)
                    for tg in range(4):
                        cp("dve", H(hu)[:, tg * 512:(tg + 1) * 512], bank(b0 + tg), r=[PS_t[b0 + tg]], wp=[H_t[hu]])
                    b1 = nfm()
                    fm_block(wg, wgt, slice(cb * 128, (cb + 1) * 128), 16, rhs_A, b1)
                    act(H(hg), PS[:, b1 * 512:(b1 + 4) * 512], AF.Silu, r=[PS_t[b1 + k] for k in range(4)], w=[H_t[hg]])
                    tt("pool", H(hu), H(hu), H(hg), ALU.mult, r=[H_t[hu], H_t[hg]], w=[H_t[hu]])
                    dma("sp", UG[fc], H(hu), r=[H_t[hu]], w=[UG_t[fc]])
            for cg in range(4):
                if mods:
                    mods.pop(0)()
                wv_, wt_ = wload(w_in[:, 2048 + cg * 512:2048 + (cg + 1) * 512], 16, 512)
                for t_ in range(16):
                    b = tm_tile(t_, lambda kc: (wv_[:, kc, :], [wt_]), 16, lhs_A)
                    hi = eh[ek % 4]
                    ek += 1
                    o = H32(hi)[:, 0:512]
                    evac_copy(o, bank(b), r=[PS_t[b]], w=[H_t[hi]])
                    dma("sp", VR[t_ * 128:(t_ + 1) * 128, cg * 512:(cg + 1) * 512], o, r=[H_t[hi]], wp=[VR_t[t_]])
            wsb = H(0)[:, 0:1024].rearrange("p (g j) -> p g j", j=128)
            wsT = H(1)[:, 0:1024].rearrange("p (g i) -> p g i", i=128)
            E = Fs(1)
            Ev = E.rearrange("p (c i) -> p c i", i=128)
            dma("pool", wsb, sg_w_s[i].rearrange("g i j -> i g j"), w=[H_t[0]])
            mset("dve", wsb[0:64, :, 64:128], 0.0, wp=[H_t[0]])
            for half in range(2):
                b = nb()
                for j in range(4):
                    g = half * 4 + j
                    mm(bank(b)[:, j * 128:(j + 1) * 128], wsb[:, g, :], identb[:], True, True, r=[H_t[0], c_t], w=[PS_t[b]], sig=(j == 3))
                evac_copy(H(1)[:, half * 512:(half + 1) * 512], bank(b), r=[PS_t[b]], wp=[H_t[1]])
            dma("sp", bsrow[:], sg_b_s[i:i + 1, :], w=[bs_t])
            rs_b = [nb(), nb()]
            bs_b = [nb(), nb()]
            for half in range(2):
                for j in range(4):
                    g = half * 4 + j
                    mm(bank(rs_b[half])[:, j * 128:(j + 1) * 128], onesb[:], wsT[:, g, :], True, True, r=[H_t[1], c_t],
                       w=[PS_t[rs_b[half]]], sig=(j == 3))
                mm(bank(bs_b[half]), onesf[0:1, :], bsrow[0:1, half * 512:(half + 1) * 512], True, True, r=[bs_t, c_t],
                   w=[PS_t[bs_b[half]]], sig=True)
            bsr = H32(4)
            for half in range(2):
                cp("dve", bsr[:, half * 512:(half + 1) * 512], bank(bs_b[half]), r=[PS_t[bs_b[half]]], wp=[H_t[4]])
            for fc in range(16):
                g = fc // 2
                stt("dve", Ev[:, fc, :], bank(rs_b[g // 4])[:, (g % 4) * 128:(g % 4 + 1) * 128], lnb_t[:, i * 16 + fc:i * 16 + fc + 1],
                    bsr[:, g * 128:(g + 1) * 128], ALU.mult, ALU.add, r=[PS_t[rs_b[g // 4]], H_t[4], c_t], wp=F_t(1))

            def loads(n):
                p = n % 2
                dma("sp", Fs(2 + p), VR[n * 128:(n + 1) * 128, :], r=[VR_t[n]], w=F_t(2 + p))
                dma("sp", H(8 + p).rearrange("p (c t) -> p c t", t=128), UG[:, :, n * 128:(n + 1) * 128].rearrange("c p t -> p c t"),
                    r=UG_t, w=[H_t[8 + p]])

            loads(0)
            for n in range(16):
                if n + 1 < 16:
                    loads(n + 1)
                p = n % 2
                vt = Fs(2 + p)
                ug = H(8 + p).rearrange("p (c t) -> p c t", t=128)
                si = rr["st"] % 8
                rr["st"] += 1
                c = si * 8
                s_ = [st_t[si]]
                act(junk[:], vt, AF.Identity, r=F_t(2 + p), w=s_, accum=st[:, c:c + 1])
                act(junk[:], vt, AF.Square, r=F_t(2 + p), wp=s_, accum=st[:, c + 1:c + 2])
                ts("dve", st[:, c + 2:c + 3], st[:, c:c + 1], 1.0 / D, None, ALU.mult, None, r=s_, w=s_)
                tt("dve", st[:, c + 3:c + 4], st[:, c + 2:c + 3], st[:, c + 2:c + 3], ALU.mult, r=s_, w=s_)
                stt("dve", st[:, c + 4:c + 5], st[:, c + 1:c + 2], 1.0 / D, st[:, c + 3:c + 4], ALU.mult, ALU.subtract, r=s_, w=s_)
                act(st[:, c + 5:c + 6], st[:, c + 4:c + 5], AF.Sqrt, r=s_, w=s_, bias=EPS, scale=1.0)
                recip(st[:, c + 6:c + 7], st[:, c + 5:c + 6], r=s_, w=s_)
                stt("dve", st[:, c + 7:c + 8], st[:, c + 2:c + 3], -1.0, st[:, c + 6:c + 7], ALU.mult, ALU.mult, r=s_, w=s_)
                vn = H(10 + p)
                ts("dve", vn, vt, st[:, c + 6:c + 7], st[:, c + 7:c + 8], ALU.mult, ALU.add, r=F_t(2 + p) + s_, w=[H_t[10 + p]])
                for q in range(4):
                    b = nb()
                    for j in range(4):
                        fc = 4 * q + j
                        mm(bank(b)[:, j * 128:(j + 1) * 128], vn[:, fc * 128:(fc + 1) * 128], wsT[:, fc // 2, :], True, True,
                           r=[H_t[10 + p], H_t[1]], w=[PS_t[b]], sig=(j == 3))
                    ti = att["tmp"] % 2
                    att["tmp"] += 1
                    tmp = H32(13 + ti)[:, 0:512]
                    for j in range(4):
                        fc = 4 * q + j
                        stt("dve", tmp[:, j * 128:(j + 1) * 128], bank(b)[:, j * 128:(j + 1) * 128], lng_t[:, i * 16 + fc:i * 16 + fc + 1],
                            Ev[:, fc, :], ALU.mult, ALU.add, r=[PS_t[b], c_t] + F_t(1), wp=[H_t[13 + ti]])
                    for j in range(4):
                        fc = 4 * q + j
                        tt("pool", A3[:, fc, n * 128:(n + 1) * 128], tmp[:, j * 128:(j + 1) * 128], ug[:, fc, :], ALU.mult,
                           r=[H_t[13 + ti], H_t[8 + p]], wp=[A_t[fc][n // 4]])

        def whole():
            consts()
            chk("const")
            for cg in range(8):
                mod_group(0, cg)
                chk("mod%d" % cg)
            mod_flush()
            chk("mod")
            Xcur = x_in
            for l in range(depth):
                Xnext = XS[l % 2]
                norm_phase(l, Xcur, Xnext, has_y2=(l > 0), make_h=True)
                chk("norm%d" % l)
                if l > 0:
                    Xcur = Xnext
                mods = []
                if l == 0:
                    mods = [(lambda cg=cg: mod_group(0, cg)) for cg in range(8, 12)]
                if l + 1 < depth:
                    mods += [(lambda l1=l + 1, cg=cg: mod_group(l1, cg)) for cg in range(12)]
                i = l // 2
                if l % 2 == 0:
                    even_layer(i, mods)
                    w_out_l = ab_w_out[i]
                else:
                    odd_layer(i, mods)
                    w_out_l = sg_w_out[i]
                chk("mix%d" % l)
                out_phase(w_out_l, mods)
                chk("out%d" % l)
                while mods:
                    mods.pop(0)()
                mod_flush()
            norm_phase(depth, Xcur, out, has_y2=True, make_h=False)

        def reset_all():
            P.reset()
            for t in Tok.ALL:
                t.reset()
            for d_ in (rr, ev, att):
                for k_ in d_:
                    d_[k_] = 0
            plan["k"] = 0
            plan["issued"] = 0

        for pass_ in range(2):
            try:
                whole()
            except _Stop:
                pass
            if pass_ == 0:
                plan["replay"] = plan["specs"]
                reset_all()
        fin = {}
        for t in X_t[id(out)]:
            _merge(fin, t.w)
        P.stream["sp"].append((fin, None, None))

        with nc.Block() as block:
            @block.tensor
            def _(e):
                P.emit("pe", e)

            @block.scalar
            def _(e):
                P.emit("act", e)

            @block.vector
            def _(e):
                P.emit("dve", e)

            @block.gpsimd
            def _(e):
                P.emit("pool", e)

            @block.sync
            def _(e):
                P.emit("sp", e)
        build.stats = {e: len(s) for e, s in P.stream.items()}
    return nc


def _host_inputs(inputs, b):
    f = np.float32
    x = np.ascontiguousarray(inputs["x"][b], dtype=f)
    c = np.asarray(inputs["c"][b], dtype=f)

    def col(v):
        return np.ascontiguousarray(np.asarray(v, dtype=f).reshape(-1, 128).T)

    def cols(m):
        return np.ascontiguousarray(np.concatenate([col(m[l]) for l in range(m.shape[0])], axis=1))

    tab = np.asarray(inputs["b_rel_bias"], dtype=f)
    kk = np.arange(64)[:, None]
    cc = np.arange(320)[None, :]
    idx = np.minimum(cc - kk + 128, 256)
    TB = tab[:, :, idx]
    tb4 = np.full((2, 8, 128, 256), NEG, dtype=f)
    tb4[:, :, 0:64, :] = TB[:, :, :, 0:256]
    tb4[:, :, 64:128, 64:256] = TB[:, :, :, 0:192]
    tb3 = np.empty((2, 8, 128, 128), dtype=f)
    tb3[:, :, 0:64, :] = TB[:, :, :, 128:256]
    tb3[:, :, 64:128, :] = TB[:, :, :, 64:192]
    crep = np.ascontiguousarray(np.broadcast_to(tab[:, :, 256].reshape(1, 16), (128, 16)), dtype=f)
    half = 32
    freqs = (10000.0 ** (-np.arange(half, dtype=f) / f(half))).astype(f)
    ang = (np.arange(S, dtype=f)[None, :] * freqs[:, None]).astype(f)
    cs = np.concatenate([np.cos(ang), np.cos(ang), np.sin(ang), np.sin(ang)], 0).astype(f)
    d = {
        "x": x, "ccol": col(c),
        "w_mod": inputs["w_mod"], "b_mod": inputs["b_mod"],
        "gpre_col": cols(np.asarray(inputs["g_pre"])), "g_post": inputs["g_post"],
        "ab_w_in": inputs["ab_w_in"], "gq_col": cols(np.asarray(inputs["a_g_q"])), "a_w_uq": inputs["a_w_uq"],
        "gkv_col": cols(np.asarray(inputs["a_g_kv"])), "a_w_ukv": inputs["a_w_ukv"],
        "tb4": tb4, "tb3": tb3, "crep": crep, "ab_w_out": inputs["ab_w_out"],
        "sg_w_in": inputs["sg_w_in"], "lng_col": cols(np.asarray(inputs["sg_ln_g"])), "lnb_col": cols(np.asarray(inputs["sg_ln_b"])),
        "sg_w_s": inputs["sg_w_s"], "sg_b_s": np.asarray(inputs["sg_b_s"], dtype=f).reshape(2, 1024), "sg_w_out": inputs["sg_w_out"],
        "ident": np.eye(128, dtype=f), "ropecs": cs,
    }
    return {k: np.ascontiguousarray(np.asarray(v, dtype=f)) for k, v in d.items()}


_NC = {}


def kernel(**inputs):
    inputs = {k: np.asarray(v) for k, v in inputs.items()}
    if DEPTH not in _NC:
        _NC[DEPTH] = build(DEPTH)
    nc = _NC[DEPTH]
    shared = None
    in_maps = []
    for b in range(8):
        m = _host_inputs(inputs, b) if shared is None else dict(shared)
        if shared is None:
            shared = m
        else:
            m["x"] = np.ascontiguousarray(inputs["x"][b], dtype=np.float32)
            m["ccol"] = np.ascontiguousarray(np.asarray(inputs["c"][b], dtype=np.float32).reshape(-1, 128).T)
        in_maps.append(m)
    res = run_bass_kernel_spmd(nc, in_maps, core_ids=list(range(8)))
    return np.stack([np.asarray(r["out"]) for r in res.results], axis=0).astype(np.float32)
```

```python
import numpy as np
from contextlib import ExitStack
import concourse.bass as bass
import concourse.mybir as mybir
from concourse.bass_utils import run_bass_kernel_spmd

F32 = mybir.dt.float32
BF16 = mybir.dt.bfloat16
AF = mybir.ActivationFunctionType
ALU = mybir.AluOpType

S = 2048
D = 2048
DEPTH = 4
EPS = 1e-6
NEG = -30000.0
CE = ("pe", "act", "dve", "pool")


class Tok:
    __slots__ = ("name", "w", "r", "closed", "pr", "psum")

    ALL = []

    def __init__(self, name, psum=False):
        self.name = name
        self.psum = psum
        self.reset()
        Tok.ALL.append(self)

    def reset(self):
        self.w = {}
        self.r = {}
        self.pr = {}
        self.closed = False


def _merge(d, s):
    for k, v in s.items():
        if d.get(k, 0) < v:
            d[k] = v


class Prog:
    NS = 8

    def __init__(self, nc, es):
        self.nc = nc
        self.sem = {e: es.enter_context(nc.semaphore("s_" + e)) for e in CE}
        self.cnt = {e: 0 for e in CE}
        self.dq = ("sp", "pool", "act")
        self.dsem = {q: [es.enter_context(nc.semaphore("d_%s%d" % (q, i))) for i in range(self.NS)] for q in self.dq}
        self.reset()

    def reset(self):
        self.cnt = {e: 0 for e in CE}
        self.dval = {q: [0] * self.NS for q in self.dq}
        self.dnext = {q: 0 for q in self.dq}
        self.stream = {e: [] for e in ("pe", "act", "dve", "pool", "sp")}
        self.pend = {e: ([], [], []) for e in CE}
        self.pendset = {e: set() for e in CE}
        self.nops = 0

    def semh(self, key):
        if isinstance(key, tuple):
            return self.dsem[key[0]][key[1]]
        return self.sem[key]

    def _publish(self, r, w, wp, key, val):
        for t in r:
            if t.r.get(key, 0) < val:
                t.r[key] = val
            t.closed = True
        for t in w:
            t.w = {key: val}
            t.r = {}
            t.pr = {key: val}
            t.closed = False
        for t in wp:
            if t.closed:
                t.pr = t.r
                t.w = {key: val}
                t.r = {}
                t.closed = False
            else:
                if t.w.get(key, 0) < val:
                    t.w[key] = val

    def op(self, eng, fn, r=(), w=(), wp=(), sig=True, dma=False):
        self.nops += 1
        waits = {}
        if eng != "pe":
            xs = [t for t in (*r, *w, *wp) if t.psum]
            if xs:
                r = [t for t in r if not t.psum]
                wp = [t for t in wp if not t.psum]
                w = [t for t in w if not t.psum] + xs
                for t in xs:
                    for d_ in (t.w, t.r, t.pr):
                        for k, v in d_.items():
                            if k != eng and waits.get(k, 0) < v:
                                waits[k] = v
                own = waits.get(eng)
            else:
                own = None
        else:
            xs = ()
            own = None
        for t in (*r, *w, *wp):
            for e2, ps in self.pendset.items():
                if e2 != eng and t in ps:
                    raise RuntimeError("token %s pending on %s touched by %s" % (t.name, e2, eng))
        for t in r:
            _merge(waits, t.w)
        for t in w:
            _merge(waits, t.w)
            _merge(waits, t.r)
        for t in wp:
            _merge(waits, t.r)
            if not t.closed:
                _merge(waits, t.pr)
        if eng == "pe":
            waits.pop("pe", None)
        elif xs:
            ownv = 0
            for t in r:
                ownv = max(ownv, t.w.get(eng, 0))
            for t in w:
                if not t.psum:
                    ownv = max(ownv, t.w.get(eng, 0), t.r.get(eng, 0))
            for t in wp:
                ownv = max(ownv, t.r.get(eng, 0))
                if not t.closed:
                    ownv = max(ownv, t.pr.get(eng, 0))
            if ownv:
                waits[eng] = ownv
            else:
                waits.pop(eng, None)
        if dma:
            q = eng
            slot = self.dnext[q] % self.NS
            self.dnext[q] += 1
            key = (q, slot)
            if self.dval[q][slot] > 0:
                waits[key] = max(waits.get(key, 0), self.dval[q][slot])
            self.dval[q][slot] += 16
            self._publish(r, w, wp, key, self.dval[q][slot])
            inc = (self.dsem[q][slot], 16)
        else:
            pr, pw, pwp = self.pend[eng]
            pr.extend(r)
            pw.extend(w)
            pwp.extend(wp)
            if sig:
                self.cnt[eng] += 1
                self._publish(pr, pw, pwp, eng, self.cnt[eng])
                self.pend[eng] = ([], [], [])
                self.pendset[eng] = set()
                inc = (self.sem[eng], 1)
            else:
                assert eng == "pe"
                self.pendset[eng].update(r)
                self.pendset[eng].update(w)
                self.pendset[eng].update(wp)
                inc = None
        self.stream[eng].append((waits, fn, inc))

    def emit(self, eng, e):
        seen = {}
        for waits, fn, inc in self.stream[eng]:
            for k, v in waits.items():
                if seen.get(k, 0) >= v:
                    continue
                seen[k] = v
                e.wait_ge(self.semh(k), v)
            if fn is not None:
                ins = fn(e)
                if inc is not None:
                    ins.then_inc(inc[0], inc[1])


class _Stop(Exception):
    pass


def build(depth=DEPTH, stop=None):
    Tok.ALL = []
    nc = bass.Bass("TRN2", target_bir_lowering=False)

    def din(name, shape, dt=F32):
        return nc.dram_tensor(name, list(shape), dt, kind="ExternalInput").ap()

    def dscr(name, shape, dt):
        return nc.dram_tensor(name, list(shape), dt).ap()

    x_in = din("x", [S, D])
    ccol = din("ccol", [128, 16])
    w_mod = din("w_mod", [4, D, 6144])
    b_mod = din("b_mod", [4, 6144])
    gpre_col = din("gpre_col", [128, 64])
    g_post = din("g_post", [4, D])
    ab_w_in = din("ab_w_in", [2, D, 5952])
    gq_col = din("gq_col", [128, 8])
    a_w_uq = din("a_w_uq", [2, 512, 1536])
    gkv_col = din("gkv_col", [128, 4])
    a_w_ukv = din("a_w_ukv", [2, 256, 2048])
    tb4 = din("tb4", [2, 8, 128, 256])
    tb3 = din("tb3", [2, 8, 128, 128])
    crep = din("crep", [128, 16])
    ab_w_out = din("ab_w_out", [2, D, D])
    sg_w_in = din("sg_w_in", [2, D, 6144])
    lng_col = din("lng_col", [128, 32])
    lnb_col = din("lnb_col", [128, 32])
    sg_w_s = din("sg_w_s", [2, 8, 128, 128])
    sg_b_s = din("sg_b_s", [2, 1024])
    sg_w_out = din("sg_w_out", [2, D, D])
    ident_d = din("ident", [128, 128])
    ropecs = din("ropecs", [128, S])
    out = nc.dram_tensor("out", [S, D], F32, kind="ExternalOutput").ap()

    XS = [dscr("XA", [S, D], F32), dscr("XB", [S, D], F32)]
    Y2 = dscr("Y2", [S, D], F32)
    G2ROW = dscr("G2ROW", [4, D], F32)
    QN = dscr("QN", [8, 128, S], BF16)
    QR = dscr("QR", [8, 64, S], BF16)
    KN = dscr("KN", [8, 128, S], BF16)
    VA = dscr("VA", [S, 1024], BF16)
    BQ = dscr("BQ", [8, 128, S], BF16)
    BK = dscr("BK", [8, 128, S], BF16)
    BV = dscr("BV", [S, 1024], BF16)
    GT = dscr("GT", [16, 128, S], BF16)
    UG = dscr("UG", [16, 128, S], BF16)
    VR = dscr("VR", [S, D], F32)

    es = ExitStack()
    with es:
        def sb(name, shape, dt):
            return es.enter_context(nc.sbuf_tensor(name, list(shape), dt))

        A = sb("A", [128, 16 * 2048], BF16)
        A3 = A[:].rearrange("p (k t) -> p k t", t=2048)
        WB = [sb("WB%d" % i, [128, 8192], BF16) for i in range(3)]
        SL = [sb("SL%d" % i, [128, 4096], BF16) for i in range(8)]
        PS = es.enter_context(nc.psum_tensor("PS", [128, 4096], F32))
        identb = sb("identb", [128, 128], BF16)
        onesb = sb("onesb", [128, 128], BF16)
        onesf = sb("onesf", [128, 128], F32)
        csT = sb("csT", [128, 16], BF16)
        ccol_t = sb("ccol_t", [128, 16], F32)
        gpre_t = sb("gpre_t", [128, 64], F32)
        gq_t = sb("gq_t", [128, 8], F32)
        gkv_t = sb("gkv_t", [128, 4], F32)
        crep_t = sb("crep_t", [128, 16], F32)
        lng_t = sb("lng_t", [128, 32], F32)
        lnb_t = sb("lnb_t", [128, 32], F32)
        Acol = sb("Acol", [128, 64], F32)
        Bcol = sb("Bcol", [128, 64], F32)
        st = sb("st", [128, 64], F32)
        rows = [sb("row%d" % i, [1, 512], F32) for i in range(2)]
        foldb = sb("foldb", [128, 64], BF16)
        brow = [sb("brow%d" % i, [1, 512], F32) for i in range(2)]
        grow = [sb("grow%d" % i, [1, 512], F32) for i in range(2)]
        identf = sb("identf", [128, 128], F32)
        diag = [sb("diag%d" % i, [128, 128], F32) for i in range(2)]
        junk = sb("junk", [128, 2048], BF16)
        bt4 = [sb("bt4_%d" % i, [128, 256], F32) for i in range(2)]
        bt3 = [sb("bt3_%d" % i, [128, 128], F32) for i in range(2)]
        bsrow = sb("bsrow", [1, 1024], F32)

        P = Prog(nc, es)

        A_t = [[Tok("A%d_%d" % (k, g)) for g in range(4)] for k in range(16)]
        WB_t = [Tok("WB%d" % i) for i in range(3)]
        H_t = [Tok("H%d" % i) for i in range(16)]
        PS_t = [Tok("PS%d" % i, psum=True) for i in range(8)]
        c_t = Tok("consts")
        st_t = [Tok("st%d" % i) for i in range(8)]
        row_t = [Tok("row%d" % i) for i in range(2)]
        brow_t = [Tok("brow%d" % i) for i in range(2)]
        grow_t = [Tok("grow%d" % i) for i in range(2)]
        diag_t = [Tok("diag%d" % i) for i in range(2)]
        bt_t = [Tok("bt%d" % i) for i in range(2)]
        AB_t = [Tok("AB%d" % l) for l in range(4)]
        X_t = {id(XS[0]): [Tok("XA%d" % i) for i in range(16)], id(XS[1]): [Tok("XB%d" % i) for i in range(16)],
               id(out): [Tok("out%d" % i) for i in range(16)], id(x_in): [Tok("xin%d" % i) for i in range(16)]}
        Y2_t = [Tok("Y2_%d" % i) for i in range(16)]
        G2_t = [Tok("G2_%d" % i) for i in range(4)]
        QN_t = [Tok("QN%d" % i) for i in range(8)]
        QR_t = [Tok("QR%d" % i) for i in range(8)]
        KN_t = [Tok("KN%d" % i) for i in range(8)]
        VA_t = Tok("VA")
        BQ_t = [Tok("BQ%d" % i) for i in range(8)]
        BK_t = [Tok("BK%d" % i) for i in range(8)]
        BV_t = Tok("BV")
        GT_t = [Tok("GT%d" % i) for i in range(16)]
        UG_t = [Tok("UG%d" % i) for i in range(16)]
        VR_t = [Tok("VR%d" % i) for i in range(16)]
        bs_t = Tok("bsrow")

        def H(i):
            return SL[i // 2][:, (i % 2) * 2048:(i % 2) * 2048 + 2048]

        def H32(i):
            return H(i).bitcast(F32)

        def Fs(j):
            return SL[j][:].bitcast(F32)

        def F_t(j):
            return [H_t[2 * j], H_t[2 * j + 1]]

        def bank(b):
            return PS[:, b * 512:(b + 1) * 512]

        def mm(o, lhsT, rhs, start, stop, r, w, sig):
            P.op("pe", lambda e: e.matmul(o, lhsT, rhs, start=start, stop=stop), r=r, w=w, sig=sig)

        def act(o, i, func, r, w=(), wp=(), bias=None, scale=None, accum=None):
            kw = {}
            if bias is not None:
                kw["bias"] = bias
            if scale is not None:
                kw["scale"] = scale
            if accum is not None:
                kw["accum_out"] = accum
            P.op("act", lambda e: e.activation(out=o, in_=i, func=func, **kw), r=r, w=w, wp=wp)

        def ts(eng, o, i, s1, s2, op0, op1, r, w=(), wp=()):
            if s2 is None:
                P.op(eng, lambda e: e.tensor_scalar(o, i, s1, None, op0), r=r, w=w, wp=wp)
            else:
                P.op(eng, lambda e: e.tensor_scalar(o, i, s1, s2, op0, op1), r=r, w=w, wp=wp)

        def stt(eng, o, i0, sc, i1, op0, op1, r, w=(), wp=()):
            P.op(eng, lambda e: e.scalar_tensor_tensor(o, i0, sc, i1, op0, op1), r=r, w=w, wp=wp)

        def tt(eng, o, i0, i1, op, r, w=(), wp=()):
            P.op(eng, lambda e: e.tensor_tensor(o, i0, i1, op), r=r, w=w, wp=wp)

        def cp(eng, o, i, r, w=(), wp=()):
            P.op(eng, lambda e: e.tensor_copy(o, i), r=r, w=w, wp=wp)

        def recip(o, i, r, w=(), wp=()):
            P.op("dve", lambda e: e.reciprocal(o, i), r=r, w=w, wp=wp)

        def mset(eng, o, val, w=(), wp=()):
            P.op(eng, lambda e: e.memset(o, val), w=w, wp=wp)

        def dma(q, o, i, r=(), w=(), wp=(), slow=False):
            if slow:
                P.op(q, lambda e: e.dma_start(out=o, in_=i, allow_slow_non_contiguous=True), r=r, w=w, wp=wp, dma=True)
            else:
                P.op(q, lambda e: e.dma_start(out=o, in_=i), r=r, w=w, wp=wp, dma=True)

        def chk(name):
            if stop == name:
                raise _Stop()

        rr = {"bank": 0, "fm": 0, "w": 0, "row": 0, "st": 0}

        def nb():
            b = rr["bank"] % 8
            rr["bank"] += 1
            return b

        def nfm():
            b = 4 * (rr["fm"] % 2)
            rr["fm"] += 1
            return b

        plan = {"specs": [], "replay": None, "k": 0, "issued": 0}

        def wload_generic(issue_fn):
            k = plan["k"]
            plan["k"] += 1
            if plan["replay"] is None:
                plan["specs"].append(issue_fn)
                issue_fn(k % 3)
            else:
                specs = plan["replay"]
                while plan["issued"] <= min(k + 1, len(specs) - 1):
                    j = plan["issued"]
                    tb_ = WB_t[j % 3]
                    assert tb_.closed or not tb_.w, "weight buffer %d reloaded before its consumers were recorded (load %d)" % (j % 3, j)
                    specs[j](j % 3)
                    plan["issued"] += 1
            return k % 3

        def wload(src, nkc, ncols, bufcols=None, col_off=0):
            bufcols = bufcols or ncols

            def view(i):
                return WB[i][:, 0:nkc * bufcols].rearrange("p (k c) -> p k c", c=bufcols)

            def issue(i):
                dma("pool", view(i)[:, :, col_off:col_off + ncols], src.rearrange("(k p) c -> p k c", p=128), w=[WB_t[i]])

            i = wload_generic(issue)
            return view(i), WB_t[i]

        def consts():
            dma("pool", identb[:], ident_d[:, :], w=[c_t])
            dma("sp", identf[:], ident_d[:, :], wp=[c_t])
            mset("dve", onesb[:], 1.0, wp=[c_t])
            mset("dve", onesf[:], 1.0, wp=[c_t])
            dma("sp", ccol_t[:], ccol[:, :], wp=[c_t])
            dma("sp", gpre_t[:], gpre_col[:, :], wp=[c_t])
            dma("sp", gq_t[:], gq_col[:, :], wp=[c_t])
            dma("sp", gkv_t[:], gkv_col[:, :], wp=[c_t])
            dma("sp", crep_t[:], crep[:, :], wp=[c_t])
            dma("sp", lng_t[:], lng_col[:, :], wp=[c_t])
            dma("sp", lnb_t[:], lnb_col[:, :], wp=[c_t])
            act(csT[:], ccol_t[:], AF.Silu, r=[c_t], wp=[c_t])
            tt("dve", foldb[:], identb[:, 0:64], identb[:, 64:128], ALU.add, r=[c_t], wp=[c_t])

        modq = []

        def mod_rows_prefetch(l, cg):
            bi = cg % 2
            dma("sp", brow[bi][:], b_mod[l:l + 1, cg * 512:(cg + 1) * 512], w=[brow_t[bi]])
            if cg >= 8:
                g0 = (cg - 8) * 512
                dma("sp", grow[bi][:], g_post[l:l + 1, g0:g0 + 512], w=[grow_t[bi]])

        def mod_flush():
            while modq:
                modq.pop(0)()

        def mod_group(l, cg):
            wv, wt = wload(w_mod[l, :, cg * 512:(cg + 1) * 512], 16, 512)
            mod_flush()
            if cg == 0:
                mod_rows_prefetch(l, 0)
            if cg + 1 < 12:
                mod_rows_prefetch(l, cg + 1)
            b = nb()
            for kc in range(16):
                mm(bank(b)[0:1, :], csT[:, kc:kc + 1], wv[:, kc, :], kc == 0, kc == 15, r=[wt, c_t], w=[PS_t[b]], sig=(kc == 15))
            bi = cg % 2
            ri = cg % 2
            tt("dve", rows[ri][:], bank(b)[0:1, :], brow[bi][:], ALU.add, r=[PS_t[b], brow_t[bi]], w=[row_t[ri]])
            if cg < 8:
                def fin():
                    b2 = nb()
                    for j in range(4):
                        mm(bank(b2)[:, j:j + 1], rows[ri][0:1, j * 128:(j + 1) * 128], onesf[0:1, 0:1], True, True,
                           r=[row_t[ri], c_t], w=[PS_t[b2]], sig=(j == 3))
                    if cg < 4:
                        c0 = l * 16 + cg * 4
                        cp("dve", Bcol[:, c0:c0 + 4], bank(b2)[:, 0:4], r=[PS_t[b2]], wp=[AB_t[l]])
                    else:
                        c0 = l * 16 + (cg - 4) * 4
                        stt("dve", Acol[:, c0:c0 + 4], bank(b2)[:, 0:4], 1.0, gpre_t[:, c0:c0 + 4], ALU.add, ALU.mult,
                            r=[PS_t[b2], c_t], wp=[AB_t[l]])
                modq.append(fin)
            else:
                g0 = (cg - 8) * 512
                tt("dve", rows[ri][:], rows[ri][:], grow[bi][:], ALU.mult, r=[row_t[ri], grow_t[bi]], w=[row_t[ri]])
                dma("sp", G2ROW[l:l + 1, g0:g0 + 512], rows[ri][:], r=[row_t[ri]], wp=[G2_t[l]])

        def norm_gen(l, Xprev, Xnext, has_y2, make_h):
            G2rep = Fs(5)
            if has_y2:
                dma("sp", G2rep, G2ROW[l - 1:l, :].broadcast_to([128, D]), r=[G2_t[l - 1]], w=F_t(5))

            def ysl(t_):
                return t_ % 2

            def xsl(t_):
                return 2 + t_ % 3

            def loads(t_):
                if has_y2:
                    dma("sp", Fs(ysl(t_)), Y2[t_ * 128:(t_ + 1) * 128, :], r=[Y2_t[t_]], w=F_t(ysl(t_)))
                dma("sp", Fs(xsl(t_)), Xprev[t_ * 128:(t_ + 1) * 128, :], r=[X_t[id(Xprev)][t_]], w=F_t(xsl(t_)))

            cols = {}

            def stage_a(t_):
                y2 = Fs(ysl(t_))
                xt = Fs(xsl(t_))
                yt_, xt_ = F_t(ysl(t_)), F_t(xsl(t_))
                si = rr["st"] % 8
                rr["st"] += 1
                c = si * 8
                cols[t_] = (si, c)
                stt_ = [st_t[si]]
                if has_y2:
                    act(junk[:], y2, AF.Square, r=yt_, w=stt_, accum=st[:, c:c + 1])
                    act(st[:, c + 1:c + 2], st[:, c:c + 1], AF.Sqrt, r=stt_, w=stt_, bias=EPS, scale=1.0 / D)
                    recip(st[:, c + 2:c + 3], st[:, c + 1:c + 2], r=stt_, w=stt_)
                    tt("dve", y2, y2, G2rep, ALU.mult, r=yt_ + F_t(5), w=yt_)
                    stt("dve", xt, y2, st[:, c + 2:c + 3], xt, ALU.mult, ALU.add, r=yt_ + xt_ + stt_, w=xt_)
                    dma("sp", Xnext[t_ * 128:(t_ + 1) * 128, :], xt, r=xt_, w=[X_t[id(Xnext)][t_]])

            def stage_b(t_):
                xt = Fs(xsl(t_))
                xt_ = F_t(xsl(t_))
                si, c = cols[t_]
                stt_ = [st_t[si]]
                act(junk[:], xt, AF.Square, r=xt_, w=stt_, accum=st[:, c + 3:c + 4])
                act(st[:, c + 4:c + 5], st[:, c + 3:c + 4], AF.Sqrt, r=stt_, w=stt_, bias=EPS, scale=1.0 / D)
                recip(st[:, c + 5:c + 6], st[:, c + 4:c + 5], r=stt_, w=stt_)
                xs_ = 12 + t_ % 2
                act(H(xs_), xt, AF.Copy, r=xt_ + stt_, w=[H_t[xs_]], scale=st[:, c + 5:c + 6])

            def stage_b2(t_):
                xs_ = 12 + t_ % 2
                xn = H(xs_)
                for q in range(4):
                    b = nb()
                    for j in range(4):
                        fc = 4 * q + j
                        mm(bank(b)[:, j * 128:(j + 1) * 128], xn[:, fc * 128:(fc + 1) * 128], identb[:], True, True,
                           r=[H_t[xs_], c_t], w=[PS_t[b]], sig=(j == 3))
                    for j in range(4):
                        fc = 4 * q + j
                        o = A3[:, fc, t_ * 128:(t_ + 1) * 128]
                        i_ = bank(b)[:, j * 128:(j + 1) * 128]
                        ac = Acol[:, l * 16 + fc:l * 16 + fc + 1]
                        bc = Bcol[:, l * 16 + fc:l * 16 + fc + 1]
                        if q % 2 == 0:
                            act(o, i_, AF.Identity, r=[PS_t[b], AB_t[l]], wp=[A_t[fc][t_ // 4]], bias=bc, scale=ac)
                        else:
                            ts("dve", o, i_, ac, bc, ALU.mult, ALU.add, r=[PS_t[b], AB_t[l]], wp=[A_t[fc][t_ // 4]])

            loads(0)
            loads(1)
            stage_a(0)
            for t_ in range(16):
                if make_h and t_ >= 1:
                    stage_b2(t_ - 1)
                if t_ + 2 < 16:
                    loads(t_ + 2)
                if t_ + 1 < 16:
                    stage_a(t_ + 1)
                if make_h:
                    stage_b(t_)
                yield t_
            if make_h:
                stage_b2(15)

        def fm_block(wv, wt, cols, nkc, rhs_fn, b0, M=128, banks=None):
            for kc in range(nkc):
                for tg in range(4):
                    rhs, rt = rhs_fn(kc, tg)
                    b = b0 + tg
                    mm(bank(b)[0:M, :], wv[:, kc, cols], rhs, kc == 0, kc == nkc - 1, r=[wt] + rt, w=[PS_t[b]], sig=(kc == nkc - 1))

        def rhs_A(kc, tg):
            return A3[:, kc, tg * 512:(tg + 1) * 512], [A_t[kc][tg]]

        def tm_tile(t_, rhs_fn, nkc, lhs_fn):
            b = nb()
            for kc in range(nkc):
                lhsT, lt = lhs_fn(kc, t_)
                rhs, rt = rhs_fn(kc)
                mm(bank(b), lhsT, rhs, kc == 0, kc == nkc - 1, r=lt + rt, w=[PS_t[b]], sig=(kc == nkc - 1))
            return b

        def lhs_A(kc, t_):
            return A3[:, kc, t_ * 128:(t_ + 1) * 128], [A_t[kc][t_ // 4]]

        ev = {"i": 0}

        def evac_copy(o, i_, r, w=(), wp=()):
            ev["i"] += 1
            if ev["i"] % 2 == 0:
                act(o, i_, AF.Copy, r=r, w=w, wp=wp)
            else:
                cp("dve", o, i_, r=r, w=w, wp=wp)

        def fm_to_dram(wv, wt, cols, dst, dst_t, hslot, silu=False):
            b0 = nfm()
            fm_block(wv, wt, cols, 16, rhs_A, b0)
            o = H(hslot)
            src = PS[:, b0 * 512:(b0 + 4) * 512]
            rt = [PS_t[b0 + i] for i in range(4)]
            if silu:
                act(o, src, AF.Silu, r=rt, w=[H_t[hslot]])
            else:
                for tg in range(4):
                    evac_copy(o[:, tg * 512:(tg + 1) * 512], bank(b0 + tg), r=[PS_t[b0 + tg]], wp=[H_t[hslot]])
            dma("sp", dst, o, r=[H_t[hslot]], w=[dst_t])

        def out_phase(w_out_l, mods, ngen):
            hs = [14, 15]
            k = 0
            adv = 0
            for half in range(2):
                if half == 1:
                    while len(mods) > 4:
                        mods.pop(0)()
                    mod_flush()
                for cg in range(4):
                    if mods:
                        mods.pop(0)()
                    wv, wt = wload(w_out_l[:, cg * 512:(cg + 1) * 512], 16, 512)
                    for t_ in range(8 * half, 8 * half + 8):
                        b = tm_tile(t_, lambda kc: (wv[:, kc, :], [wt]), 16, lhs_A)
                        hi = hs[(k // 2) % 2]
                        o = H32(hi)[:, (k % 2) * 512:(k % 2) * 512 + 512]
                        k += 1
                        evac_copy(o, bank(b), r=[PS_t[b]], wp=[H_t[hi]])
                        dma("sp", Y2[t_ * 128:(t_ + 1) * 128, cg * 512:(cg + 1) * 512], o, r=[H_t[hi]], wp=[Y2_t[t_]])
                        if half == 1 and t_ % 4 == 3 and adv < 6:
                            next(ngen)
                            adv += 1

        def even_layer(i, mods):
            w_in = ab_w_in[i]
            CS = Fs(3)
            dma("sp", CS, ropecs[:, :], w=F_t(3))
            cqn = [H(0), H(1), H(2), H(3)]
            ckvn = [H(4), H(5)]
            KRs = 10
            KR = H(KRs)
            sq_h = [H32(8), H32(9)]
            rst_h = H32(11)
            Tt = H(11)[:, 1024:1536]

            def rope_fold(src_bank, tg, dst, dst_tok):
                tt("dve", Tt, bank(src_bank), CS[:, tg * 512:(tg + 1) * 512], ALU.mult, r=[PS_t[src_bank]] + F_t(3), w=[H_t[11]])
                bf = nb()
                mm(bank(bf)[0:64, :], foldb[:], Tt, True, True, r=[H_t[11], c_t], w=[PS_t[bf]], sig=True)
                evac_copy(dst[0:64, tg * 512:(tg + 1) * 512], bank(bf)[0:64, :], r=[PS_t[bf]], wp=[dst_tok])

            def lowrank_group(wv, wt, ncb, g_t, gcol0, outs, with_kr):
                for tg in range(4):
                    for cb in range(ncb):
                        for kc in range(16):
                            mm(bank(cb), wv[:, kc, cb * 128:(cb + 1) * 128], A3[:, kc, tg * 512:(tg + 1) * 512],
                               kc == 0, kc == 15, r=[wt, A_t[kc][tg]], w=[PS_t[cb]], sig=(kc == 15))
                    if with_kr:
                        for kc in range(16):
                            mm(bank(2), wv[:, kc, 256:384], A3[:, kc, tg * 512:(tg + 1) * 512],
                               kc == 0, kc == 15, r=[wt, A_t[kc][tg]], w=[PS_t[2]], sig=(kc == 15))
                    sbk = 4 + (tg % 2)
                    for cb in range(ncb):
                        sq = sq_h[cb % 2][:, (cb // 2 % 2) * 512:(cb // 2 % 2) * 512 + 512]
                        sq_tok = H_t[8 + cb % 2]
                        act(sq, bank(cb), AF.Square, r=[PS_t[cb]], w=[sq_tok])
                        mm(bank(sbk), onesf[:], sq, cb == 0, cb == ncb - 1, r=[sq_tok, c_t], w=[PS_t[sbk]], sig=True)
                    rst = rst_h[:, 0:512]
                    act(rst, bank(sbk), AF.Sqrt, r=[PS_t[sbk]], w=[H_t[11]], bias=EPS, scale=1.0 / (ncb * 128))
                    recip(rst, rst, r=[H_t[11]], w=[H_t[11]])
                    for cb in range(ncb):
                        o, ot = outs[cb]
                        stt("dve", o[:, tg * 512:(tg + 1) * 512], bank(cb), g_t[:, gcol0 + cb:gcol0 + cb + 1], rst, ALU.mult, ALU.mult,
                            r=[PS_t[cb], H_t[11], c_t], wp=[ot])
                    if with_kr:
                        rope_fold(2, tg, KR, H_t[KRs])

            wv, wt = wload(w_in[:, 0:512], 16, 512)
            wv1, wt1 = wload(w_in[:, 512:832], 16, 320, bufcols=384)
            lowrank_group(wv, wt, 4, gq_t, i * 4, [(cqn[k], H_t[k]) for k in range(4)], False)
            chk("g0")
            ts("dve", wv1[:, :, 320:352], wv1[:, :, 288:320], -1.0, None, ALU.mult, None, r=[wt1], wp=[wt1])
            cp("dve", wv1[:, :, 352:384], wv1[:, :, 256:288], r=[wt1], wp=[wt1])
            lowrank_group(wv1, wt1, 2, gkv_t, i * 2, [(ckvn[k], H_t[4 + k]) for k in range(2)], True)

            chk("g1")
            if mods:
                mods.pop(0)()
            wkv, wkt = wload(a_w_ukv[i], 2, 2048)
            wkv4 = wkv.rearrange("p k (h c) -> p k h c", c=256)
            def wq_view(i_):
                return WB[i_][:, 0:8192].rearrange("p (k h c) -> p k h c", k=4, c=256)

            def wq_issue(i_):
                for kc in range(4):
                    dma("pool", wq_view(i_)[:, kc, :, 0:192], a_w_uq[i, kc * 128:(kc + 1) * 128, :].rearrange("p (h c) -> p h c", c=192),
                        w=[WB_t[i_]] if kc == 0 else [], wp=[] if kc == 0 else [WB_t[i_]])

            iq = wload_generic(wq_issue)
            wqt = WB_t[iq]
            wq4 = wq_view(iq)
            ts("dve", wq4[:, :, :, 192:224], wq4[:, :, :, 160:192], -1.0, None, ALU.mult, None, r=[wqt], wp=[wqt])
            cp("dve", wq4[:, :, :, 224:256], wq4[:, :, :, 128:160], r=[wqt], wp=[wqt])

            def rhs_ckv(kc, tg):
                return ckvn[kc][:, tg * 512:(tg + 1) * 512], [H_t[4 + kc]]

            eh = [12, 13, 14, 15]
            ek = 0
            for h in range(8):
                b0 = nfm()
                fm_block(wkv, wkt, slice(h * 256, h * 256 + 128), 2, rhs_ckv, b0)
                hi = eh[ek % 4]
                ek += 1
                for tg in range(4):
                    evac_copy(H(hi)[:, tg * 512:(tg + 1) * 512], bank(b0 + tg), r=[PS_t[b0 + tg]], wp=[H_t[hi]])
                dma("sp", KN[h], H(hi), r=[H_t[hi]], w=[KN_t[h]])
            for t_ in range(16):
                for half in range(2):
                    b = tm_tile(t_, lambda kc: (wkv4[:, kc, 4 * half:4 * half + 4, 128:256], [wkt]), 2,
                                lambda kc, t2: (ckvn[kc][:, t2 * 128:(t2 + 1) * 128], [H_t[4 + kc]]))
                    hi = eh[ek % 4]
                    ek += 1
                    o = H(hi)[:, 0:512]
                    evac_copy(o, bank(b), r=[PS_t[b]], w=[H_t[hi]])
                    dma("sp", VA[t_ * 128:(t_ + 1) * 128, half * 512:(half + 1) * 512], o, r=[H_t[hi]], wp=[VA_t])
            chk("kv")
            for h in range(8):
                b0 = nfm()
                for kc in range(4):
                    for tg in range(4):
                        mm(bank(b0 + tg), wq4[:, kc, h, 0:128], cqn[kc][:, tg * 512:(tg + 1) * 512], kc == 0, kc == 3,
                           r=[wqt, H_t[kc]], w=[PS_t[b0 + tg]], sig=(kc == 3))
                hi = eh[ek % 4]
                ek += 1
                for tg in range(4):
                    evac_copy(H(hi)[:, tg * 512:(tg + 1) * 512], bank(b0 + tg), r=[PS_t[b0 + tg]], wp=[H_t[hi]])
                dma("sp", QN[h], H(hi), r=[H_t[hi]], w=[QN_t[h]])
                hi = eh[ek % 4]
                ek += 1
                for tg in range(4):
                    br = nb()
                    for kc in range(4):
                        mm(bank(br), wq4[:, kc, h, 128:256], cqn[kc][:, tg * 512:(tg + 1) * 512],
                           kc == 0, kc == 3, r=[wqt, H_t[kc]], w=[PS_t[br]], sig=(kc == 3))
                    rope_fold(br, tg, H(hi), H_t[hi])
                dma("sp", QR[h], H(hi)[0:64, :], r=[H_t[hi]], w=[QR_t[h]])

            chk("q")
            def fm_cols(c0, nblk, dst, dst_t, blk0, silu):
                nonlocal ek
                done = 0
                while done < nblk:
                    n = min(4, nblk - done)
                    if mods:
                        mods.pop(0)()
                    wv_, wt_ = wload(w_in[:, c0 + done * 128:c0 + (done + n) * 128], 16, n * 128)
                    for cb in range(n):
                        hi = eh[ek % 4]
                        ek += 1
                        blk = blk0 + done + cb
                        fm_to_dram(wv_, wt_, slice(cb * 128, (cb + 1) * 128), dst[blk], dst_t[blk], hi, silu=silu)
                    done += n

            fm_cols(832, 8, BQ, BQ_t, 0, False)
            fm_cols(1856, 8, BK, BK_t, 0, False)
            for cg in range(2):
                if mods:
                    mods.pop(0)()
                wv_, wt_ = wload(w_in[:, 2880 + cg * 512:2880 + (cg + 1) * 512], 16, 512)
                for t_ in range(16):
                    b = tm_tile(t_, lambda kc: (wv_[:, kc, :], [wt_]), 16, lhs_A)
                    hi = eh[ek % 4]
                    ek += 1
                    o = H(hi)[:, 0:512]
                    evac_copy(o, bank(b), r=[PS_t[b]], w=[H_t[hi]])
                    dma("sp", BV[t_ * 128:(t_ + 1) * 128, cg * 512:(cg + 1) * 512], o, r=[H_t[hi]], wp=[BV_t])
            fm_cols(3904, 16, GT, GT_t, 0, True)

            chk("inproj")
            attn(i)

        def attn_head(kind, i, h, par, loads_only=False, compute_only=False):
            base = par * 5
            qs, qrs, ks, vs, gs = base, base + 1, base + 2, base + 3, base + 4
            chunk = h if kind == "a" else 8 + h
            if not compute_only:
                if kind == "a":
                    dma("sp", H(qs), QN[h], r=[QN_t[h]], w=[H_t[qs]])
                    dma("sp", H(qrs)[0:64, :], QR[h], r=[QR_t[h]], w=[H_t[qrs]])
                    dma("sp", H(ks), KN[h], r=[KN_t[h]], w=[H_t[ks]])
                    dma("sp", H(vs).rearrange("p (t d) -> p t d", d=128), VA[:, h * 128:(h + 1) * 128].rearrange("(t p) d -> p t d", p=128),
                        r=[VA_t], w=[H_t[vs]])
                else:
                    dma("sp", H(qs), BQ[h], r=[BQ_t[h]], w=[H_t[qs]])
                    dma("sp", H(ks), BK[h], r=[BK_t[h]], w=[H_t[ks]])
                    dma("sp", H(vs).rearrange("p (t d) -> p t d", d=128), BV[:, h * 128:(h + 1) * 128].rearrange("(t p) d -> p t d", p=128),
                        r=[BV_t], w=[H_t[vs]])
                    dma("sp", bt4[par][:], tb4[i, h], w=[bt_t[par]])
                    dma("sp", bt3[par][:], tb3[i, h], wp=[bt_t[par]])
                dma("sp", H(gs), GT[chunk], r=[GT_t[chunk]], w=[H_t[gs]])
            if loads_only:
                return
            q_, k_, v_, g_ = H(qs), H(ks), H(vs).rearrange("p (t d) -> p t d", d=128), H(gs)
            qr_ = H(qrs)
            KR = H(10)
            PT_s = [11, 12]
            tmp_s = [13, 14]
            scale = (192.0 ** -0.5) if kind == "a" else (128.0 ** -0.5)
            cb_ = crep_t[:, i * 8 + h:i * 8 + h + 1]
            for g in range(4):
                ob = g % 2
                sb_ = 2 + g % 2
                tiles = []
                if kind == "a":
                    for kt in range(4 * g + 4):
                        c0 = 0 if kt < 4 * g else (kt - 4 * g) * 128
                        tiles.append((kt, c0, 512 - c0, "diag" if kt >= 4 * g else "plain"))
                else:
                    for u in [4, 5, 6, 7, 0, 1, 2, 3]:
                        t_ = 4 * g - 4 + u
                        if t_ < 0:
                            continue
                        if u < 4:
                            tiles.append((t_, 0, 128 * (u + 1), "u%d" % u))
                        else:
                            tiles.append((t_, 128 * (u - 4), 512 - 128 * (u - 4), "u%d" % u))
                nt = len(tiles)
                ptl = []

                def s_stage(idx):
                    kt, c0, N, kindt = tiles[idx]
                    k_i = att["pt"] % 8
                    att["pt"] += 1
                    sbk = 4 + att["s"] % 4
                    att["s"] += 1
                    pt_tok = PT_t[k_i]
                    pt = H(PT_s[k_i // 4])[:, (k_i % 4) * 512:(k_i % 4) * 512 + 512]
                    q0 = g * 512 + c0
                    if kind == "a":
                        mm(bank(sbk)[:, 0:N], k_[:, kt * 128:(kt + 1) * 128], q_[:, q0:q0 + N], True, False,
                           r=[H_t[ks], H_t[qs]], w=[PS_t[sbk]], sig=False)
                        mm(bank(sbk)[:, 0:N], KR[0:64, kt * 128:(kt + 1) * 128], qr_[0:64, q0:q0 + N], False, True,
                           r=[H_t[10], H_t[qrs]], w=[PS_t[sbk]], sig=True)
                        act(pt[:, 0:N], bank(sbk)[:, 0:N], AF.Exp, r=[PS_t[sbk]], w=[pt_tok], scale=scale)
                        if kindt == "diag":
                            mset("dve", pt[64:128, 0:64], 0.0, w=[pt_tok])
                    else:
                        mm(bank(sbk)[:, 0:N], k_[:, kt * 128:(kt + 1) * 128], q_[:, q0:q0 + N], True, True,
                           r=[H_t[ks], H_t[qs]], w=[PS_t[sbk]], sig=True)
                        u = int(kindt[1:])
                        if u >= 3:
                            nbias = min(256, N) if u >= 4 else 128
                            btile = bt4[par] if u >= 4 else bt3[par]
                            ti = att["tmp"] % 2
                            att["tmp"] += 1
                            tmp = H32(tmp_s[ti])[:, 0:nbias]
                            stt("dve", tmp, bank(sbk)[:, 0:nbias], scale, btile[:, 0:nbias], ALU.mult, ALU.add,
                                r=[PS_t[sbk], bt_t[par]], w=[H_t[tmp_s[ti]]])
                            act(pt[:, 0:nbias], tmp, AF.Exp, r=[H_t[tmp_s[ti]]], w=[pt_tok])
                            if N > nbias:
                                act(pt[:, nbias:N], bank(sbk)[:, nbias:N], AF.Exp, r=[PS_t[sbk], c_t], wp=[pt_tok], scale=scale, bias=cb_)
                        else:
                            act(pt[:, 0:N], bank(sbk)[:, 0:N], AF.Exp, r=[PS_t[sbk], c_t], w=[pt_tok], scale=scale, bias=cb_)
                        if u < 4:
                            mset("dve", pt[0:64, N - 64:N], 0.0, w=[pt_tok])
                    ptl.append((pt, pt_tok))

                def pv_stage(idx):
                    kt, c0, N, kindt = tiles[idx]
                    pt, pt_tok = ptl[idx]
                    mm(bank(ob)[:, c0:c0 + N], v_[:, kt, :], pt[:, 0:N], idx == 0, idx == nt - 1,
                       r=[H_t[vs], pt_tok], w=[PS_t[ob]], sig=False)
                    mm(bank(sb_)[:, c0:c0 + N], onesb[:], pt[:, 0:N], idx == 0, idx == nt - 1,
                       r=[c_t, pt_tok], w=[PS_t[sb_]], sig=True)

                LA = 3
                for idx in range(nt + LA):
                    if idx < nt:
                        s_stage(idx)
                    if idx >= LA:
                        pv_stage(idx - LA)
                ti = att["tmp"] % 2
                att["tmp"] += 1
                rec = H32(tmp_s[ti])[:, 0:512]
                o32 = H32(tmp_s[ti])[:, 512:1024]
                act(rec, bank(sb_), AF.Ln, r=[PS_t[sb_]], w=[H_t[tmp_s[ti]]])
                act(rec, rec, AF.Exp, r=[H_t[tmp_s[ti]]], w=[H_t[tmp_s[ti]]], scale=-1.0)
                tt("dve", o32, bank(ob), rec, ALU.mult, r=[PS_t[ob], H_t[tmp_s[ti]]], w=[H_t[tmp_s[ti]]])
                tt("pool", A3[:, chunk, g * 512:(g + 1) * 512], o32, g_[:, g * 512:(g + 1) * 512], ALU.mult,
                   r=[H_t[tmp_s[ti]], H_t[gs]], w=[A_t[chunk][g]])

        att = {"pt": 0, "s": 0, "tmp": 0}
        PT_t = [Tok("PT%d" % k) for k in range(8)]

        def tok_split(parent, children):
            for ch in children:
                ch.w = dict(parent.w)
                ch.r = dict(parent.r)
                ch.pr = dict(parent.pr)
                ch.closed = parent.closed

        def tok_join(parent, children):
            w, r, pr = {}, {}, {}
            for ch in children:
                _merge(w, ch.w)
                _merge(r, ch.r)
                _merge(r, ch.w)
                _merge(pr, ch.pr)
            parent.w, parent.r, parent.pr, parent.closed = w, r, pr, True

        def attn(i):
            tok_split(H_t[11], PT_t[0:4])
            tok_split(H_t[12], PT_t[4:8])
            attn_inner(i)
            tok_join(H_t[11], PT_t[0:4])
            tok_join(H_t[12], PT_t[4:8])

        def attn_inner(i):
            heads = [("a", h) for h in range(8)] + [("b", h) for h in range(8)]
            attn_head(heads[0][0], i, heads[0][1], 0, loads_only=True)
            for n, (kind, h) in enumerate(heads):
                if n + 1 < len(heads):
                    attn_head(heads[n + 1][0], i, heads[n + 1][1], (n + 1) % 2, loads_only=True)
                attn_head(kind, i, h, n % 2, compute_only=True)

        def odd_layer(i, mods):
            w_in = sg_w_in[i]
            eh = [12, 13, 14, 15]
            ek = 0
            for cg in range(4):
                if mods:
                    mods.pop(0)()
                wu, wut = wload(w_in[:, cg * 512:(cg + 1) * 512], 16, 512)
                wg, wgt = wload(w_in[:, 4096 + cg * 512:4096 + (cg + 1) * 512], 16, 512)
                for cb in range(4):
                    fc = cg * 4 + cb
                    hu = eh[ek % 4]
                    hg = eh[(ek + 1) % 4]
                    ek += 2
                    b0 = nfm()
                    fm_block(wu, wut, slice(cb * 128, (cb + 1) * 128), 16, rhs_A, b0)
                    for tg in range(4):
                        cp("dve", H(hu)[:, tg * 512:(tg + 1) * 512], bank(b0 + tg), r=[PS_t[b0 + tg]], wp=[H_t[hu]])
                    b1 = nfm()
                    fm_block(wg, wgt, slice(cb * 128, (cb + 1) * 128), 16, rhs_A, b1)
                    act(H(hg), PS[:, b1 * 512:(b1 + 4) * 512], AF.Silu, r=[PS_t[b1 + k] for k in range(4)], w=[H_t[hg]])
                    tt("pool", H(hu), H(hu), H(hg), ALU.mult, r=[H_t[hu], H_t[hg]], w=[H_t[hu]])
                    dma("sp", UG[fc], H(hu), r=[H_t[hu]], w=[UG_t[fc]])
            for cg in range(4):
                if mods:
                    mods.pop(0)()
                wv_, wt_ = wload(w_in[:, 2048 + cg * 512:2048 + (cg + 1) * 512], 16, 512)
                for t_ in range(16):
                    b = tm_tile(t_, lambda kc: (wv_[:, kc, :], [wt_]), 16, lhs_A)
                    hi = eh[ek % 4]
                    ek += 1
                    o = H32(hi)[:, 0:512]
                    evac_copy(o, bank(b), r=[PS_t[b]], w=[H_t[hi]])
                    dma("sp", VR[t_ * 128:(t_ + 1) * 128, cg * 512:(cg + 1) * 512], o, r=[H_t[hi]], wp=[VR_t[t_]])
            wsb = H(0)[:, 0:1024].rearrange("p (g j) -> p g j", j=128)
            wsT = H(1)[:, 0:1024].rearrange("p (g i) -> p g i", i=128)
            E = Fs(1)
            Ev = E.rearrange("p (c i) -> p c i", i=128)
            dma("pool", wsb, sg_w_s[i].rearrange("g i j -> i g j"), w=[H_t[0]])
            mset("dve", wsb[0:64, :, 64:128], 0.0, wp=[H_t[0]])
            for half in range(2):
                b = nb()
                for j in range(4):
                    g = half * 4 + j
                    mm(bank(b)[:, j * 128:(j + 1) * 128], wsb[:, g, :], identb[:], True, True, r=[H_t[0], c_t], w=[PS_t[b]], sig=(j == 3))
                evac_copy(H(1)[:, half * 512:(half + 1) * 512], bank(b), r=[PS_t[b]], wp=[H_t[1]])
            dma("sp", bsrow[:], sg_b_s[i:i + 1, :], w=[bs_t])
            rs_b = [nb(), nb()]
            bs_b = [nb(), nb()]
            for half in range(2):
                for j in range(4):
                    g = half * 4 + j
                    mm(bank(rs_b[half])[:, j * 128:(j + 1) * 128], onesb[:], wsT[:, g, :], True, True, r=[H_t[1], c_t],
                       w=[PS_t[rs_b[half]]], sig=(j == 3))
                mm(bank(bs_b[half]), onesf[0:1, :], bsrow[0:1, half * 512:(half + 1) * 512], True, True, r=[bs_t, c_t],
                   w=[PS_t[bs_b[half]]], sig=True)
            bsr = H32(4)
            for half in range(2):
                cp("dve", bsr[:, half * 512:(half + 1) * 512], bank(bs_b[half]), r=[PS_t[bs_b[half]]], wp=[H_t[4]])
            for fc in range(16):
                g = fc // 2
                stt("dve", Ev[:, fc, :], bank(rs_b[g // 4])[:, (g % 4) * 128:(g % 4 + 1) * 128], lnb_t[:, i * 16 + fc:i * 16 + fc + 1],
                    bsr[:, g * 128:(g + 1) * 128], ALU.mult, ALU.add, r=[PS_t[rs_b[g // 4]], H_t[4], c_t], wp=F_t(1))

            def loads(n):
                p = n % 2
                dma("sp", Fs(2 + p), VR[n * 128:(n + 1) * 128, :], r=[VR_t[n]], w=F_t(2 + p))
                dma("sp", H(8 + p).rearrange("p (c t) -> p c t", t=128), UG[:, :, n * 128:(n + 1) * 128].rearrange("c p t -> p c t"),
                    r=UG_t, w=[H_t[8 + p]])

            loads(0)
            for n in range(16):
                if n + 1 < 16:
                    loads(n + 1)
                p = n % 2
                vt = Fs(2 + p)
                ug = H(8 + p).rearrange("p (c t) -> p c t", t=128)
                si = rr["st"] % 8
                rr["st"] += 1
                c = si * 8
                s_ = [st_t[si]]
                act(junk[:], vt, AF.Identity, r=F_t(2 + p), w=s_, accum=st[:, c:c + 1])
                act(junk[:], vt, AF.Square, r=F_t(2 + p), wp=s_, accum=st[:, c + 1:c + 2])
                ts("dve", st[:, c + 2:c + 3], st[:, c:c + 1], 1.0 / D, None, ALU.mult, None, r=s_, w=s_)
                tt("dve", st[:, c + 3:c + 4], st[:, c + 2:c + 3], st[:, c + 2:c + 3], ALU.mult, r=s_, w=s_)
                stt("dve", st[:, c + 4:c + 5], st[:, c + 1:c + 2], 1.0 / D, st[:, c + 3:c + 4], ALU.mult, ALU.subtract, r=s_, w=s_)
                act(st[:, c + 5:c + 6], st[:, c + 4:c + 5], AF.Sqrt, r=s_, w=s_, bias=EPS, scale=1.0)
                recip(st[:, c + 6:c + 7], st[:, c + 5:c + 6], r=s_, w=s_)
                stt("dve", st[:, c + 7:c + 8], st[:, c + 2:c + 3], -1.0, st[:, c + 6:c + 7], ALU.mult, ALU.mult, r=s_, w=s_)
                vn = H(10 + p)
                act(vn, vt, AF.Identity, r=F_t(2 + p) + s_, w=[H_t[10 + p]], bias=st[:, c + 7:c + 8], scale=st[:, c + 6:c + 7])
                for q in range(4):
                    b = nb()
                    for j in range(4):
                        fc = 4 * q + j
                        mm(bank(b)[:, j * 128:(j + 1) * 128], vn[:, fc * 128:(fc + 1) * 128], wsT[:, fc // 2, :], True, True,
                           r=[H_t[10 + p], H_t[1]], w=[PS_t[b]], sig=(j == 3))
                    ti = att["tmp"] % 2
                    att["tmp"] += 1
                    tmp = H32(13 + ti)[:, 0:512]
                    for j in range(4):
                        fc = 4 * q + j
                        stt("dve", tmp[:, j * 128:(j + 1) * 128], bank(b)[:, j * 128:(j + 1) * 128], lng_t[:, i * 16 + fc:i * 16 + fc + 1],
                            Ev[:, fc, :], ALU.mult, ALU.add, r=[PS_t[b], c_t] + F_t(1), wp=[H_t[13 + ti]])
                    tt("pool", A3[:, 4 * q:4 * q + 4, n * 128:(n + 1) * 128], tmp.rearrange("p (c t) -> p c t", t=128), ug[:, 4 * q:4 * q + 4, :], ALU.mult,
                       r=[H_t[13 + ti], H_t[8 + p]], wp=[A_t[4 * q + j][n // 4] for j in range(4)])

        def whole():
            consts()
            chk("const")
            for cg in range(8):
                mod_group(0, cg)
                chk("mod%d" % cg)
            mod_flush()
            chk("mod")
            for _ in norm_gen(0, x_in, XS[0], False, True):
                pass
            chk("norm0")
            Xcur = x_in
            for l in range(depth):
                mods = []
                if l == 0:
                    mods = [(lambda cg=cg: mod_group(0, cg)) for cg in range(8, 12)]
                if l + 1 < depth:
                    mods += [(lambda l1=l + 1, cg=cg: mod_group(l1, cg)) for cg in range(12)]
                i = l // 2
                if l % 2 == 0:
                    even_layer(i, mods)
                    w_out_l = ab_w_out[i]
                else:
                    odd_layer(i, mods)
                    w_out_l = sg_w_out[i]
                chk("mix%d" % l)
                Xnext = out if l + 1 == depth else XS[(l + 1) % 2]
                ngen = norm_gen(l + 1, Xcur, Xnext, True, l + 1 < depth)
                out_phase(w_out_l, mods, ngen)
                chk("out%d" % l)
                while mods:
                    mods.pop(0)()
                mod_flush()
                for _ in ngen:
                    pass
                Xcur = Xnext

        def reset_all():
            P.reset()
            for t in Tok.ALL:
                t.reset()
            for d_ in (rr, ev, att):
                for k_ in d_:
                    d_[k_] = 0
            plan["k"] = 0
            plan["issued"] = 0

        for pass_ in range(2):
            try:
                whole()
            except _Stop:
                pass
            if pass_ == 0:
                plan["replay"] = plan["specs"]
                reset_all()
        fin = {}
        for t in X_t[id(out)]:
            _merge(fin, t.w)
        P.stream["sp"].append((fin, None, None))

        with nc.Block() as block:
            @block.tensor
            def _(e):
                P.emit("pe", e)

            @block.scalar
            def _(e):
                P.emit("act", e)

            @block.vector
            def _(e):
                P.emit("dve", e)

            @block.gpsimd
            def _(e):
                P.emit("pool", e)

            @block.sync
            def _(e):
                P.emit("sp", e)
        build.stats = {e: len(s) for e, s in P.stream.items()}
    return nc


def _host_inputs(inputs, b):
    f = np.float32
    x = np.ascontiguousarray(inputs["x"][b], dtype=f)
    c = np.asarray(inputs["c"][b], dtype=f)

    def col(v):
        return np.ascontiguousarray(np.asarray(v, dtype=f).reshape(-1, 128).T)

    def cols(m):
        return np.ascontiguousarray(np.concatenate([col(m[l]) for l in range(m.shape[0])], axis=1))

    tab = np.asarray(inputs["b_rel_bias"], dtype=f)
    kk = np.arange(64)[:, None]
    cc = np.arange(320)[None, :]
    idx = np.minimum(cc - kk + 128, 256)
    TB = tab[:, :, idx]
    tb4 = np.full((2, 8, 128, 256), NEG, dtype=f)
    tb4[:, :, 0:64, :] = TB[:, :, :, 0:256]
    tb4[:, :, 64:128, 64:256] = TB[:, :, :, 0:192]
    tb3 = np.empty((2, 8, 128, 128), dtype=f)
    tb3[:, :, 0:64, :] = TB[:, :, :, 128:256]
    tb3[:, :, 64:128, :] = TB[:, :, :, 64:192]
    crep = np.ascontiguousarray(np.broadcast_to(tab[:, :, 256].reshape(1, 16), (128, 16)), dtype=f)
    half = 32
    freqs = (10000.0 ** (-np.arange(half, dtype=f) / f(half))).astype(f)
    ang = (np.arange(S, dtype=f)[None, :] * freqs[:, None]).astype(f)
    cs = np.concatenate([np.cos(ang), np.cos(ang), np.sin(ang), np.sin(ang)], 0).astype(f)
    d = {
        "x": x, "ccol": col(c),
        "w_mod": inputs["w_mod"], "b_mod": inputs["b_mod"],
        "gpre_col": cols(np.asarray(inputs["g_pre"])), "g_post": inputs["g_post"],
        "ab_w_in": inputs["ab_w_in"], "gq_col": cols(np.asarray(inputs["a_g_q"])), "a_w_uq": inputs["a_w_uq"],
        "gkv_col": cols(np.asarray(inputs["a_g_kv"])), "a_w_ukv": inputs["a_w_ukv"],
        "tb4": tb4, "tb3": tb3, "crep": crep, "ab_w_out": inputs["ab_w_out"],
        "sg_w_in": inputs["sg_w_in"], "lng_col": cols(np.asarray(inputs["sg_ln_g"])), "lnb_col": cols(np.asarray(inputs["sg_ln_b"])),
        "sg_w_s": inputs["sg_w_s"], "sg_b_s": np.asarray(inputs["sg_b_s"], dtype=f).reshape(2, 1024), "sg_w_out": inputs["sg_w_out"],
        "ident": np.eye(128, dtype=f), "ropecs": cs,
    }
    return {k: np.ascontiguousarray(np.asarray(v, dtype=f)) for k, v in d.items()}


_NC = {}


def kernel(**inputs):
    inputs = {k: np.asarray(v) for k, v in inputs.items()}
    if DEPTH not in _NC:
        _NC[DEPTH] = build(DEPTH)
    nc = _NC[DEPTH]
    shared = None
    in_maps = []
    for b in range(8):
        m = _host_inputs(inputs, b) if shared is None else dict(shared)
        if shared is None:
            shared = m
        else:
            m["x"] = np.ascontiguousarray(inputs["x"][b], dtype=np.float32)
            m["ccol"] = np.ascontiguousarray(np.asarray(inputs["c"][b], dtype=np.float32).reshape(-1, 128).T)
        in_maps.append(m)
    res = run_bass_kernel_spmd(nc, in_maps, core_ids=list(range(8)))
    return np.stack([np.asarray(r["out"]) for r in res.results], axis=0).astype(np.float32)
```

```python
import numpy as np
from contextlib import ExitStack
import concourse.bass as bass
import concourse.mybir as mybir
from concourse.bass_utils import run_bass_kernel_spmd

F32 = mybir.dt.float32
BF16 = mybir.dt.bfloat16
AF = mybir.ActivationFunctionType
ALU = mybir.AluOpType

S = 2048
D = 2048
DEPTH = 4
EPS = 1e-6
NEG = -30000.0
CE = ("pe", "act", "dve", "pool")


class Tok:
    __slots__ = ("name", "w", "r", "closed", "pr", "psum")

    ALL = []

    def __init__(self, name, psum=False):
        self.name = name
        self.psum = psum
        self.reset()
        Tok.ALL.append(self)

    def reset(self):
        self.w = {}
        self.r = {}
        self.pr = {}
        self.closed = False


def _merge(d, s):
    for k, v in s.items():
        if d.get(k, 0) < v:
            d[k] = v


class Prog:
    NS = 8

    def __init__(self, nc, es):
        self.nc = nc
        self.sem = {e: es.enter_context(nc.semaphore("s_" + e)) for e in CE}
        self.cnt = {e: 0 for e in CE}
        self.dq = ("sp", "pool", "act")
        self.dsem = {q: [es.enter_context(nc.semaphore("d_%s%d" % (q, i))) for i in range(self.NS)] for q in self.dq}
        self.reset()

    def reset(self):
        self.cnt = {e: 0 for e in CE}
        self.dval = {q: [0] * self.NS for q in self.dq}
        self.dnext = {q: 0 for q in self.dq}
        self.stream = {e: [] for e in ("pe", "act", "dve", "pool", "sp")}
        self.pend = {e: ([], [], []) for e in CE}
        self.pendset = {e: set() for e in CE}
        self.nops = 0

    def semh(self, key):
        if isinstance(key, tuple):
            return self.dsem[key[0]][key[1]]
        return self.sem[key]

    def _publish(self, r, w, wp, key, val):
        for t in r:
            if t.r.get(key, 0) < val:
                t.r[key] = val
            t.closed = True
        for t in w:
            t.w = {key: val}
            t.r = {}
            t.pr = {key: val}
            t.closed = False
        for t in wp:
            if t.closed:
                t.pr = t.r
                t.w = {key: val}
                t.r = {}
                t.closed = False
            else:
                if t.w.get(key, 0) < val:
                    t.w[key] = val

    def op(self, eng, fn, r=(), w=(), wp=(), sig=True, dma=False):
        self.nops += 1
        waits = {}
        if eng != "pe":
            xs = [t for t in (*r, *w, *wp) if t.psum]
            if xs:
                r = [t for t in r if not t.psum]
                wp = [t for t in wp if not t.psum]
                w = [t for t in w if not t.psum] + xs
                for t in xs:
                    for d_ in (t.w, t.r, t.pr):
                        for k, v in d_.items():
                            if k != eng and waits.get(k, 0) < v:
                                waits[k] = v
                own = waits.get(eng)
            else:
                own = None
        else:
            xs = ()
            own = None
        for t in (*r, *w, *wp):
            for e2, ps in self.pendset.items():
                if e2 != eng and t in ps:
                    raise RuntimeError("token %s pending on %s touched by %s" % (t.name, e2, eng))
        for t in r:
            _merge(waits, t.w)
        for t in w:
            _merge(waits, t.w)
            _merge(waits, t.r)
        for t in wp:
            _merge(waits, t.r)
            if not t.closed:
                _merge(waits, t.pr)
        if eng == "pe":
            waits.pop("pe", None)
        elif xs:
            ownv = 0
            for t in r:
                ownv = max(ownv, t.w.get(eng, 0))
            for t in w:
                if not t.psum:
                    ownv = max(ownv, t.w.get(eng, 0), t.r.get(eng, 0))
            for t in wp:
                ownv = max(ownv, t.r.get(eng, 0))
                if not t.closed:
                    ownv = max(ownv, t.pr.get(eng, 0))
            if ownv:
                waits[eng] = ownv
            else:
                waits.pop(eng, None)
        if dma:
            q = eng
            slot = self.dnext[q] % self.NS
            self.dnext[q] += 1
            key = (q, slot)
            if self.dval[q][slot] > 0:
                waits[key] = max(waits.get(key, 0), self.dval[q][slot])
            self.dval[q][slot] += 16
            self._publish(r, w, wp, key, self.dval[q][slot])
            inc = (self.dsem[q][slot], 16)
        else:
            pr, pw, pwp = self.pend[eng]
            pr.extend(r)
            pw.extend(w)
            pwp.extend(wp)
            if sig:
                self.cnt[eng] += 1
                self._publish(pr, pw, pwp, eng, self.cnt[eng])
                self.pend[eng] = ([], [], [])
                self.pendset[eng] = set()
                inc = (self.sem[eng], 1)
            else:
                assert eng == "pe"
                self.pendset[eng].update(r)
                self.pendset[eng].update(w)
                self.pendset[eng].update(wp)
                inc = None
        self.stream[eng].append((waits, fn, inc))

    def emit(self, eng, e):
        seen = {}
        for waits, fn, inc in self.stream[eng]:
            for k, v in waits.items():
                if seen.get(k, 0) >= v:
                    continue
                seen[k] = v
                e.wait_ge(self.semh(k), v)
            if fn is not None:
                ins = fn(e)
                if inc is not None:
                    ins.then_inc(inc[0], inc[1])


class _Stop(Exception):
    pass


def build(depth=DEPTH, stop=None):
    Tok.ALL = []
    nc = bass.Bass("TRN2", target_bir_lowering=False)

    def din(name, shape, dt=F32):
        return nc.dram_tensor(name, list(shape), dt, kind="ExternalInput").ap()

    def dscr(name, shape, dt):
        return nc.dram_tensor(name, list(shape), dt).ap()

    x_in = din("x", [S, D])
    ccol = din("ccol", [128, 16])
    w_mod = din("w_mod", [4, D, 6144])
    b_mod = din("b_mod", [4, 6144])
    gpre_col = din("gpre_col", [128, 64])
    g_post = din("g_post", [4, D])
    ab_w_in = din("ab_w_in", [2, D, 5952])
    gq_col = din("gq_col", [128, 8])
    a_w_uq = din("a_w_uq", [2, 512, 1536])
    gkv_col = din("gkv_col", [128, 4])
    a_w_ukv = din("a_w_ukv", [2, 256, 2048])
    tb4 = din("tb4", [2, 8, 128, 256])
    tb3 = din("tb3", [2, 8, 128, 128])
    crep = din("crep", [128, 16])
    ab_w_out = din("ab_w_out", [2, D, D])
    sg_w_in = din("sg_w_in", [2, D, 6144])
    lng_col = din("lng_col", [128, 32])
    lnb_col = din("lnb_col", [128, 32])
    sg_w_s = din("sg_w_s", [2, 8, 128, 128])
    sg_b_s = din("sg_b_s", [2, 1024])
    sg_w_out = din("sg_w_out", [2, D, D])
    ident_d = din("ident", [128, 128])
    ropecs = din("ropecs", [128, S])
    out = nc.dram_tensor("out", [S, D], F32, kind="ExternalOutput").ap()

    XS = [dscr("XA", [S, D], F32), dscr("XB", [S, D], F32)]
    Y2 = dscr("Y2", [S, D], F32)
    G2ROW = dscr("G2ROW", [4, D], F32)
    QN = dscr("QN", [8, 128, S], BF16)
    QR = dscr("QR", [8, 64, S], BF16)
    KN = dscr("KN", [8, 128, S], BF16)
    VA = dscr("VA", [S, 1024], BF16)
    BQ = dscr("BQ", [8, 128, S], BF16)
    BK = dscr("BK", [8, 128, S], BF16)
    BV = dscr("BV", [S, 1024], BF16)
    GT = dscr("GT", [16, 128, S], BF16)
    UG = dscr("UG", [16, 128, S], BF16)
    VR = dscr("VR", [S, D], F32)

    es = ExitStack()
    with es:
        def sb(name, shape, dt):
            return es.enter_context(nc.sbuf_tensor(name, list(shape), dt))

        A = sb("A", [128, 16 * 2048], BF16)
        A3 = A[:].rearrange("p (k t) -> p k t", t=2048)
        WB = [sb("WB%d" % i, [128, 8192], BF16) for i in range(3)]
        SL = [sb("SL%d" % i, [128, 4096], BF16) for i in range(8)]
        PS = es.enter_context(nc.psum_tensor("PS", [128, 4096], F32))
        identb = sb("identb", [128, 128], BF16)
        onesb = sb("onesb", [128, 128], BF16)
        onesf = sb("onesf", [128, 128], F32)
        csT = sb("csT", [128, 16], BF16)
        ccol_t = sb("ccol_t", [128, 16], F32)
        gpre_t = sb("gpre_t", [128, 64], F32)
        gq_t = sb("gq_t", [128, 8], F32)
        gkv_t = sb("gkv_t", [128, 4], F32)
        crep_t = sb("crep_t", [128, 16], F32)
        lng_t = sb("lng_t", [128, 32], F32)
        lnb_t = sb("lnb_t", [128, 32], F32)
        Acol = sb("Acol", [128, 64], F32)
        Bcol = sb("Bcol", [128, 64], F32)
        st = sb("st", [128, 64], F32)
        rows = [sb("row%d" % i, [1, 512], F32) for i in range(2)]
        foldb = sb("foldb", [128, 64], BF16)
        brow = [sb("brow%d" % i, [1, 512], F32) for i in range(2)]
        grow = [sb("grow%d" % i, [1, 512], F32) for i in range(2)]
        identf = sb("identf", [128, 128], F32)
        diag = [sb("diag%d" % i, [128, 128], F32) for i in range(2)]
        junk = sb("junk", [128, 2048], BF16)
        bt4 = [sb("bt4_%d" % i, [128, 256], F32) for i in range(2)]
        bt3 = [sb("bt3_%d" % i, [128, 128], F32) for i in range(2)]
        bsrow = sb("bsrow", [1, 1024], F32)

        P = Prog(nc, es)

        A_t = [[Tok("A%d_%d" % (k, g)) for g in range(4)] for k in range(16)]
        WB_t = [Tok("WB%d" % i) for i in range(3)]
        H_t = [Tok("H%d" % i) for i in range(16)]
        PS_t = [Tok("PS%d" % i, psum=True) for i in range(8)]
        c_t = Tok("consts")
        st_t = [Tok("st%d" % i) for i in range(8)]
        row_t = [Tok("row%d" % i) for i in range(2)]
        brow_t = [Tok("brow%d" % i) for i in range(2)]
        grow_t = [Tok("grow%d" % i) for i in range(2)]
        diag_t = [Tok("diag%d" % i) for i in range(2)]
        bt_t = [Tok("bt%d" % i) for i in range(2)]
        AB_t = [Tok("AB%d" % l) for l in range(4)]
        X_t = {id(XS[0]): [Tok("XA%d" % i) for i in range(16)], id(XS[1]): [Tok("XB%d" % i) for i in range(16)],
               id(out): [Tok("out%d" % i) for i in range(16)], id(x_in): [Tok("xin%d" % i) for i in range(16)]}
        Y2_t = [Tok("Y2_%d" % i) for i in range(16)]
        G2_t = [Tok("G2_%d" % i) for i in range(4)]
        QN_t = [Tok("QN%d" % i) for i in range(8)]
        QR_t = [Tok("QR%d" % i) for i in range(8)]
        KN_t = [Tok("KN%d" % i) for i in range(8)]
        VA_t = Tok("VA")
        BQ_t = [Tok("BQ%d" % i) for i in range(8)]
        BK_t = [Tok("BK%d" % i) for i in range(8)]
        BV_t = Tok("BV")
        GT_t = [Tok("GT%d" % i) for i in range(16)]
        UG_t = [Tok("UG%d" % i) for i in range(16)]
        VR_t = [Tok("VR%d" % i) for i in range(16)]
        bs_t = Tok("bsrow")

        def H(i):
            return SL[i // 2][:, (i % 2) * 2048:(i % 2) * 2048 + 2048]

        def H32(i):
            return H(i).bitcast(F32)

        def Fs(j):
            return SL[j][:].bitcast(F32)

        def F_t(j):
            return [H_t[2 * j], H_t[2 * j + 1]]

        def bank(b):
            return PS[:, b * 512:(b + 1) * 512]

        def mm(o, lhsT, rhs, start, stop, r, w, sig):
            P.op("pe", lambda e: e.matmul(o, lhsT, rhs, start=start, stop=stop), r=r, w=w, sig=sig)

        def act(o, i, func, r, w=(), wp=(), bias=None, scale=None, accum=None):
            kw = {}
            if bias is not None:
                kw["bias"] = bias
            if scale is not None:
                kw["scale"] = scale
            if accum is not None:
                kw["accum_out"] = accum
            P.op("act", lambda e: e.activation(out=o, in_=i, func=func, **kw), r=r, w=w, wp=wp)

        def ts(eng, o, i, s1, s2, op0, op1, r, w=(), wp=()):
            if s2 is None:
                P.op(eng, lambda e: e.tensor_scalar(o, i, s1, None, op0), r=r, w=w, wp=wp)
            else:
                P.op(eng, lambda e: e.tensor_scalar(o, i, s1, s2, op0, op1), r=r, w=w, wp=wp)

        def stt(eng, o, i0, sc, i1, op0, op1, r, w=(), wp=()):
            P.op(eng, lambda e: e.scalar_tensor_tensor(o, i0, sc, i1, op0, op1), r=r, w=w, wp=wp)

        def tt(eng, o, i0, i1, op, r, w=(), wp=()):
            P.op(eng, lambda e: e.tensor_tensor(o, i0, i1, op), r=r, w=w, wp=wp)

        def cp(eng, o, i, r, w=(), wp=()):
            P.op(eng, lambda e: e.tensor_copy(o, i), r=r, w=w, wp=wp)

        def recip(o, i, r, w=(), wp=()):
            P.op("dve", lambda e: e.reciprocal(o, i), r=r, w=w, wp=wp)

        def mset(eng, o, val, w=(), wp=()):
            P.op(eng, lambda e: e.memset(o, val), w=w, wp=wp)

        def dma(q, o, i, r=(), w=(), wp=(), slow=False):
            if slow:
                P.op(q, lambda e: e.dma_start(out=o, in_=i, allow_slow_non_contiguous=True), r=r, w=w, wp=wp, dma=True)
            else:
                P.op(q, lambda e: e.dma_start(out=o, in_=i), r=r, w=w, wp=wp, dma=True)

        def chk(name):
            if stop == name:
                raise _Stop()

        rr = {"bank": 0, "fm": 0, "w": 0, "row": 0, "st": 0}

        def nb():
            b = rr["bank"] % 8
            rr["bank"] += 1
            return b

        def nfm():
            b = 4 * (rr["fm"] % 2)
            rr["fm"] += 1
            return b

        plan = {"specs": [], "replay": None, "k": 0, "issued": 0}

        def wload_generic(issue_fn):
            k = plan["k"]
            plan["k"] += 1
            if plan["replay"] is None:
                plan["specs"].append(issue_fn)
                issue_fn(k % 3)
            else:
                specs = plan["replay"]
                while plan["issued"] <= min(k + 1, len(specs) - 1):
                    j = plan["issued"]
                    tb_ = WB_t[j % 3]
                    assert tb_.closed or not tb_.w, "weight buffer %d reloaded before its consumers were recorded (load %d)" % (j % 3, j)
                    specs[j](j % 3)
                    plan["issued"] += 1
            return k % 3

        def wload(src, nkc, ncols, bufcols=None, col_off=0):
            bufcols = bufcols or ncols

            def view(i):
                return WB[i][:, 0:nkc * bufcols].rearrange("p (k c) -> p k c", c=bufcols)

            def issue(i):
                dma("pool", view(i)[:, :, col_off:col_off + ncols], src.rearrange("(k p) c -> p k c", p=128), w=[WB_t[i]])

            i = wload_generic(issue)
            return view(i), WB_t[i]

        def consts():
            dma("pool", identb[:], ident_d[:, :], w=[c_t])
            dma("sp", identf[:], ident_d[:, :], wp=[c_t])
            mset("dve", onesb[:], 1.0, wp=[c_t])
            mset("dve", onesf[:], 1.0, wp=[c_t])
            dma("sp", ccol_t[:], ccol[:, :], wp=[c_t])
            dma("sp", gpre_t[:], gpre_col[:, :], wp=[c_t])
            dma("sp", gq_t[:], gq_col[:, :], wp=[c_t])
            dma("sp", gkv_t[:], gkv_col[:, :], wp=[c_t])
            dma("sp", crep_t[:], crep[:, :], wp=[c_t])
            dma("sp", lng_t[:], lng_col[:, :], wp=[c_t])
            dma("sp", lnb_t[:], lnb_col[:, :], wp=[c_t])
            act(csT[:], ccol_t[:], AF.Silu, r=[c_t], wp=[c_t])
            tt("dve", foldb[:], identb[:, 0:64], identb[:, 64:128], ALU.add, r=[c_t], wp=[c_t])

        modq = []

        def mod_rows_prefetch(l, cg):
            bi = cg % 2
            dma("sp", brow[bi][:], b_mod[l:l + 1, cg * 512:(cg + 1) * 512], w=[brow_t[bi]])
            if cg >= 8:
                g0 = (cg - 8) * 512
                dma("sp", grow[bi][:], g_post[l:l + 1, g0:g0 + 512], w=[grow_t[bi]])

        def mod_flush():
            while modq:
                modq.pop(0)()

        def mod_group(l, cg):
            wv, wt = wload(w_mod[l, :, cg * 512:(cg + 1) * 512], 16, 512)
            mod_flush()
            if cg == 0:
                mod_rows_prefetch(l, 0)
            if cg + 1 < 12:
                mod_rows_prefetch(l, cg + 1)
            b = nb()
            for kc in range(16):
                mm(bank(b)[0:1, :], csT[:, kc:kc + 1], wv[:, kc, :], kc == 0, kc == 15, r=[wt, c_t], w=[PS_t[b]], sig=(kc == 15))
            bi = cg % 2
            ri = cg % 2
            tt("dve", rows[ri][:], bank(b)[0:1, :], brow[bi][:], ALU.add, r=[PS_t[b], brow_t[bi]], w=[row_t[ri]])
            if cg < 8:
                def fin():
                    b2 = nb()
                    for j in range(4):
                        mm(bank(b2)[:, j:j + 1], rows[ri][0:1, j * 128:(j + 1) * 128], onesf[0:1, 0:1], True, True,
                           r=[row_t[ri], c_t], w=[PS_t[b2]], sig=(j == 3))
                    if cg < 4:
                        c0 = l * 16 + cg * 4
                        cp("dve", Bcol[:, c0:c0 + 4], bank(b2)[:, 0:4], r=[PS_t[b2]], wp=[AB_t[l]])
                    else:
                        c0 = l * 16 + (cg - 4) * 4
                        stt("dve", Acol[:, c0:c0 + 4], bank(b2)[:, 0:4], 1.0, gpre_t[:, c0:c0 + 4], ALU.add, ALU.mult,
                            r=[PS_t[b2], c_t], wp=[AB_t[l]])
                modq.append(fin)
            else:
                g0 = (cg - 8) * 512
                tt("dve", rows[ri][:], rows[ri][:], grow[bi][:], ALU.mult, r=[row_t[ri], grow_t[bi]], w=[row_t[ri]])
                dma("sp", G2ROW[l:l + 1, g0:g0 + 512], rows[ri][:], r=[row_t[ri]], wp=[G2_t[l]])

        def norm_gen(l, Xprev, Xnext, has_y2, make_h):
            G2rep = Fs(5)
            if has_y2:
                dma("sp", G2rep, G2ROW[l - 1:l, :].broadcast_to([128, D]), r=[G2_t[l - 1]], w=F_t(5))

            def ysl(t_):
                return t_ % 2

            def xsl(t_):
                return 2 + t_ % 3

            def loads(t_):
                if has_y2:
                    dma("sp", Fs(ysl(t_)), Y2[t_ * 128:(t_ + 1) * 128, :], r=[Y2_t[t_]], w=F_t(ysl(t_)))
                dma("sp", Fs(xsl(t_)), Xprev[t_ * 128:(t_ + 1) * 128, :], r=[X_t[id(Xprev)][t_]], w=F_t(xsl(t_)))

            cols = {}

            def stage_a(t_):
                y2 = Fs(ysl(t_))
                xt = Fs(xsl(t_))
                yt_, xt_ = F_t(ysl(t_)), F_t(xsl(t_))
                si = rr["st"] % 8
                rr["st"] += 1
                c = si * 8
                cols[t_] = (si, c)
                stt_ = [st_t[si]]
                if has_y2:
                    act(junk[:], y2, AF.Square, r=yt_, w=stt_, accum=st[:, c:c + 1])
                    act(st[:, c + 1:c + 2], st[:, c:c + 1], AF.Sqrt, r=stt_, w=stt_, bias=EPS, scale=1.0 / D)
                    recip(st[:, c + 2:c + 3], st[:, c + 1:c + 2], r=stt_, w=stt_)
                    tt("dve", y2, y2, G2rep, ALU.mult, r=yt_ + F_t(5), w=yt_)
                    stt("dve", xt, y2, st[:, c + 2:c + 3], xt, ALU.mult, ALU.add, r=yt_ + xt_ + stt_, w=xt_)
                    dma("sp", Xnext[t_ * 128:(t_ + 1) * 128, :], xt, r=xt_, w=[X_t[id(Xnext)][t_]])

            def stage_b(t_):
                xt = Fs(xsl(t_))
                xt_ = F_t(xsl(t_))
                si, c = cols[t_]
                stt_ = [st_t[si]]
                act(junk[:], xt, AF.Square, r=xt_, w=stt_, accum=st[:, c + 3:c + 4])
                act(st[:, c + 4:c + 5], st[:, c + 3:c + 4], AF.Sqrt, r=stt_, w=stt_, bias=EPS, scale=1.0 / D)
                recip(st[:, c + 5:c + 6], st[:, c + 4:c + 5], r=stt_, w=stt_)
                dg = diag[t_ % 2]
                ts("dve", dg[:], identf[:], st[:, c + 5:c + 6], None, ALU.mult, None, r=stt_ + [c_t], w=[diag_t[t_ % 2]])

            def stage_b2(t_):
                xt = Fs(xsl(t_))
                xt_ = F_t(xsl(t_))
                dg = diag[t_ % 2]
                for q in range(4):
                    b = nb()
                    for j in range(4):
                        fc = 4 * q + j
                        mm(bank(b)[:, j * 128:(j + 1) * 128], xt[:, fc * 128:(fc + 1) * 128], dg[:], True, True,
                           r=xt_ + [diag_t[t_ % 2]], w=[PS_t[b]], sig=(j == 3))
                    for j in range(4):
                        fc = 4 * q + j
                        o = A3[:, fc, t_ * 128:(t_ + 1) * 128]
                        i_ = bank(b)[:, j * 128:(j + 1) * 128]
                        ac = Acol[:, l * 16 + fc:l * 16 + fc + 1]
                        bc = Bcol[:, l * 16 + fc:l * 16 + fc + 1]
                        if q % 2 == 0:
                            act(o, i_, AF.Identity, r=[PS_t[b], AB_t[l]], wp=[A_t[fc][t_ // 4]], bias=bc, scale=ac)
                        else:
                            ts("dve", o, i_, ac, bc, ALU.mult, ALU.add, r=[PS_t[b], AB_t[l]], wp=[A_t[fc][t_ // 4]])

            loads(0)
            loads(1)
            stage_a(0)
            for t_ in range(16):
                if make_h and t_ >= 1:
                    stage_b2(t_ - 1)
                if t_ + 2 < 16:
                    loads(t_ + 2)
                if t_ + 1 < 16:
                    stage_a(t_ + 1)
                if make_h:
                    stage_b(t_)
                yield t_
            if make_h:
                stage_b2(15)

        def fm_block(wv, wt, cols, nkc, rhs_fn, b0, M=128, banks=None):
            for kc in range(nkc):
                for tg in range(4):
                    rhs, rt = rhs_fn(kc, tg)
                    b = b0 + tg
                    mm(bank(b)[0:M, :], wv[:, kc, cols], rhs, kc == 0, kc == nkc - 1, r=[wt] + rt, w=[PS_t[b]], sig=(kc == nkc - 1))

        def rhs_A(kc, tg):
            return A3[:, kc, tg * 512:(tg + 1) * 512], [A_t[kc][tg]]

        def tm_tile(t_, rhs_fn, nkc, lhs_fn):
            b = nb()
            for kc in range(nkc):
                lhsT, lt = lhs_fn(kc, t_)
                rhs, rt = rhs_fn(kc)
                mm(bank(b), lhsT, rhs, kc == 0, kc == nkc - 1, r=lt + rt, w=[PS_t[b]], sig=(kc == nkc - 1))
            return b

        def lhs_A(kc, t_):
            return A3[:, kc, t_ * 128:(t_ + 1) * 128], [A_t[kc][t_ // 4]]

        ev = {"i": 0}

        def evac_copy(o, i_, r, w=(), wp=()):
            ev["i"] += 1
            if ev["i"] % 2 == 0:
                act(o, i_, AF.Copy, r=r, w=w, wp=wp)
            else:
                cp("dve", o, i_, r=r, w=w, wp=wp)

        def fm_to_dram(wv, wt, cols, dst, dst_t, hslot, silu=False):
            b0 = nfm()
            fm_block(wv, wt, cols, 16, rhs_A, b0)
            o = H(hslot)
            src = PS[:, b0 * 512:(b0 + 4) * 512]
            rt = [PS_t[b0 + i] for i in range(4)]
            if silu:
                act(o, src, AF.Silu, r=rt, w=[H_t[hslot]])
            else:
                for tg in range(4):
                    evac_copy(o[:, tg * 512:(tg + 1) * 512], bank(b0 + tg), r=[PS_t[b0 + tg]], wp=[H_t[hslot]])
            dma("sp", dst, o, r=[H_t[hslot]], w=[dst_t])

        def out_phase(w_out_l, mods, ngen):
            hs = [12, 13, 14, 15]
            k = 0
            adv = 0
            for half in range(2):
                if half == 1:
                    while len(mods) > 4:
                        mods.pop(0)()
                    mod_flush()
                for cg in range(4):
                    if mods:
                        mods.pop(0)()
                    wv, wt = wload(w_out_l[:, cg * 512:(cg + 1) * 512], 16, 512)
                    for t_ in range(8 * half, 8 * half + 8):
                        b = tm_tile(t_, lambda kc: (wv[:, kc, :], [wt]), 16, lhs_A)
                        hi = hs[k % 4]
                        k += 1
                        o = H32(hi)[:, 0:512]
                        evac_copy(o, bank(b), r=[PS_t[b]], w=[H_t[hi]])
                        dma("sp", Y2[t_ * 128:(t_ + 1) * 128, cg * 512:(cg + 1) * 512], o, r=[H_t[hi]], wp=[Y2_t[t_]])
                        if half == 1 and t_ % 4 == 3 and adv < 6:
                            next(ngen)
                            adv += 1

        def even_layer(i, mods):
            w_in = ab_w_in[i]
            CS = Fs(3)
            dma("sp", CS, ropecs[:, :], w=F_t(3))
            cqn = [H(0), H(1), H(2), H(3)]
            ckvn = [H(4), H(5)]
            KRs = 10
            KR = H(KRs)
            sq_h = [H32(8), H32(9)]
            rst_h = H32(11)
            Tt = H(11)[:, 1024:1536]

            def rope_fold(src_bank, tg, dst, dst_tok):
                tt("dve", Tt, bank(src_bank), CS[:, tg * 512:(tg + 1) * 512], ALU.mult, r=[PS_t[src_bank]] + F_t(3), w=[H_t[11]])
                bf = nb()
                mm(bank(bf)[0:64, :], foldb[:], Tt, True, True, r=[H_t[11], c_t], w=[PS_t[bf]], sig=True)
                evac_copy(dst[0:64, tg * 512:(tg + 1) * 512], bank(bf)[0:64, :], r=[PS_t[bf]], wp=[dst_tok])

            def lowrank_group(wv, wt, ncb, g_t, gcol0, outs, with_kr):
                for tg in range(4):
                    for cb in range(ncb):
                        for kc in range(16):
                            mm(bank(cb), wv[:, kc, cb * 128:(cb + 1) * 128], A3[:, kc, tg * 512:(tg + 1) * 512],
                               kc == 0, kc == 15, r=[wt, A_t[kc][tg]], w=[PS_t[cb]], sig=(kc == 15))
                    if with_kr:
                        for kc in range(16):
                            mm(bank(2), wv[:, kc, 256:384], A3[:, kc, tg * 512:(tg + 1) * 512],
                               kc == 0, kc == 15, r=[wt, A_t[kc][tg]], w=[PS_t[2]], sig=(kc == 15))
                    sbk = 4 + (tg % 2)
                    for cb in range(ncb):
                        sq = sq_h[cb % 2][:, (cb // 2 % 2) * 512:(cb // 2 % 2) * 512 + 512]
                        sq_tok = H_t[8 + cb % 2]
                        act(sq, bank(cb), AF.Square, r=[PS_t[cb]], w=[sq_tok])
                        mm(bank(sbk), onesf[:], sq, cb == 0, cb == ncb - 1, r=[sq_tok, c_t], w=[PS_t[sbk]], sig=True)
                    rst = rst_h[:, 0:512]
                    act(rst, bank(sbk), AF.Sqrt, r=[PS_t[sbk]], w=[H_t[11]], bias=EPS, scale=1.0 / (ncb * 128))
                    recip(rst, rst, r=[H_t[11]], w=[H_t[11]])
                    for cb in range(ncb):
                        o, ot = outs[cb]
                        stt("dve", o[:, tg * 512:(tg + 1) * 512], bank(cb), g_t[:, gcol0 + cb:gcol0 + cb + 1], rst, ALU.mult, ALU.mult,
                            r=[PS_t[cb], H_t[11], c_t], wp=[ot])
                    if with_kr:
                        rope_fold(2, tg, KR, H_t[KRs])

            wv, wt = wload(w_in[:, 0:512], 16, 512)
            wv1, wt1 = wload(w_in[:, 512:832], 16, 320, bufcols=384)
            lowrank_group(wv, wt, 4, gq_t, i * 4, [(cqn[k], H_t[k]) for k in range(4)], False)
            chk("g0")
            ts("dve", wv1[:, :, 320:352], wv1[:, :, 288:320], -1.0, None, ALU.mult, None, r=[wt1], wp=[wt1])
            cp("dve", wv1[:, :, 352:384], wv1[:, :, 256:288], r=[wt1], wp=[wt1])
            lowrank_group(wv1, wt1, 2, gkv_t, i * 2, [(ckvn[k], H_t[4 + k]) for k in range(2)], True)

            chk("g1")
            if mods:
                mods.pop(0)()
            wkv, wkt = wload(a_w_ukv[i], 2, 2048)
            wkv4 = wkv.rearrange("p k (h c) -> p k h c", c=256)
            def wq_view(i_):
                return WB[i_][:, 0:8192].rearrange("p (k h c) -> p k h c", k=4, c=256)

            def wq_issue(i_):
                for kc in range(4):
                    dma("pool", wq_view(i_)[:, kc, :, 0:192], a_w_uq[i, kc * 128:(kc + 1) * 128, :].rearrange("p (h c) -> p h c", c=192),
                        w=[WB_t[i_]] if kc == 0 else [], wp=[] if kc == 0 else [WB_t[i_]])

            iq = wload_generic(wq_issue)
            wqt = WB_t[iq]
            wq4 = wq_view(iq)
            ts("dve", wq4[:, :, :, 192:224], wq4[:, :, :, 160:192], -1.0, None, ALU.mult, None, r=[wqt], wp=[wqt])
            cp("dve", wq4[:, :, :, 224:256], wq4[:, :, :, 128:160], r=[wqt], wp=[wqt])

            def rhs_ckv(kc, tg):
                return ckvn[kc][:, tg * 512:(tg + 1) * 512], [H_t[4 + kc]]

            eh = [12, 13, 14, 15]
            ek = 0
            for h in range(8):
                b0 = nfm()
                fm_block(wkv, wkt, slice(h * 256, h * 256 + 128), 2, rhs_ckv, b0)
                hi = eh[ek % 4]
                ek += 1
                for tg in range(4):
                    evac_copy(H(hi)[:, tg * 512:(tg + 1) * 512], bank(b0 + tg), r=[PS_t[b0 + tg]], wp=[H_t[hi]])
                dma("sp", KN[h], H(hi), r=[H_t[hi]], w=[KN_t[h]])
            for t_ in range(16):
                for half in range(2):
                    b = tm_tile(t_, lambda kc: (wkv4[:, kc, 4 * half:4 * half + 4, 128:256], [wkt]), 2,
                                lambda kc, t2: (ckvn[kc][:, t2 * 128:(t2 + 1) * 128], [H_t[4 + kc]]))
                    hi = eh[ek % 4]
                    ek += 1
                    o = H(hi)[:, 0:512]
                    evac_copy(o, bank(b), r=[PS_t[b]], w=[H_t[hi]])
                    dma("sp", VA[t_ * 128:(t_ + 1) * 128, half * 512:(half + 1) * 512], o, r=[H_t[hi]], wp=[VA_t])
            chk("kv")
            for h in range(8):
                b0 = nfm()
                for kc in range(4):
                    for tg in range(4):
                        mm(bank(b0 + tg), wq4[:, kc, h, 0:128], cqn[kc][:, tg * 512:(tg + 1) * 512], kc == 0, kc == 3,
                           r=[wqt, H_t[kc]], w=[PS_t[b0 + tg]], sig=(kc == 3))
                hi = eh[ek % 4]
                ek += 1
                for tg in range(4):
                    evac_copy(H(hi)[:, tg * 512:(tg + 1) * 512], bank(b0 + tg), r=[PS_t[b0 + tg]], wp=[H_t[hi]])
                dma("sp", QN[h], H(hi), r=[H_t[hi]], w=[QN_t[h]])
                hi = eh[ek % 4]
                ek += 1
                for tg in range(4):
                    br = nb()
                    for kc in range(4):
                        mm(bank(br), wq4[:, kc, h, 128:256], cqn[kc][:, tg * 512:(tg + 1) * 512],
                           kc == 0, kc == 3, r=[wqt, H_t[kc]], w=[PS_t[br]], sig=(kc == 3))
                    rope_fold(br, tg, H(hi), H_t[hi])
                dma("sp", QR[h], H(hi)[0:64, :], r=[H_t[hi]], w=[QR_t[h]])

            chk("q")
            def fm_cols(c0, nblk, dst, dst_t, blk0, silu):
                nonlocal ek
                done = 0
                while done < nblk:
                    n = min(4, nblk - done)
                    if mods:
                        mods.pop(0)()
                    wv_, wt_ = wload(w_in[:, c0 + done * 128:c0 + (done + n) * 128], 16, n * 128)
                    for cb in range(n):
                        hi = eh[ek % 4]
                        ek += 1
                        blk = blk0 + done + cb
                        fm_to_dram(wv_, wt_, slice(cb * 128, (cb + 1) * 128), dst[blk], dst_t[blk], hi, silu=silu)
                    done += n

            fm_cols(832, 8, BQ, BQ_t, 0, False)
            fm_cols(1856, 8, BK, BK_t, 0, False)
            for cg in range(2):
                if mods:
                    mods.pop(0)()
                wv_, wt_ = wload(w_in[:, 2880 + cg * 512:2880 + (cg + 1) * 512], 16, 512)
                for t_ in range(16):
                    b = tm_tile(t_, lambda kc: (wv_[:, kc, :], [wt_]), 16, lhs_A)
                    hi = eh[ek % 4]
                    ek += 1
                    o = H(hi)[:, 0:512]
                    evac_copy(o, bank(b), r=[PS_t[b]], w=[H_t[hi]])
                    dma("sp", BV[t_ * 128:(t_ + 1) * 128, cg * 512:(cg + 1) * 512], o, r=[H_t[hi]], wp=[BV_t])
            fm_cols(3904, 16, GT, GT_t, 0, True)

            chk("inproj")
            attn(i)

        def attn_head(kind, i, h, par, loads_only=False, compute_only=False):
            base = par * 5
            qs, qrs, ks, vs, gs = base, base + 1, base + 2, base + 3, base + 4
            chunk = h if kind == "a" else 8 + h
            if not compute_only:
                if kind == "a":
                    dma("sp", H(qs), QN[h], r=[QN_t[h]], w=[H_t[qs]])
                    dma("sp", H(qrs)[0:64, :], QR[h], r=[QR_t[h]], w=[H_t[qrs]])
                    dma("sp", H(ks), KN[h], r=[KN_t[h]], w=[H_t[ks]])
                    dma("sp", H(vs).rearrange("p (t d) -> p t d", d=128), VA[:, h * 128:(h + 1) * 128].rearrange("(t p) d -> p t d", p=128),
                        r=[VA_t], w=[H_t[vs]])
                else:
                    dma("sp", H(qs), BQ[h], r=[BQ_t[h]], w=[H_t[qs]])
                    dma("sp", H(ks), BK[h], r=[BK_t[h]], w=[H_t[ks]])
                    dma("sp", H(vs).rearrange("p (t d) -> p t d", d=128), BV[:, h * 128:(h + 1) * 128].rearrange("(t p) d -> p t d", p=128),
                        r=[BV_t], w=[H_t[vs]])
                    dma("sp", bt4[par][:], tb4[i, h], w=[bt_t[par]])
                    dma("sp", bt3[par][:], tb3[i, h], wp=[bt_t[par]])
                dma("sp", H(gs), GT[chunk], r=[GT_t[chunk]], w=[H_t[gs]])
            if loads_only:
                return
            q_, k_, v_, g_ = H(qs), H(ks), H(vs).rearrange("p (t d) -> p t d", d=128), H(gs)
            qr_ = H(qrs)
            KR = H(10)
            PT_s = [11, 12]
            tmp_s = [13, 14]
            scale = (192.0 ** -0.5) if kind == "a" else (128.0 ** -0.5)
            cb_ = crep_t[:, i * 8 + h:i * 8 + h + 1]
            def group_tiles(g):
                tiles = []
                if kind == "a":
                    for kt in range(4 * g + 4):
                        c0 = 0 if kt < 4 * g else (kt - 4 * g) * 128
                        tiles.append((kt, c0, 512 - c0, "diag" if kt >= 4 * g else "plain"))
                else:
                    for u in [4, 5, 6, 7, 0, 1, 2, 3]:
                        t_ = 4 * g - 4 + u
                        if t_ < 0:
                            continue
                        if u < 4:
                            tiles.append((t_, 0, 128 * (u + 1), "u%d" % u))
                        else:
                            tiles.append((t_, 128 * (u - 4), 512 - 128 * (u - 4), "u%d" % u))
                return tiles

            jobs = []
            for g in range(4):
                tl = group_tiles(g)
                for idx, tile in enumerate(tl):
                    jobs.append((g, idx, len(tl), tile))
            ptl = {}

            def s_stage(n):
                g, idx, nt, (kt, c0, N, kindt) = jobs[n]
                k_i = att["pt"] % 8
                att["pt"] += 1
                sbk = 4 + att["s"] % 4
                att["s"] += 1
                pt_tok = PT_t[k_i]
                pt = H(PT_s[k_i // 4])[:, (k_i % 4) * 512:(k_i % 4) * 512 + 512]
                q0 = g * 512 + c0
                if kind == "a":
                    mm(bank(sbk)[:, 0:N], k_[:, kt * 128:(kt + 1) * 128], q_[:, q0:q0 + N], True, False,
                       r=[H_t[ks], H_t[qs]], w=[PS_t[sbk]], sig=False)
                    mm(bank(sbk)[:, 0:N], KR[0:64, kt * 128:(kt + 1) * 128], qr_[0:64, q0:q0 + N], False, True,
                       r=[H_t[10], H_t[qrs]], w=[PS_t[sbk]], sig=True)
                    act(pt[:, 0:N], bank(sbk)[:, 0:N], AF.Exp, r=[PS_t[sbk]], w=[pt_tok], scale=scale)
                    if kindt == "diag":
                        mset("dve", pt[64:128, 0:64], 0.0, w=[pt_tok])
                else:
                    mm(bank(sbk)[:, 0:N], k_[:, kt * 128:(kt + 1) * 128], q_[:, q0:q0 + N], True, True,
                       r=[H_t[ks], H_t[qs]], w=[PS_t[sbk]], sig=True)
                    u = int(kindt[1:])
                    if u >= 3:
                        nbias = min(256, N) if u >= 4 else 128
                        btile = bt4[par] if u >= 4 else bt3[par]
                        ti = att["tmp"] % 2
                        att["tmp"] += 1
                        tmp = H32(tmp_s[ti])[:, 0:nbias]
                        stt("dve", tmp, bank(sbk)[:, 0:nbias], scale, btile[:, 0:nbias], ALU.mult, ALU.add,
                            r=[PS_t[sbk], bt_t[par]], w=[H_t[tmp_s[ti]]])
                        act(pt[:, 0:nbias], tmp, AF.Exp, r=[H_t[tmp_s[ti]]], w=[pt_tok])
                        if N > nbias:
                            act(pt[:, nbias:N], bank(sbk)[:, nbias:N], AF.Exp, r=[PS_t[sbk], c_t], wp=[pt_tok], scale=scale, bias=cb_)
                    else:
                        act(pt[:, 0:N], bank(sbk)[:, 0:N], AF.Exp, r=[PS_t[sbk], c_t], w=[pt_tok], scale=scale, bias=cb_)
                    if u < 4:
                        mset("dve", pt[0:64, N - 64:N], 0.0, w=[pt_tok])
                ptl[n] = (pt, pt_tok)

            def pv_stage(n):
                g, idx, nt, (kt, c0, N, kindt) = jobs[n]
                ob = g % 2
                sb_ = 2 + g % 2
                pt, pt_tok = ptl.pop(n)
                mm(bank(ob)[:, c0:c0 + N], v_[:, kt, :], pt[:, 0:N], idx == 0, idx == nt - 1,
                   r=[H_t[vs], pt_tok], w=[PS_t[ob]], sig=False)
                mm(bank(sb_)[:, c0:c0 + N], onesb[:], pt[:, 0:N], idx == 0, idx == nt - 1,
                   r=[c_t, pt_tok], w=[PS_t[sb_]], sig=True)
                if idx == nt - 1:
                    ti = att["tmp"] % 2
                    att["tmp"] += 1
                    rec = H32(tmp_s[ti])[:, 0:512]
                    o32 = H32(tmp_s[ti])[:, 512:1024]
                    act(rec, bank(sb_), AF.Ln, r=[PS_t[sb_]], w=[H_t[tmp_s[ti]]])
                    act(rec, rec, AF.Exp, r=[H_t[tmp_s[ti]]], w=[H_t[tmp_s[ti]]], scale=-1.0)
                    tt("dve", o32, bank(ob), rec, ALU.mult, r=[PS_t[ob], H_t[tmp_s[ti]]], w=[H_t[tmp_s[ti]]])
                    tt("pool", A3[:, chunk, g * 512:(g + 1) * 512], o32, g_[:, g * 512:(g + 1) * 512], ALU.mult,
                       r=[H_t[tmp_s[ti]], H_t[gs]], w=[A_t[chunk][g]])

            LA = 3
            for n in range(len(jobs) + LA):
                if n < len(jobs):
                    s_stage(n)
                if n >= LA:
                    pv_stage(n - LA)

        att = {"pt": 0, "s": 0, "tmp": 0}
        PT_t = [Tok("PT%d" % k) for k in range(8)]

        def tok_split(parent, children):
            for ch in children:
                ch.w = dict(parent.w)
                ch.r = dict(parent.r)
                ch.pr = dict(parent.pr)
                ch.closed = parent.closed

        def tok_join(parent, children):
            w, r, pr = {}, {}, {}
            for ch in children:
                _merge(w, ch.w)
                _merge(r, ch.r)
                _merge(r, ch.w)
                _merge(pr, ch.pr)
            parent.w, parent.r, parent.pr, parent.closed = w, r, pr, True

        def attn(i):
            tok_split(H_t[11], PT_t[0:4])
            tok_split(H_t[12], PT_t[4:8])
            attn_inner(i)
            tok_join(H_t[11], PT_t[0:4])
            tok_join(H_t[12], PT_t[4:8])

        def attn_inner(i):
            heads = [("a", h) for h in range(8)] + [("b", h) for h in range(8)]
            attn_head(heads[0][0], i, heads[0][1], 0, loads_only=True)
            for n, (kind, h) in enumerate(heads):
                if n + 1 < len(heads):
                    attn_head(heads[n + 1][0], i, heads[n + 1][1], (n + 1) % 2, loads_only=True)
                attn_head(kind, i, h, n % 2, compute_only=True)

        def odd_layer(i, mods):
            w_in = sg_w_in[i]
            eh = [12, 13, 14, 15]
            ek = 0
            for cg in range(4):
                if mods:
                    mods.pop(0)()
                wu, wut = wload(w_in[:, cg * 512:(cg + 1) * 512], 16, 512)
                wg, wgt = wload(w_in[:, 4096 + cg * 512:4096 + (cg + 1) * 512], 16, 512)
                for cb in range(4):
                    fc = cg * 4 + cb
                    hu = eh[ek % 4]
                    hg = eh[(ek + 1) % 4]
                    ek += 2
                    b0 = nfm()
                    fm_block(wu, wut, slice(cb * 128, (cb + 1) * 128), 16, rhs_A, b0)
                    for tg in range(4):
                        cp("dve", H(hu)[:, tg * 512:(tg + 1) * 512], bank(b0 + tg), r=[PS_t[b0 + tg]], wp=[H_t[hu]])
                    b1 = nfm()
                    fm_block(wg, wgt, slice(cb * 128, (cb + 1) * 128), 16, rhs_A, b1)
                    act(H(hg), PS[:, b1 * 512:(b1 + 4) * 512], AF.Silu, r=[PS_t[b1 + k] for k in range(4)], w=[H_t[hg]])
                    tt("pool", H(hu), H(hu), H(hg), ALU.mult, r=[H_t[hu], H_t[hg]], w=[H_t[hu]])
                    dma("sp", UG[fc], H(hu), r=[H_t[hu]], w=[UG_t[fc]])
            for cg in range(4):
                if mods:
                    mods.pop(0)()
                wv_, wt_ = wload(w_in[:, 2048 + cg * 512:2048 + (cg + 1) * 512], 16, 512)
                for t_ in range(16):
                    b = tm_tile(t_, lambda kc: (wv_[:, kc, :], [wt_]), 16, lhs_A)
                    hi = eh[ek % 4]
                    ek += 1
                    o = H32(hi)[:, 0:512]
                    evac_copy(o, bank(b), r=[PS_t[b]], w=[H_t[hi]])
                    dma("sp", VR[t_ * 128:(t_ + 1) * 128, cg * 512:(cg + 1) * 512], o, r=[H_t[hi]], wp=[VR_t[t_]])
            wsb = H(0)[:, 0:1024].rearrange("p (g j) -> p g j", j=128)
            wsT = H(1)[:, 0:1024].rearrange("p (g i) -> p g i", i=128)
            E = Fs(1)
            Ev = E.rearrange("p (c i) -> p c i", i=128)
            dma("pool", wsb, sg_w_s[i].rearrange("g i j -> i g j"), w=[H_t[0]])
            mset("dve", wsb[0:64, :, 64:128], 0.0, wp=[H_t[0]])
            for half in range(2):
                b = nb()
                for j in range(4):
                    g = half * 4 + j
                    mm(bank(b)[:, j * 128:(j + 1) * 128], wsb[:, g, :], identb[:], True, True, r=[H_t[0], c_t], w=[PS_t[b]], sig=(j == 3))
                evac_copy(H(1)[:, half * 512:(half + 1) * 512], bank(b), r=[PS_t[b]], wp=[H_t[1]])
            dma("sp", bsrow[:], sg_b_s[i:i + 1, :], w=[bs_t])
            rs_b = [nb(), nb()]
            bs_b = [nb(), nb()]
            for half in range(2):
                for j in range(4):
                    g = half * 4 + j
                    mm(bank(rs_b[half])[:, j * 128:(j + 1) * 128], onesb[:], wsT[:, g, :], True, True, r=[H_t[1], c_t],
                       w=[PS_t[rs_b[half]]], sig=(j == 3))
                mm(bank(bs_b[half]), onesf[0:1, :], bsrow[0:1, half * 512:(half + 1) * 512], True, True, r=[bs_t, c_t],
                   w=[PS_t[bs_b[half]]], sig=True)
            bsr = H32(4)
            for half in range(2):
                cp("dve", bsr[:, half * 512:(half + 1) * 512], bank(bs_b[half]), r=[PS_t[bs_b[half]]], wp=[H_t[4]])
            for fc in range(16):
                g = fc // 2
                stt("dve", Ev[:, fc, :], bank(rs_b[g // 4])[:, (g % 4) * 128:(g % 4 + 1) * 128], lnb_t[:, i * 16 + fc:i * 16 + fc + 1],
                    bsr[:, g * 128:(g + 1) * 128], ALU.mult, ALU.add, r=[PS_t[rs_b[g // 4]], H_t[4], c_t], wp=F_t(1))

            def loads(n):
                p = n % 2
                dma("sp", Fs(2 + p), VR[n * 128:(n + 1) * 128, :], r=[VR_t[n]], w=F_t(2 + p))
                dma("sp", H(8 + p).rearrange("p (c t) -> p c t", t=128), UG[:, :, n * 128:(n + 1) * 128].rearrange("c p t -> p c t"),
                    r=UG_t, w=[H_t[8 + p]])

            loads(0)
            for n in range(16):
                if n + 1 < 16:
                    loads(n + 1)
                p = n % 2
                vt = Fs(2 + p)
                ug = H(8 + p).rearrange("p (c t) -> p c t", t=128)
                si = rr["st"] % 8
                rr["st"] += 1
                c = si * 8
                s_ = [st_t[si]]
                act(junk[:], vt, AF.Identity, r=F_t(2 + p), w=s_, accum=st[:, c:c + 1])
                act(junk[:], vt, AF.Square, r=F_t(2 + p), wp=s_, accum=st[:, c + 1:c + 2])
                ts("dve", st[:, c + 2:c + 3], st[:, c:c + 1], 1.0 / D, None, ALU.mult, None, r=s_, w=s_)
                tt("dve", st[:, c + 3:c + 4], st[:, c + 2:c + 3], st[:, c + 2:c + 3], ALU.mult, r=s_, w=s_)
                stt("dve", st[:, c + 4:c + 5], st[:, c + 1:c + 2], 1.0 / D, st[:, c + 3:c + 4], ALU.mult, ALU.subtract, r=s_, w=s_)
                act(st[:, c + 5:c + 6], st[:, c + 4:c + 5], AF.Sqrt, r=s_, w=s_, bias=EPS, scale=1.0)
                recip(st[:, c + 6:c + 7], st[:, c + 5:c + 6], r=s_, w=s_)
                stt("dve", st[:, c + 7:c + 8], st[:, c + 2:c + 3], -1.0, st[:, c + 6:c + 7], ALU.mult, ALU.mult, r=s_, w=s_)
                vn = H(10 + p)
                ts("dve", vn, vt, st[:, c + 6:c + 7], st[:, c + 7:c + 8], ALU.mult, ALU.add, r=F_t(2 + p) + s_, w=[H_t[10 + p]])
                for q in range(4):
                    b = nb()
                    for j in range(4):
                        fc = 4 * q + j
                        mm(bank(b)[:, j * 128:(j + 1) * 128], vn[:, fc * 128:(fc + 1) * 128], wsT[:, fc // 2, :], True, True,
                           r=[H_t[10 + p], H_t[1]], w=[PS_t[b]], sig=(j == 3))
                    ti = att["tmp"] % 2
                    att["tmp"] += 1
                    tmp = H32(13 + ti)[:, 0:512]
                    for j in range(4):
                        fc = 4 * q + j
                        stt("dve", tmp[:, j * 128:(j + 1) * 128], bank(b)[:, j * 128:(j + 1) * 128], lng_t[:, i * 16 + fc:i * 16 + fc + 1],
                            Ev[:, fc, :], ALU.mult, ALU.add, r=[PS_t[b], c_t] + F_t(1), wp=[H_t[13 + ti]])
                    for j in range(4):
                        fc = 4 * q + j
                        tt("pool", A3[:, fc, n * 128:(n + 1) * 128], tmp[:, j * 128:(j + 1) * 128], ug[:, fc, :], ALU.mult,
                           r=[H_t[13 + ti], H_t[8 + p]], wp=[A_t[fc][n // 4]])

        def whole():
            consts()
            chk("const")
            for cg in range(8):
                mod_group(0, cg)
                chk("mod%d" % cg)
            mod_flush()
            chk("mod")
            for _ in norm_gen(0, x_in, XS[0], False, True):
                pass
            chk("norm0")
            Xcur = x_in
            for l in range(depth):
                mods = []
                if l == 0:
                    mods = [(lambda cg=cg: mod_group(0, cg)) for cg in range(8, 12)]
                if l + 1 < depth:
                    mods += [(lambda l1=l + 1, cg=cg: mod_group(l1, cg)) for cg in range(12)]
                i = l // 2
                if l % 2 == 0:
                    even_layer(i, mods)
                    w_out_l = ab_w_out[i]
                else:
                    odd_layer(i, mods)
                    w_out_l = sg_w_out[i]
                chk("mix%d" % l)
                Xnext = out if l + 1 == depth else XS[(l + 1) % 2]
                ngen = norm_gen(l + 1, Xcur, Xnext, True, l + 1 < depth)
                out_phase(w_out_l, mods, ngen)
                chk("out%d" % l)
                while mods:
                    mods.pop(0)()
                mod_flush()
                for _ in ngen:
                    pass
                Xcur = Xnext

        def reset_all():
            P.reset()
            for t in Tok.ALL:
                t.reset()
            for d_ in (rr, ev, att):
                for k_ in d_:
                    d_[k_] = 0
            plan["k"] = 0
            plan["issued"] = 0

        for pass_ in range(2):
            try:
                whole()
            except _Stop:
                pass
            if pass_ == 0:
                plan["replay"] = plan["specs"]
                reset_all()
        fin = {}
        for t in X_t[id(out)]:
            _merge(fin, t.w)
        P.stream["sp"].append((fin, None, None))

        with nc.Block() as block:
            @block.tensor
            def _(e):
                P.emit("pe", e)

            @block.scalar
            def _(e):
                P.emit("act", e)

            @block.vector
            def _(e):
                P.emit("dve", e)

            @block.gpsimd
            def _(e):
                P.emit("pool", e)

            @block.sync
            def _(e):
                P.emit("sp", e)
        build.stats = {e: len(s) for e, s in P.stream.items()}
    return nc


def _host_inputs(inputs, b):
    f = np.float32
    x = np.ascontiguousarray(inputs["x"][b], dtype=f)
    c = np.asarray(inputs["c"][b], dtype=f)

    def col(v):
        return np.ascontiguousarray(np.asarray(v, dtype=f).reshape(-1, 128).T)

    def cols(m):
        return np.ascontiguousarray(np.concatenate([col(m[l]) for l in range(m.shape[0])], axis=1))

    tab = np.asarray(inputs["b_rel_bias"], dtype=f)
    kk = np.arange(64)[:, None]
    cc = np.arange(320)[None, :]
    idx = np.minimum(cc - kk + 128, 256)
    TB = tab[:, :, idx]
    tb4 = np.full((2, 8, 128, 256), NEG, dtype=f)
    tb4[:, :, 0:64, :] = TB[:, :, :, 0:256]
    tb4[:, :, 64:128, 64:256] = TB[:, :, :, 0:192]
    tb3 = np.empty((2, 8, 128, 128), dtype=f)
    tb3[:, :, 0:64, :] = TB[:, :, :, 128:256]
    tb3[:, :, 64:128, :] = TB[:, :, :, 64:192]
    crep = np.ascontiguousarray(np.broadcast_to(tab[:, :, 256].reshape(1, 16), (128, 16)), dtype=f)
    half = 32
    freqs = (10000.0 ** (-np.arange(half, dtype=f) / f(half))).astype(f)
    ang = (np.arange(S, dtype=f)[None, :] * freqs[:, None]).astype(f)
    cs = np.concatenate([np.cos(ang), np.cos(ang), np.sin(ang), np.sin(ang)], 0).astype(f)
    d = {
        "x": x, "ccol": col(c),
        "w_mod": inputs["w_mod"], "b_mod": inputs["b_mod"],
        "gpre_col": cols(np.asarray(inputs["g_pre"])), "g_post": inputs["g_post"],
        "ab_w_in": inputs["ab_w_in"], "gq_col": cols(np.asarray(inputs["a_g_q"])), "a_w_uq": inputs["a_w_uq"],
        "gkv_col": cols(np.asarray(inputs["a_g_kv"])), "a_w_ukv": inputs["a_w_ukv"],
        "tb4": tb4, "tb3": tb3, "crep": crep, "ab_w_out": inputs["ab_w_out"],
        "sg_w_in": inputs["sg_w_in"], "lng_col": cols(np.asarray(inputs["sg_ln_g"])), "lnb_col": cols(np.asarray(inputs["sg_ln_b"])),
        "sg_w_s": inputs["sg_w_s"], "sg_b_s": np.asarray(inputs["sg_b_s"], dtype=f).reshape(2, 1024), "sg_w_out": inputs["sg_w_out"],
        "ident": np.eye(128, dtype=f), "ropecs": cs,
    }
    return {k: np.ascontiguousarray(np.asarray(v, dtype=f)) for k, v in d.items()}


_NC = {}


def kernel(**inputs):
    inputs = {k: np.asarray(v) for k, v in inputs.items()}
    if DEPTH not in _NC:
        _NC[DEPTH] = build(DEPTH)
    nc = _NC[DEPTH]
    shared = None
    in_maps = []
    for b in range(8):
        m = _host_inputs(inputs, b) if shared is None else dict(shared)
        if shared is None:
            shared = m
        else:
            m["x"] = np.ascontiguousarray(inputs["x"][b], dtype=np.float32)
            m["ccol"] = np.ascontiguousarray(np.asarray(inputs["c"][b], dtype=np.float32).reshape(-1, 128).T)
        in_maps.append(m)
    res = run_bass_kernel_spmd(nc, in_maps, core_ids=list(range(8)))
    return np.stack([np.asarray(r["out"]) for r in res.results], axis=0).astype(np.float32)
```

```python
import numpy as np
from contextlib import ExitStack
import concourse.bass as bass
import concourse.mybir as mybir
from concourse.bass_utils import run_bass_kernel_spmd

F32 = mybir.dt.float32
BF16 = mybir.dt.bfloat16
AF = mybir.ActivationFunctionType
ALU = mybir.AluOpType

S = 2048
D = 2048
DEPTH = 4
EPS = 1e-6
NEG = -30000.0
CE = ("pe", "act", "dve", "pool")


class Tok:
    __slots__ = ("name", "w", "r", "closed", "pr", "psum")

    ALL = []

    def __init__(self, name, psum=False):
        self.name = name
        self.psum = psum
        self.reset()
        Tok.ALL.append(self)

    def reset(self):
        self.w = {}
        self.r = {}
        self.pr = {}
        self.closed = False


def _merge(d, s):
    for k, v in s.items():
        if d.get(k, 0) < v:
            d[k] = v


class Prog:
    NS = 8

    def __init__(self, nc, es):
        self.nc = nc
        self.sem = {e: es.enter_context(nc.semaphore("s_" + e)) for e in CE}
        self.cnt = {e: 0 for e in CE}
        self.dq = ("sp", "pool", "act")
        self.dsem = {q: [es.enter_context(nc.semaphore("d_%s%d" % (q, i))) for i in range(self.NS)] for q in self.dq}
        self.reset()

    def reset(self):
        self.cnt = {e: 0 for e in CE}
        self.dval = {q: [0] * self.NS for q in self.dq}
        self.dnext = {q: 0 for q in self.dq}
        self.stream = {e: [] for e in ("pe", "act", "dve", "pool", "sp")}
        self.pend = {e: ([], [], []) for e in CE}
        self.pendset = {e: set() for e in CE}
        self.nops = 0

    def semh(self, key):
        if isinstance(key, tuple):
            return self.dsem[key[0]][key[1]]
        return self.sem[key]

    def _publish(self, r, w, wp, key, val):
        for t in r:
            if t.r.get(key, 0) < val:
                t.r[key] = val
            t.closed = True
        for t in w:
            t.w = {key: val}
            t.r = {}
            t.pr = {key: val}
            t.closed = False
        for t in wp:
            if t.closed:
                t.pr = t.r
                t.w = {key: val}
                t.r = {}
                t.closed = False
            else:
                if t.w.get(key, 0) < val:
                    t.w[key] = val

    def op(self, eng, fn, r=(), w=(), wp=(), sig=True, dma=False):
        self.nops += 1
        waits = {}
        if eng != "pe":
            xs = [t for t in (*r, *w, *wp) if t.psum]
            if xs:
                r = [t for t in r if not t.psum]
                wp = [t for t in wp if not t.psum]
                w = [t for t in w if not t.psum] + xs
                for t in xs:
                    for d_ in (t.w, t.r, t.pr):
                        for k, v in d_.items():
                            if k != eng and waits.get(k, 0) < v:
                                waits[k] = v
                own = waits.get(eng)
            else:
                own = None
        else:
            xs = ()
            own = None
        for t in (*r, *w, *wp):
            for e2, ps in self.pendset.items():
                if e2 != eng and t in ps:
                    raise RuntimeError("token %s pending on %s touched by %s" % (t.name, e2, eng))
        for t in r:
            _merge(waits, t.w)
        for t in w:
            _merge(waits, t.w)
            _merge(waits, t.r)
        for t in wp:
            _merge(waits, t.r)
            if not t.closed:
                _merge(waits, t.pr)
        if eng == "pe":
            waits.pop("pe", None)
        elif xs:
            ownv = 0
            for t in r:
                ownv = max(ownv, t.w.get(eng, 0))
            for t in w:
                if not t.psum:
                    ownv = max(ownv, t.w.get(eng, 0), t.r.get(eng, 0))
            for t in wp:
                ownv = max(ownv, t.r.get(eng, 0))
                if not t.closed:
                    ownv = max(ownv, t.pr.get(eng, 0))
            if ownv:
                waits[eng] = ownv
            else:
                waits.pop(eng, None)
        if dma:
            q = eng
            slot = self.dnext[q] % self.NS
            self.dnext[q] += 1
            key = (q, slot)
            if self.dval[q][slot] > 0:
                waits[key] = max(waits.get(key, 0), self.dval[q][slot])
            self.dval[q][slot] += 16
            self._publish(r, w, wp, key, self.dval[q][slot])
            inc = (self.dsem[q][slot], 16)
        else:
            pr, pw, pwp = self.pend[eng]
            pr.extend(r)
            pw.extend(w)
            pwp.extend(wp)
            if sig:
                self.cnt[eng] += 1
                self._publish(pr, pw, pwp, eng, self.cnt[eng])
                self.pend[eng] = ([], [], [])
                self.pendset[eng] = set()
                inc = (self.sem[eng], 1)
            else:
                assert eng == "pe"
                self.pendset[eng].update(r)
                self.pendset[eng].update(w)
                self.pendset[eng].update(wp)
                inc = None
        self.stream[eng].append((waits, fn, inc))

    def emit(self, eng, e):
        seen = {}
        for waits, fn, inc in self.stream[eng]:
            for k, v in waits.items():
                if seen.get(k, 0) >= v:
                    continue
                seen[k] = v
                e.wait_ge(self.semh(k), v)
            if fn is not None:
                ins = fn(e)
                if inc is not None:
                    ins.then_inc(inc[0], inc[1])


class _Stop(Exception):
    pass


def build(depth=DEPTH, stop=None):
    Tok.ALL = []
    nc = bass.Bass("TRN2", target_bir_lowering=False)

    def din(name, shape, dt=F32):
        return nc.dram_tensor(name, list(shape), dt, kind="ExternalInput").ap()

    def dscr(name, shape, dt):
        return nc.dram_tensor(name, list(shape), dt).ap()

    x_in = din("x", [S, D])
    ccol = din("ccol", [128, 16])
    w_mod = din("w_mod", [4, D, 6144])
    b_mod = din("b_mod", [4, 6144])
    gpre_col = din("gpre_col", [128, 64])
    g_post = din("g_post", [4, D])
    ab_w_in = din("ab_w_in", [2, D, 5952])
    gq_col = din("gq_col", [128, 8])
    a_w_uq = din("a_w_uq", [2, 512, 1536])
    gkv_col = din("gkv_col", [128, 4])
    a_w_ukv = din("a_w_ukv", [2, 256, 2048])
    tb4 = din("tb4", [2, 8, 128, 256])
    tb3 = din("tb3", [2, 8, 128, 128])
    crep = din("crep", [128, 16])
    ab_w_out = din("ab_w_out", [2, D, D])
    sg_w_in = din("sg_w_in", [2, D, 6144])
    lng_col = din("lng_col", [128, 32])
    lnb_col = din("lnb_col", [128, 32])
    sg_w_s = din("sg_w_s", [2, 8, 128, 128])
    sg_b_s = din("sg_b_s", [2, 1024])
    sg_w_out = din("sg_w_out", [2, D, D])
    ident_d = din("ident", [128, 128])
    ropecs = din("ropecs", [128, S])
    out = nc.dram_tensor("out", [S, D], F32, kind="ExternalOutput").ap()

    XS = [dscr("XA", [S, D], F32), dscr("XB", [S, D], F32)]
    Y2 = dscr("Y2", [S, D], F32)
    G2ROW = dscr("G2ROW", [4, D], F32)
    QN = dscr("QN", [8, 128, S], BF16)
    QR = dscr("QR", [8, 64, S], BF16)
    KN = dscr("KN", [8, 128, S], BF16)
    VA = dscr("VA", [S, 1024], BF16)
    BQ = dscr("BQ", [8, 128, S], BF16)
    BK = dscr("BK", [8, 128, S], BF16)
    BV = dscr("BV", [S, 1024], BF16)
    GT = dscr("GT", [16, 128, S], BF16)
    UG = dscr("UG", [16, 128, S], BF16)
    VR = dscr("VR", [S, D], F32)

    es = ExitStack()
    with es:
        def sb(name, shape, dt):
            return es.enter_context(nc.sbuf_tensor(name, list(shape), dt))

        A = sb("A", [128, 16 * 2048], BF16)
        A3 = A[:].rearrange("p (k t) -> p k t", t=2048)
        WB = [sb("WB%d" % i, [128, 8192], BF16) for i in range(3)]
        SL = [sb("SL%d" % i, [128, 4096], BF16) for i in range(8)]
        PS = es.enter_context(nc.psum_tensor("PS", [128, 4096], F32))
        identb = sb("identb", [128, 128], BF16)
        onesb = sb("onesb", [128, 128], BF16)
        onesf = sb("onesf", [128, 128], F32)
        csT = sb("csT", [128, 16], BF16)
        ccol_t = sb("ccol_t", [128, 16], F32)
        gpre_t = sb("gpre_t", [128, 64], F32)
        gq_t = sb("gq_t", [128, 8], F32)
        gkv_t = sb("gkv_t", [128, 4], F32)
        crep_t = sb("crep_t", [128, 16], F32)
        lng_t = sb("lng_t", [128, 32], F32)
        lnb_t = sb("lnb_t", [128, 32], F32)
        Acol = sb("Acol", [128, 64], F32)
        Bcol = sb("Bcol", [128, 64], F32)
        st = sb("st", [128, 64], F32)
        rows = [sb("row%d" % i, [1, 512], F32) for i in range(2)]
        foldb = sb("foldb", [128, 64], BF16)
        brow = [sb("brow%d" % i, [1, 512], F32) for i in range(2)]
        grow = [sb("grow%d" % i, [1, 512], F32) for i in range(2)]
        identf = sb("identf", [128, 128], F32)
        diag = [sb("diag%d" % i, [128, 128], F32) for i in range(2)]
        junk = sb("junk", [128, 2048], BF16)
        bt4 = [sb("bt4_%d" % i, [128, 256], F32) for i in range(2)]
        bt3 = [sb("bt3_%d" % i, [128, 128], F32) for i in range(2)]
        bsrow = sb("bsrow", [1, 1024], F32)

        P = Prog(nc, es)

        A_t = [[Tok("A%d_%d" % (k, g)) for g in range(4)] for k in range(16)]
        WB_t = [Tok("WB%d" % i) for i in range(3)]
        H_t = [Tok("H%d" % i) for i in range(16)]
        PS_t = [Tok("PS%d" % i, psum=True) for i in range(8)]
        c_t = Tok("consts")
        st_t = [Tok("st%d" % i) for i in range(8)]
        row_t = [Tok("row%d" % i) for i in range(2)]
        brow_t = [Tok("brow%d" % i) for i in range(2)]
        grow_t = [Tok("grow%d" % i) for i in range(2)]
        diag_t = [Tok("diag%d" % i) for i in range(2)]
        bt_t = [Tok("bt%d" % i) for i in range(2)]
        AB_t = [Tok("AB%d" % l) for l in range(4)]
        X_t = {id(XS[0]): [Tok("XA%d" % i) for i in range(16)], id(XS[1]): [Tok("XB%d" % i) for i in range(16)],
               id(out): [Tok("out%d" % i) for i in range(16)], id(x_in): [Tok("xin%d" % i) for i in range(16)]}
        Y2_t = [Tok("Y2_%d" % i) for i in range(16)]
        G2_t = [Tok("G2_%d" % i) for i in range(4)]
        QN_t = [Tok("QN%d" % i) for i in range(8)]
        QR_t = [Tok("QR%d" % i) for i in range(8)]
        KN_t = [Tok("KN%d" % i) for i in range(8)]
        VA_t = Tok("VA")
        BQ_t = [Tok("BQ%d" % i) for i in range(8)]
        BK_t = [Tok("BK%d" % i) for i in range(8)]
        BV_t = Tok("BV")
        GT_t = [Tok("GT%d" % i) for i in range(16)]
        UG_t = [Tok("UG%d" % i) for i in range(16)]
        VR_t = [Tok("VR%d" % i) for i in range(16)]
        bs_t = Tok("bsrow")

        def H(i):
            return SL[i // 2][:, (i % 2) * 2048:(i % 2) * 2048 + 2048]

        def H32(i):
            return H(i).bitcast(F32)

        def Fs(j):
            return SL[j][:].bitcast(F32)

        def F_t(j):
            return [H_t[2 * j], H_t[2 * j + 1]]

        def bank(b):
            return PS[:, b * 512:(b + 1) * 512]

        def mm(o, lhsT, rhs, start, stop, r, w, sig):
            P.op("pe", lambda e: e.matmul(o, lhsT, rhs, start=start, stop=stop), r=r, w=w, sig=sig)

        def act(o, i, func, r, w=(), wp=(), bias=None, scale=None, accum=None):
            kw = {}
            if bias is not None:
                kw["bias"] = bias
            if scale is not None:
                kw["scale"] = scale
            if accum is not None:
                kw["accum_out"] = accum
            P.op("act", lambda e: e.activation(out=o, in_=i, func=func, **kw), r=r, w=w, wp=wp)

        def ts(eng, o, i, s1, s2, op0, op1, r, w=(), wp=()):
            if s2 is None:
                P.op(eng, lambda e: e.tensor_scalar(o, i, s1, None, op0), r=r, w=w, wp=wp)
            else:
                P.op(eng, lambda e: e.tensor_scalar(o, i, s1, s2, op0, op1), r=r, w=w, wp=wp)

        def stt(eng, o, i0, sc, i1, op0, op1, r, w=(), wp=()):
            P.op(eng, lambda e: e.scalar_tensor_tensor(o, i0, sc, i1, op0, op1), r=r, w=w, wp=wp)

        def tt(eng, o, i0, i1, op, r, w=(), wp=()):
            P.op(eng, lambda e: e.tensor_tensor(o, i0, i1, op), r=r, w=w, wp=wp)

        def cp(eng, o, i, r, w=(), wp=()):
            P.op(eng, lambda e: e.tensor_copy(o, i), r=r, w=w, wp=wp)

        def recip(o, i, r, w=(), wp=()):
            P.op("dve", lambda e: e.reciprocal(o, i), r=r, w=w, wp=wp)

        def mset(eng, o, val, w=(), wp=()):
            P.op(eng, lambda e: e.memset(o, val), w=w, wp=wp)

        def dma(q, o, i, r=(), w=(), wp=(), slow=False):
            if slow:
                P.op(q, lambda e: e.dma_start(out=o, in_=i, allow_slow_non_contiguous=True), r=r, w=w, wp=wp, dma=True)
            else:
                P.op(q, lambda e: e.dma_start(out=o, in_=i), r=r, w=w, wp=wp, dma=True)

        def chk(name):
            if stop == name:
                raise _Stop()

        rr = {"bank": 0, "fm": 0, "w": 0, "row": 0, "st": 0}

        def nb():
            b = rr["bank"] % 8
            rr["bank"] += 1
            return b

        def nfm():
            b = 4 * (rr["fm"] % 2)
            rr["fm"] += 1
            return b

        plan = {"specs": [], "replay": None, "k": 0, "issued": 0}

        def wload_generic(issue_fn):
            k = plan["k"]
            plan["k"] += 1
            if plan["replay"] is None:
                plan["specs"].append(issue_fn)
                issue_fn(k % 3)
            else:
                specs = plan["replay"]
                while plan["issued"] <= min(k + 1, len(specs) - 1):
                    j = plan["issued"]
                    tb_ = WB_t[j % 3]
                    assert tb_.closed or not tb_.w, "weight buffer %d reloaded before its consumers were recorded (load %d)" % (j % 3, j)
                    specs[j](j % 3)
                    plan["issued"] += 1
            return k % 3

        def wload(src, nkc, ncols, bufcols=None, col_off=0):
            bufcols = bufcols or ncols

            def view(i):
                return WB[i][:, 0:nkc * bufcols].rearrange("p (k c) -> p k c", c=bufcols)

            def issue(i):
                dma("pool", view(i)[:, :, col_off:col_off + ncols], src.rearrange("(k p) c -> p k c", p=128), w=[WB_t[i]])

            i = wload_generic(issue)
            return view(i), WB_t[i]

        def consts():
            dma("pool", identb[:], ident_d[:, :], w=[c_t])
            dma("sp", identf[:], ident_d[:, :], wp=[c_t])
            mset("dve", onesb[:], 1.0, wp=[c_t])
            mset("dve", onesf[:], 1.0, wp=[c_t])
            dma("sp", ccol_t[:], ccol[:, :], wp=[c_t])
            dma("sp", gpre_t[:], gpre_col[:, :], wp=[c_t])
            dma("sp", gq_t[:], gq_col[:, :], wp=[c_t])
            dma("sp", gkv_t[:], gkv_col[:, :], wp=[c_t])
            dma("sp", crep_t[:], crep[:, :], wp=[c_t])
            dma("sp", lng_t[:], lng_col[:, :], wp=[c_t])
            dma("sp", lnb_t[:], lnb_col[:, :], wp=[c_t])
            act(csT[:], ccol_t[:], AF.Silu, r=[c_t], wp=[c_t])
            tt("dve", foldb[:], identb[:, 0:64], identb[:, 64:128], ALU.add, r=[c_t], wp=[c_t])

        modq = []

        def mod_rows_prefetch(l, cg):
            bi = cg % 2
            dma("sp", brow[bi][:], b_mod[l:l + 1, cg * 512:(cg + 1) * 512], w=[brow_t[bi]])
            if cg >= 8:
                g0 = (cg - 8) * 512
                dma("sp", grow[bi][:], g_post[l:l + 1, g0:g0 + 512], w=[grow_t[bi]])

        def mod_flush():
            while modq:
                modq.pop(0)()

        def mod_group(l, cg):
            wv, wt = wload(w_mod[l, :, cg * 512:(cg + 1) * 512], 16, 512)
            mod_flush()
            if cg == 0:
                mod_rows_prefetch(l, 0)
            if cg + 1 < 12:
                mod_rows_prefetch(l, cg + 1)
            b = nb()
            for kc in range(16):
                mm(bank(b)[0:1, :], csT[:, kc:kc + 1], wv[:, kc, :], kc == 0, kc == 15, r=[wt, c_t], w=[PS_t[b]], sig=(kc == 15))
            bi = cg % 2
            ri = cg % 2
            tt("dve", rows[ri][:], bank(b)[0:1, :], brow[bi][:], ALU.add, r=[PS_t[b], brow_t[bi]], w=[row_t[ri]])
            if cg < 8:
                def fin():
                    b2 = nb()
                    for j in range(4):
                        mm(bank(b2)[:, j:j + 1], rows[ri][0:1, j * 128:(j + 1) * 128], onesf[0:1, 0:1], True, True,
                           r=[row_t[ri], c_t], w=[PS_t[b2]], sig=(j == 3))
                    if cg < 4:
                        c0 = l * 16 + cg * 4
                        cp("dve", Bcol[:, c0:c0 + 4], bank(b2)[:, 0:4], r=[PS_t[b2]], wp=[AB_t[l]])
                    else:
                        c0 = l * 16 + (cg - 4) * 4
                        stt("dve", Acol[:, c0:c0 + 4], bank(b2)[:, 0:4], 1.0, gpre_t[:, c0:c0 + 4], ALU.add, ALU.mult,
                            r=[PS_t[b2], c_t], wp=[AB_t[l]])
                modq.append(fin)
            else:
                g0 = (cg - 8) * 512
                tt("dve", rows[ri][:], rows[ri][:], grow[bi][:], ALU.mult, r=[row_t[ri], grow_t[bi]], w=[row_t[ri]])
                dma("sp", G2ROW[l:l + 1, g0:g0 + 512], rows[ri][:], r=[row_t[ri]], wp=[G2_t[l]])

        def norm_gen(l, Xprev, Xnext, has_y2, make_h):
            G2rep = Fs(5)
            if has_y2:
                dma("sp", G2rep, G2ROW[l - 1:l, :].broadcast_to([128, D]), r=[G2_t[l - 1]], w=F_t(5))

            def ysl(t_):
                return t_ % 2

            def xsl(t_):
                return 2 + t_ % 3

            def loads(t_):
                if has_y2:
                    dma("sp", Fs(ysl(t_)), Y2[t_ * 128:(t_ + 1) * 128, :], r=[Y2_t[t_]], w=F_t(ysl(t_)))
                dma("sp", Fs(xsl(t_)), Xprev[t_ * 128:(t_ + 1) * 128, :], r=[X_t[id(Xprev)][t_]], w=F_t(xsl(t_)))

            cols = {}

            def stage_a(t_):
                y2 = Fs(ysl(t_))
                xt = Fs(xsl(t_))
                yt_, xt_ = F_t(ysl(t_)), F_t(xsl(t_))
                si = rr["st"] % 8
                rr["st"] += 1
                c = si * 8
                cols[t_] = (si, c)
                stt_ = [st_t[si]]
                if has_y2:
                    act(junk[:], y2, AF.Square, r=yt_, w=stt_, accum=st[:, c:c + 1])
                    act(st[:, c + 1:c + 2], st[:, c:c + 1], AF.Sqrt, r=stt_, w=stt_, bias=EPS, scale=1.0 / D)
                    recip(st[:, c + 2:c + 3], st[:, c + 1:c + 2], r=stt_, w=stt_)
                    tt("dve", y2, y2, G2rep, ALU.mult, r=yt_ + F_t(5), w=yt_)
                    stt("dve", xt, y2, st[:, c + 2:c + 3], xt, ALU.mult, ALU.add, r=yt_ + xt_ + stt_, w=xt_)
                    dma("sp", Xnext[t_ * 128:(t_ + 1) * 128, :], xt, r=xt_, w=[X_t[id(Xnext)][t_]])

            def stage_b(t_):
                xt = Fs(xsl(t_))
                xt_ = F_t(xsl(t_))
                si, c = cols[t_]
                stt_ = [st_t[si]]
                act(junk[:], xt, AF.Square, r=xt_, w=stt_, accum=st[:, c + 3:c + 4])
                act(st[:, c + 4:c + 5], st[:, c + 3:c + 4], AF.Sqrt, r=stt_, w=stt_, bias=EPS, scale=1.0 / D)
                recip(st[:, c + 5:c + 6], st[:, c + 4:c + 5], r=stt_, w=stt_)
                dg = diag[t_ % 2]
                ts("dve", dg[:], identf[:], st[:, c + 5:c + 6], None, ALU.mult, None, r=stt_ + [c_t], w=[diag_t[t_ % 2]])

            def stage_b2(t_):
                xt = Fs(xsl(t_))
                xt_ = F_t(xsl(t_))
                dg = diag[t_ % 2]
                for q in range(4):
                    b = nb()
                    for j in range(4):
                        fc = 4 * q + j
                        mm(bank(b)[:, j * 128:(j + 1) * 128], xt[:, fc * 128:(fc + 1) * 128], dg[:], True, True,
                           r=xt_ + [diag_t[t_ % 2]], w=[PS_t[b]], sig=(j == 3))
                    for j in range(4):
                        fc = 4 * q + j
                        o = A3[:, fc, t_ * 128:(t_ + 1) * 128]
                        i_ = bank(b)[:, j * 128:(j + 1) * 128]
                        ac = Acol[:, l * 16 + fc:l * 16 + fc + 1]
                        bc = Bcol[:, l * 16 + fc:l * 16 + fc + 1]
                        if q % 2 == 0:
                            act(o, i_, AF.Identity, r=[PS_t[b], AB_t[l]], wp=[A_t[fc][t_ // 4]], bias=bc, scale=ac)
                        else:
                            ts("dve", o, i_, ac, bc, ALU.mult, ALU.add, r=[PS_t[b], AB_t[l]], wp=[A_t[fc][t_ // 4]])

            loads(0)
            loads(1)
            stage_a(0)
            for t_ in range(16):
                if make_h and t_ >= 1:
                    stage_b2(t_ - 1)
                if t_ + 2 < 16:
                    loads(t_ + 2)
                if t_ + 1 < 16:
                    stage_a(t_ + 1)
                if make_h:
                    stage_b(t_)
                yield t_
            if make_h:
                stage_b2(15)

        def fm_block(wv, wt, cols, nkc, rhs_fn, b0, M=128, banks=None):
            for kc in range(nkc):
                for tg in range(4):
                    rhs, rt = rhs_fn(kc, tg)
                    b = b0 + tg
                    mm(bank(b)[0:M, :], wv[:, kc, cols], rhs, kc == 0, kc == nkc - 1, r=[wt] + rt, w=[PS_t[b]], sig=(kc == nkc - 1))

        def rhs_A(kc, tg):
            return A3[:, kc, tg * 512:(tg + 1) * 512], [A_t[kc][tg]]

        def tm_tile(t_, rhs_fn, nkc, lhs_fn):
            b = nb()
            for kc in range(nkc):
                lhsT, lt = lhs_fn(kc, t_)
                rhs, rt = rhs_fn(kc)
                mm(bank(b), lhsT, rhs, kc == 0, kc == nkc - 1, r=lt + rt, w=[PS_t[b]], sig=(kc == nkc - 1))
            return b

        def lhs_A(kc, t_):
            return A3[:, kc, t_ * 128:(t_ + 1) * 128], [A_t[kc][t_ // 4]]

        ev = {"i": 0}

        def evac_copy(o, i_, r, w=(), wp=()):
            ev["i"] += 1
            if ev["i"] % 2 == 0:
                act(o, i_, AF.Copy, r=r, w=w, wp=wp)
            else:
                cp("dve", o, i_, r=r, w=w, wp=wp)

        def fm_to_dram(wv, wt, cols, dst, dst_t, hslot, silu=False):
            b0 = nfm()
            fm_block(wv, wt, cols, 16, rhs_A, b0)
            o = H(hslot)
            src = PS[:, b0 * 512:(b0 + 4) * 512]
            rt = [PS_t[b0 + i] for i in range(4)]
            if silu:
                act(o, src, AF.Silu, r=rt, w=[H_t[hslot]])
            else:
                for tg in range(4):
                    evac_copy(o[:, tg * 512:(tg + 1) * 512], bank(b0 + tg), r=[PS_t[b0 + tg]], wp=[H_t[hslot]])
            dma("sp", dst, o, r=[H_t[hslot]], w=[dst_t])

        def out_phase(w_out_l, mods, ngen):
            hs = [12, 13, 14, 15]
            k = 0
            adv = 0
            for half in range(2):
                if half == 1:
                    while len(mods) > 4:
                        mods.pop(0)()
                    mod_flush()
                for cg in range(4):
                    if mods:
                        mods.pop(0)()
                    wv, wt = wload(w_out_l[:, cg * 512:(cg + 1) * 512], 16, 512)
                    for t_ in range(8 * half, 8 * half + 8):
                        b = tm_tile(t_, lambda kc: (wv[:, kc, :], [wt]), 16, lhs_A)
                        hi = hs[k % 4]
                        k += 1
                        o = H32(hi)[:, 0:512]
                        evac_copy(o, bank(b), r=[PS_t[b]], w=[H_t[hi]])
                        dma("sp", Y2[t_ * 128:(t_ + 1) * 128, cg * 512:(cg + 1) * 512], o, r=[H_t[hi]], wp=[Y2_t[t_]])
                        if half == 1 and t_ % 4 == 3 and adv < 6:
                            next(ngen)
                            adv += 1

        def even_layer(i, mods):
            w_in = ab_w_in[i]
            CS = Fs(3)
            dma("sp", CS, ropecs[:, :], w=F_t(3))
            cqn = [H(0), H(1), H(2), H(3)]
            ckvn = [H(4), H(5)]
            KRs = 10
            KR = H(KRs)
            sq_h = [H32(8), H32(9)]
            rst_h = H32(11)
            Tt = H(11)[:, 1024:1536]

            def rope_fold(src_bank, tg, dst, dst_tok):
                tt("dve", Tt, bank(src_bank), CS[:, tg * 512:(tg + 1) * 512], ALU.mult, r=[PS_t[src_bank]] + F_t(3), w=[H_t[11]])
                bf = nb()
                mm(bank(bf)[0:64, :], foldb[:], Tt, True, True, r=[H_t[11], c_t], w=[PS_t[bf]], sig=True)
                evac_copy(dst[0:64, tg * 512:(tg + 1) * 512], bank(bf)[0:64, :], r=[PS_t[bf]], wp=[dst_tok])

            def lowrank_group(wv, wt, ncb, g_t, gcol0, outs, with_kr):
                for tg in range(4):
                    for cb in range(ncb):
                        for kc in range(16):
                            mm(bank(cb), wv[:, kc, cb * 128:(cb + 1) * 128], A3[:, kc, tg * 512:(tg + 1) * 512],
                               kc == 0, kc == 15, r=[wt, A_t[kc][tg]], w=[PS_t[cb]], sig=(kc == 15))
                    if with_kr:
                        for kc in range(16):
                            mm(bank(2), wv[:, kc, 256:384], A3[:, kc, tg * 512:(tg + 1) * 512],
                               kc == 0, kc == 15, r=[wt, A_t[kc][tg]], w=[PS_t[2]], sig=(kc == 15))
                    sbk = 4 + (tg % 2)
                    for cb in range(ncb):
                        sq = sq_h[cb % 2][:, (cb // 2 % 2) * 512:(cb // 2 % 2) * 512 + 512]
                        sq_tok = H_t[8 + cb % 2]
                        act(sq, bank(cb), AF.Square, r=[PS_t[cb]], w=[sq_tok])
                        mm(bank(sbk), onesf[:], sq, cb == 0, cb == ncb - 1, r=[sq_tok, c_t], w=[PS_t[sbk]], sig=True)
                    rst = rst_h[:, 0:512]
                    act(rst, bank(sbk), AF.Sqrt, r=[PS_t[sbk]], w=[H_t[11]], bias=EPS, scale=1.0 / (ncb * 128))
                    recip(rst, rst, r=[H_t[11]], w=[H_t[11]])
                    for cb in range(ncb):
                        o, ot = outs[cb]
                        stt("dve", o[:, tg * 512:(tg + 1) * 512], bank(cb), g_t[:, gcol0 + cb:gcol0 + cb + 1], rst, ALU.mult, ALU.mult,
                            r=[PS_t[cb], H_t[11], c_t], wp=[ot])
                    if with_kr:
                        rope_fold(2, tg, KR, H_t[KRs])

            wv, wt = wload(w_in[:, 0:512], 16, 512)
            wv1, wt1 = wload(w_in[:, 512:832], 16, 320, bufcols=384)
            lowrank_group(wv, wt, 4, gq_t, i * 4, [(cqn[k], H_t[k]) for k in range(4)], False)
            chk("g0")
            ts("dve", wv1[:, :, 320:352], wv1[:, :, 288:320], -1.0, None, ALU.mult, None, r=[wt1], wp=[wt1])
            cp("dve", wv1[:, :, 352:384], wv1[:, :, 256:288], r=[wt1], wp=[wt1])
            lowrank_group(wv1, wt1, 2, gkv_t, i * 2, [(ckvn[k], H_t[4 + k]) for k in range(2)], True)

            chk("g1")
            if mods:
                mods.pop(0)()
            wkv, wkt = wload(a_w_ukv[i], 2, 2048)
            wkv4 = wkv.rearrange("p k (h c) -> p k h c", c=256)
            def wq_view(i_):
                return WB[i_][:, 0:8192].rearrange("p (k h c) -> p k h c", k=4, c=256)

            def wq_issue(i_):
                for kc in range(4):
                    dma("pool", wq_view(i_)[:, kc, :, 0:192], a_w_uq[i, kc * 128:(kc + 1) * 128, :].rearrange("p (h c) -> p h c", c=192),
                        w=[WB_t[i_]] if kc == 0 else [], wp=[] if kc == 0 else [WB_t[i_]])

            iq = wload_generic(wq_issue)
            wqt = WB_t[iq]
            wq4 = wq_view(iq)
            ts("dve", wq4[:, :, :, 192:224], wq4[:, :, :, 160:192], -1.0, None, ALU.mult, None, r=[wqt], wp=[wqt])
            cp("dve", wq4[:, :, :, 224:256], wq4[:, :, :, 128:160], r=[wqt], wp=[wqt])

            def rhs_ckv(kc, tg):
                return ckvn[kc][:, tg * 512:(tg + 1) * 512], [H_t[4 + kc]]

            eh = [12, 13, 14, 15]
            ek = 0
            for h in range(8):
                b0 = nfm()
                fm_block(wkv, wkt, slice(h * 256, h * 256 + 128), 2, rhs_ckv, b0)
                hi = eh[ek % 4]
                ek += 1
                for tg in range(4):
                    evac_copy(H(hi)[:, tg * 512:(tg + 1) * 512], bank(b0 + tg), r=[PS_t[b0 + tg]], wp=[H_t[hi]])
                dma("sp", KN[h], H(hi), r=[H_t[hi]], w=[KN_t[h]])
            for t_ in range(16):
                for half in range(2):
                    b = tm_tile(t_, lambda kc: (wkv4[:, kc, 4 * half:4 * half + 4, 128:256], [wkt]), 2,
                                lambda kc, t2: (ckvn[kc][:, t2 * 128:(t2 + 1) * 128], [H_t[4 + kc]]))
                    hi = eh[ek % 4]
                    ek += 1
                    o = H(hi)[:, 0:512]
                    evac_copy(o, bank(b), r=[PS_t[b]], w=[H_t[hi]])
                    dma("sp", VA[t_ * 128:(t_ + 1) * 128, half * 512:(half + 1) * 512], o, r=[H_t[hi]], wp=[VA_t])
            chk("kv")
            for h in range(8):
                b0 = nfm()
                for kc in range(4):
                    for tg in range(4):
                        mm(bank(b0 + tg), wq4[:, kc, h, 0:128], cqn[kc][:, tg * 512:(tg + 1) * 512], kc == 0, kc == 3,
                           r=[wqt, H_t[kc]], w=[PS_t[b0 + tg]], sig=(kc == 3))
                hi = eh[ek % 4]
                ek += 1
                for tg in range(4):
                    evac_copy(H(hi)[:, tg * 512:(tg + 1) * 512], bank(b0 + tg), r=[PS_t[b0 + tg]], wp=[H_t[hi]])
                dma("sp", QN[h], H(hi), r=[H_t[hi]], w=[QN_t[h]])
                hi = eh[ek % 4]
                ek += 1
                for tg in range(4):
                    br = nb()
                    for kc in range(4):
                        mm(bank(br), wq4[:, kc, h, 128:256], cqn[kc][:, tg * 512:(tg + 1) * 512],
                           kc == 0, kc == 3, r=[wqt, H_t[kc]], w=[PS_t[br]], sig=(kc == 3))
                    rope_fold(br, tg, H(hi), H_t[hi])
                dma("sp", QR[h], H(hi)[0:64, :], r=[H_t[hi]], w=[QR_t[h]])

            chk("q")
            def fm_cols(c0, nblk, dst, dst_t, blk0, silu):
                nonlocal ek
                done = 0
                while done < nblk:
                    n = min(4, nblk - done)
                    if mods:
                        mods.pop(0)()
                    wv_, wt_ = wload(w_in[:, c0 + done * 128:c0 + (done + n) * 128], 16, n * 128)
                    for cb in range(n):
                        hi = eh[ek % 4]
                        ek += 1
                        blk = blk0 + done + cb
                        fm_to_dram(wv_, wt_, slice(cb * 128, (cb + 1) * 128), dst[blk], dst_t[blk], hi, silu=silu)
                    done += n

            fm_cols(832, 8, BQ, BQ_t, 0, False)
            fm_cols(1856, 8, BK, BK_t, 0, False)
            for cg in range(2):
                if mods:
                    mods.pop(0)()
                wv_, wt_ = wload(w_in[:, 2880 + cg * 512:2880 + (cg + 1) * 512], 16, 512)
                for t_ in range(16):
                    b = tm_tile(t_, lambda kc: (wv_[:, kc, :], [wt_]), 16, lhs_A)
                    hi = eh[ek % 4]
                    ek += 1
                    o = H(hi)[:, 0:512]
                    evac_copy(o, bank(b), r=[PS_t[b]], w=[H_t[hi]])
                    dma("sp", BV[t_ * 128:(t_ + 1) * 128, cg * 512:(cg + 1) * 512], o, r=[H_t[hi]], wp=[BV_t])
            fm_cols(3904, 16, GT, GT_t, 0, True)

            chk("inproj")
            attn(i)

        def attn_head(kind, i, h, par, loads_only=False, compute_only=False):
            base = par * 5
            qs, qrs, ks, vs, gs = base, base + 1, base + 2, base + 3, base + 4
            chunk = h if kind == "a" else 8 + h
            if not compute_only:
                if kind == "a":
                    dma("sp", H(qs), QN[h], r=[QN_t[h]], w=[H_t[qs]])
                    dma("sp", H(qrs)[0:64, :], QR[h], r=[QR_t[h]], w=[H_t[qrs]])
                    dma("sp", H(ks), KN[h], r=[KN_t[h]], w=[H_t[ks]])
                    dma("sp", H(vs).rearrange("p (t d) -> p t d", d=128), VA[:, h * 128:(h + 1) * 128].rearrange("(t p) d -> p t d", p=128),
                        r=[VA_t], w=[H_t[vs]])
                else:
                    dma("sp", H(qs), BQ[h], r=[BQ_t[h]], w=[H_t[qs]])
                    dma("sp", H(ks), BK[h], r=[BK_t[h]], w=[H_t[ks]])
                    dma("sp", H(vs).rearrange("p (t d) -> p t d", d=128), BV[:, h * 128:(h + 1) * 128].rearrange("(t p) d -> p t d", p=128),
                        r=[BV_t], w=[H_t[vs]])
                    dma("sp", bt4[par][:], tb4[i, h], w=[bt_t[par]])
                    dma("sp", bt3[par][:], tb3[i, h], wp=[bt_t[par]])
                dma("sp", H(gs), GT[chunk], r=[GT_t[chunk]], w=[H_t[gs]])
            if loads_only:
                return
            q_, k_, v_, g_ = H(qs), H(ks), H(vs).rearrange("p (t d) -> p t d", d=128), H(gs)
            qr_ = H(qrs)
            KR = H(10)
            PT_s = [11, 12]
            tmp_s = [13, 14]
            scale = (192.0 ** -0.5) if kind == "a" else (128.0 ** -0.5)
            cb_ = crep_t[:, i * 8 + h:i * 8 + h + 1]
            def group_tiles(g):
                tiles = []
                if kind == "a":
                    for kt in range(4 * g + 4):
                        c0 = 0 if kt < 4 * g else (kt - 4 * g) * 128
                        tiles.append((kt, c0, 512 - c0, "diag" if kt >= 4 * g else "plain"))
                else:
                    for u in [4, 5, 6, 7, 0, 1, 2, 3]:
                        t_ = 4 * g - 4 + u
                        if t_ < 0:
                            continue
                        if u < 4:
                            tiles.append((t_, 0, 128 * (u + 1), "u%d" % u))
                        else:
                            tiles.append((t_, 128 * (u - 4), 512 - 128 * (u - 4), "u%d" % u))
                return tiles

            jobs = []
            for g in range(4):
                tl = group_tiles(g)
                for idx, tile in enumerate(tl):
                    jobs.append((g, idx, len(tl), tile))
            ptl = {}

            def s_stage(n):
                g, idx, nt, (kt, c0, N, kindt) = jobs[n]
                k_i = att["pt"] % 8
                att["pt"] += 1
                sbk = 4 + att["s"] % 4
                att["s"] += 1
                pt_tok = PT_t[k_i]
                pt = H(PT_s[k_i // 4])[:, (k_i % 4) * 512:(k_i % 4) * 512 + 512]
                q0 = g * 512 + c0
                if kind == "a":
                    mm(bank(sbk)[:, 0:N], k_[:, kt * 128:(kt + 1) * 128], q_[:, q0:q0 + N], True, False,
                       r=[H_t[ks], H_t[qs]], w=[PS_t[sbk]], sig=False)
                    mm(bank(sbk)[:, 0:N], KR[0:64, kt * 128:(kt + 1) * 128], qr_[0:64, q0:q0 + N], False, True,
                       r=[H_t[10], H_t[qrs]], w=[PS_t[sbk]], sig=True)
                    act(pt[:, 0:N], bank(sbk)[:, 0:N], AF.Exp, r=[PS_t[sbk]], w=[pt_tok], scale=scale)
                    if kindt == "diag":
                        mset("dve", pt[64:128, 0:64], 0.0, w=[pt_tok])
                else:
                    mm(bank(sbk)[:, 0:N], k_[:, kt * 128:(kt + 1) * 128], q_[:, q0:q0 + N], True, True,
                       r=[H_t[ks], H_t[qs]], w=[PS_t[sbk]], sig=True)
                    u = int(kindt[1:])
                    if u >= 3:
                        nbias = min(256, N) if u >= 4 else 128
                        btile = bt4[par] if u >= 4 else bt3[par]
                        ti = att["tmp"] % 2
                        att["tmp"] += 1
                        tmp = H32(tmp_s[ti])[:, 0:nbias]
                        stt("dve", tmp, bank(sbk)[:, 0:nbias], scale, btile[:, 0:nbias], ALU.mult, ALU.add,
                            r=[PS_t[sbk], bt_t[par]], w=[H_t[tmp_s[ti]]])
                        act(pt[:, 0:nbias], tmp, AF.Exp, r=[H_t[tmp_s[ti]]], w=[pt_tok])
                        if N > nbias:
                            act(pt[:, nbias:N], bank(sbk)[:, nbias:N], AF.Exp, r=[PS_t[sbk], c_t], wp=[pt_tok], scale=scale, bias=cb_)
                    else:
                        act(pt[:, 0:N], bank(sbk)[:, 0:N], AF.Exp, r=[PS_t[sbk], c_t], w=[pt_tok], scale=scale, bias=cb_)
                    if u < 4:
                        mset("dve", pt[0:64, N - 64:N], 0.0, w=[pt_tok])
                ptl[n] = (pt, pt_tok)

            def pv_stage(n):
                g, idx, nt, (kt, c0, N, kindt) = jobs[n]
                ob = g % 2
                sb_ = 2 + g % 2
                pt, pt_tok = ptl.pop(n)
                mm(bank(ob)[:, c0:c0 + N], v_[:, kt, :], pt[:, 0:N], idx == 0, idx == nt - 1,
                   r=[H_t[vs], pt_tok], w=[PS_t[ob]], sig=False)
                mm(bank(sb_)[:, c0:c0 + N], onesb[:], pt[:, 0:N], idx == 0, idx == nt - 1,
                   r=[c_t, pt_tok], w=[PS_t[sb_]], sig=True)
                if idx == nt - 1:
                    ti = att["tmp"] % 2
                    att["tmp"] += 1
                    rec = H32(tmp_s[ti])[:, 0:512]
                    o32 = H32(tmp_s[ti])[:, 512:1024]
                    act(rec, bank(sb_), AF.Ln, r=[PS_t[sb_]], w=[H_t[tmp_s[ti]]])
                    act(rec, rec, AF.Exp, r=[H_t[tmp_s[ti]]], w=[H_t[tmp_s[ti]]], scale=-1.0)
                    tt("dve", o32, bank(ob), rec, ALU.mult, r=[PS_t[ob], H_t[tmp_s[ti]]], w=[H_t[tmp_s[ti]]])
                    tt("pool", A3[:, chunk, g * 512:(g + 1) * 512], o32, g_[:, g * 512:(g + 1) * 512], ALU.mult,
                       r=[H_t[tmp_s[ti]], H_t[gs]], w=[A_t[chunk][g]])

            LA = 3
            for n in range(len(jobs) + LA):
                if n < len(jobs):
                    s_stage(n)
                if n >= LA:
                    pv_stage(n - LA)

        att = {"pt": 0, "s": 0, "tmp": 0}
        PT_t = [Tok("PT%d" % k) for k in range(8)]

        def tok_split(parent, children):
            for ch in children:
                ch.w = dict(parent.w)
                ch.r = dict(parent.r)
                ch.pr = dict(parent.pr)
                ch.closed = parent.closed

        def tok_join(parent, children):
            w, r, pr = {}, {}, {}
            for ch in children:
                _merge(w, ch.w)
                _merge(r, ch.r)
                _merge(r, ch.w)
                _merge(pr, ch.pr)
            parent.w, parent.r, parent.pr, parent.closed = w, r, pr, True

        def attn(i):
            tok_split(H_t[11], PT_t[0:4])
            tok_split(H_t[12], PT_t[4:8])
            attn_inner(i)
            tok_join(H_t[11], PT_t[0:4])
            tok_join(H_t[12], PT_t[4:8])

        def attn_inner(i):
            heads = [("a", h) for h in range(8)] + [("b", h) for h in range(8)]
            attn_head(heads[0][0], i, heads[0][1], 0, loads_only=True)
            for n, (kind, h) in enumerate(heads):
                if n + 1 < len(heads):
                    attn_head(heads[n + 1][0], i, heads[n + 1][1], (n + 1) % 2, loads_only=True)
                attn_head(kind, i, h, n % 2, compute_only=True)

        def odd_layer(i, mods):
            w_in = sg_w_in[i]
            eh = [12, 13, 14, 15]
            ek = 0
            for cg in range(4):
                if mods:
                    mods.pop(0)()
                wu, wut = wload(w_in[:, cg * 512:(cg + 1) * 512], 16, 512)
                wg, wgt = wload(w_in[:, 4096 + cg * 512:4096 + (cg + 1) * 512], 16, 512)
                for cb in range(4):
                    fc = cg * 4 + cb
                    hu = eh[ek % 4]
                    hg = eh[(ek + 1) % 4]
                    ek += 2
                    b0 = nfm()
                    fm_block(wu, wut, slice(cb * 128, (cb + 1) * 128), 16, rhs_A, b0)
                    for tg in range(4):
                        cp("dve", H(hu)[:, tg * 512:(tg + 1) * 512], bank(b0 + tg), r=[PS_t[b0 + tg]], wp=[H_t[hu]])
                    b1 = nfm()
                    fm_block(wg, wgt, slice(cb * 128, (cb + 1) * 128), 16, rhs_A, b1)
                    act(H(hg), PS[:, b1 * 512:(b1 + 4) * 512], AF.Silu, r=[PS_t[b1 + k] for k in range(4)], w=[H_t[hg]])
                    tt("pool", H(hu), H(hu), H(hg), ALU.mult, r=[H_t[hu], H_t[hg]], w=[H_t[hu]])
                    dma("sp", UG[fc], H(hu), r=[H_t[hu]], w=[UG_t[fc]])
            for cg in range(4):
                if mods:
                    mods.pop(0)()
                wv_, wt_ = wload(w_in[:, 2048 + cg * 512:2048 + (cg + 1) * 512], 16, 512)
                for t_ in range(16):
                    b = tm_tile(t_, lambda kc: (wv_[:, kc, :], [wt_]), 16, lhs_A)
                    hi = eh[ek % 4]
                    ek += 1
                    o = H32(hi)[:, 0:512]
                    evac_copy(o, bank(b), r=[PS_t[b]], w=[H_t[hi]])
                    dma("sp", VR[t_ * 128:(t_ + 1) * 128, cg * 512:(cg + 1) * 512], o, r=[H_t[hi]], wp=[VR_t[t_]])
            wsb = H(0)[:, 0:1024].rearrange("p (g j) -> p g j", j=128)
            wsT = H(1)[:, 0:1024].rearrange("p (g i) -> p g i", i=128)
            E = Fs(1)
            Ev = E.rearrange("p (c i) -> p c i", i=128)
            dma("pool", wsb, sg_w_s[i].rearrange("g i j -> i g j"), w=[H_t[0]])
            mset("dve", wsb[0:64, :, 64:128], 0.0, wp=[H_t[0]])
            for half in range(2):
                b = nb()
                for j in range(4):
                    g = half * 4 + j
                    mm(bank(b)[:, j * 128:(j + 1) * 128], wsb[:, g, :], identb[:], True, True, r=[H_t[0], c_t], w=[PS_t[b]], sig=(j == 3))
                evac_copy(H(1)[:, half * 512:(half + 1) * 512], bank(b), r=[PS_t[b]], wp=[H_t[1]])
            dma("sp", bsrow[:], sg_b_s[i:i + 1, :], w=[bs_t])
            rs_b = [nb(), nb()]
            bs_b = [nb(), nb()]
            for half in range(2):
                for j in range(4):
                    g = half * 4 + j
                    mm(bank(rs_b[half])[:, j * 128:(j + 1) * 128], onesb[:], wsT[:, g, :], True, True, r=[H_t[1], c_t],
                       w=[PS_t[rs_b[half]]], sig=(j == 3))
                mm(bank(bs_b[half]), onesf[0:1, :], bsrow[0:1, half * 512:(half + 1) * 512], True, True, r=[bs_t, c_t],
                   w=[PS_t[bs_b[half]]], sig=True)
            bsr = H32(4)
            for half in range(2):
                cp("dve", bsr[:, half * 512:(half + 1) * 512], bank(bs_b[half]), r=[PS_t[bs_b[half]]], wp=[H_t[4]])
            for fc in range(16):
                g = fc // 2
                stt("dve", Ev[:, fc, :], bank(rs_b[g // 4])[:, (g % 4) * 128:(g % 4 + 1) * 128], lnb_t[:, i * 16 + fc:i * 16 + fc + 1],
                    bsr[:, g * 128:(g + 1) * 128], ALU.mult, ALU.add, r=[PS_t[rs_b[g // 4]], H_t[4], c_t], wp=F_t(1))

            def load_v(n):
                p = n % 2
                dma("sp", Fs(2 + p), VR[n * 128:(n + 1) * 128, :], r=[VR_t[n]], w=F_t(2 + p))

            def load_ug(n):
                p = n % 2
                dma("sp", H(8 + p).rearrange("p (c t) -> p c t", t=128), UG[:, :, n * 128:(n + 1) * 128].rearrange("c p t -> p c t"),
                    r=UG_t, w=[H_t[8 + p]])

            def s1(n):
                p = n % 2
                vt = Fs(2 + p)
                si = rr["st"] % 8
                rr["st"] += 1
                c = si * 8
                s_ = [st_t[si]]
                act(junk[:], vt, AF.Identity, r=F_t(2 + p), w=s_, accum=st[:, c:c + 1])
                act(junk[:], vt, AF.Square, r=F_t(2 + p), wp=s_, accum=st[:, c + 1:c + 2])
                ts("dve", st[:, c + 2:c + 3], st[:, c:c + 1], 1.0 / D, None, ALU.mult, None, r=s_, w=s_)
                tt("dve", st[:, c + 3:c + 4], st[:, c + 2:c + 3], st[:, c + 2:c + 3], ALU.mult, r=s_, w=s_)
                stt("dve", st[:, c + 4:c + 5], st[:, c + 1:c + 2], 1.0 / D, st[:, c + 3:c + 4], ALU.mult, ALU.subtract, r=s_, w=s_)
                act(st[:, c + 5:c + 6], st[:, c + 4:c + 5], AF.Sqrt, r=s_, w=s_, bias=EPS, scale=1.0)
                recip(st[:, c + 6:c + 7], st[:, c + 5:c + 6], r=s_, w=s_)
                stt("dve", st[:, c + 7:c + 8], st[:, c + 2:c + 3], -1.0, st[:, c + 6:c + 7], ALU.mult, ALU.mult, r=s_, w=s_)
                vn = H(10 + p)
                act(vn, vt, AF.Identity, r=F_t(2 + p) + s_, w=[H_t[10 + p]], bias=st[:, c + 7:c + 8], scale=st[:, c + 6:c + 7])

            def s2(n):
                p = n % 2
                vn = H(10 + p)
                ug = H(8 + p).rearrange("p (c t) -> p c t", t=128)
                for q in range(4):
                    b = nb()
                    for j in range(4):
                        fc = 4 * q + j
                        mm(bank(b)[:, j * 128:(j + 1) * 128], vn[:, fc * 128:(fc + 1) * 128], wsT[:, fc // 2, :], True, True,
                           r=[H_t[10 + p], H_t[1]], w=[PS_t[b]], sig=(j == 3))
                    ti = att["tmp"] % 2
                    att["tmp"] += 1
                    tmp = H32(13 + ti)[:, 0:512]
                    for j in range(4):
                        fc = 4 * q + j
                        stt("dve", tmp[:, j * 128:(j + 1) * 128], bank(b)[:, j * 128:(j + 1) * 128], lng_t[:, i * 16 + fc:i * 16 + fc + 1],
                            Ev[:, fc, :], ALU.mult, ALU.add, r=[PS_t[b], c_t] + F_t(1), wp=[H_t[13 + ti]])
                    tt("pool", A3[:, 4 * q:4 * q + 4, n * 128:(n + 1) * 128], tmp.rearrange("p (c t) -> p c t", t=128), ug[:, 4 * q:4 * q + 4, :], ALU.mult,
                       r=[H_t[13 + ti], H_t[8 + p]], wp=[A_t[4 * q + j][n // 4] for j in range(4)])

            load_v(0)
            load_v(1)
            load_ug(0)
            load_ug(1)
            s1(0)
            for n in range(16):
                if n + 1 < 16:
                    s1(n + 1)
                s2(n)
                if n + 2 < 16:
                    load_v(n + 2)
                    load_ug(n + 2)

        def whole():
            consts()
            chk("const")
            for cg in range(8):
                mod_group(0, cg)
                chk("mod%d" % cg)
            mod_flush()
            chk("mod")
            for _ in norm_gen(0, x_in, XS[0], False, True):
                pass
            chk("norm0")
            Xcur = x_in
            for l in range(depth):
                mods = []
                if l == 0:
                    mods = [(lambda cg=cg: mod_group(0, cg)) for cg in range(8, 12)]
                if l + 1 < depth:
                    mods += [(lambda l1=l + 1, cg=cg: mod_group(l1, cg)) for cg in range(12)]
                i = l // 2
                if l % 2 == 0:
                    even_layer(i, mods)
                    w_out_l = ab_w_out[i]
                else:
                    odd_layer(i, mods)
                    w_out_l = sg_w_out[i]
                chk("mix%d" % l)
                Xnext = out if l + 1 == depth else XS[(l + 1) % 2]
                ngen = norm_gen(l + 1, Xcur, Xnext, True, l + 1 < depth)
                out_phase(w_out_l, mods, ngen)
                chk("out%d" % l)
                while mods:
                    mods.pop(0)()
                mod_flush()
                for _ in ngen:
                    pass
                Xcur = Xnext

        def reset_all():
            P.reset()
            for t in Tok.ALL:
                t.reset()
            for d_ in (rr, ev, att):
                for k_ in d_:
                    d_[k_] = 0
            plan["k"] = 0
            plan["issued"] = 0

        for pass_ in range(2):
            try:
                whole()
            except _Stop:
                pass
            if pass_ == 0:
                plan["replay"] = plan["specs"]
                reset_all()
        fin = {}
        for t in X_t[id(out)]:
            _merge(fin, t.w)
        P.stream["sp"].append((fin, None, None))

        with nc.Block() as block:
            @block.tensor
            def _(e):
                P.emit("pe", e)

            @block.scalar
            def _(e):
                P.emit("act", e)

            @block.vector
            def _(e):
                P.emit("dve", e)

            @block.gpsimd
            def _(e):
                P.emit("pool", e)

            @block.sync
            def _(e):
                P.emit("sp", e)
        build.stats = {e: len(s) for e, s in P.stream.items()}
    return nc


def _host_inputs(inputs, b):
    f = np.float32
    x = np.ascontiguousarray(inputs["x"][b], dtype=f)
    c = np.asarray(inputs["c"][b], dtype=f)

    def col(v):
        return np.ascontiguousarray(np.asarray(v, dtype=f).reshape(-1, 128).T)

    def cols(m):
        return np.ascontiguousarray(np.concatenate([col(m[l]) for l in range(m.shape[0])], axis=1))

    tab = np.asarray(inputs["b_rel_bias"], dtype=f)
    kk = np.arange(64)[:, None]
    cc = np.arange(320)[None, :]
    idx = np.minimum(cc - kk + 128, 256)
    TB = tab[:, :, idx]
    tb4 = np.full((2, 8, 128, 256), NEG, dtype=f)
    tb4[:, :, 0:64, :] = TB[:, :, :, 0:256]
    tb4[:, :, 64:128, 64:256] = TB[:, :, :, 0:192]
    tb3 = np.empty((2, 8, 128, 128), dtype=f)
    tb3[:, :, 0:64, :] = TB[:, :, :, 128:256]
    tb3[:, :, 64:128, :] = TB[:, :, :, 64:192]
    crep = np.ascontiguousarray(np.broadcast_to(tab[:, :, 256].reshape(1, 16), (128, 16)), dtype=f)
    half = 32
    freqs = (10000.0 ** (-np.arange(half, dtype=f) / f(half))).astype(f)
    ang = (np.arange(S, dtype=f)[None, :] * freqs[:, None]).astype(f)
    cs = np.concatenate([np.cos(ang), np.cos(ang), np.sin(ang), np.sin(ang)], 0).astype(f)
    d = {
        "x": x, "ccol": col(c),
        "w_mod": inputs["w_mod"], "b_mod": inputs["b_mod"],
        "gpre_col": cols(np.asarray(inputs["g_pre"])), "g_post": inputs["g_post"],
        "ab_w_in": inputs["ab_w_in"], "gq_col": cols(np.asarray(inputs["a_g_q"])), "a_w_uq": inputs["a_w_uq"],
        "gkv_col": cols(np.asarray(inputs["a_g_kv"])), "a_w_ukv": inputs["a_w_ukv"],
        "tb4": tb4, "tb3": tb3, "crep": crep, "ab_w_out": inputs["ab_w_out"],
        "sg_w_in": inputs["sg_w_in"], "lng_col": cols(np.asarray(inputs["sg_ln_g"])), "lnb_col": cols(np.asarray(inputs["sg_ln_b"])),
        "sg_w_s": inputs["sg_w_s"], "sg_b_s": np.asarray(inputs["sg_b_s"], dtype=f).reshape(2, 1024), "sg_w_out": inputs["sg_w_out"],
        "ident": np.eye(128, dtype=f), "ropecs": cs,
    }
    return {k: np.ascontiguousarray(np.asarray(v, dtype=f)) for k, v in d.items()}


_NC = {}


def kernel(**inputs):
    inputs = {k: np.asarray(v) for k, v in inputs.items()}
    if DEPTH not in _NC:
        _NC[DEPTH] = build(DEPTH)
    nc = _NC[DEPTH]
    shared = None
    in_maps = []
    for b in range(8):
        m = _host_inputs(inputs, b) if shared is None else dict(shared)
        if shared is None:
            shared = m
        else:
            m["x"] = np.ascontiguousarray(inputs["x"][b], dtype=np.float32)
            m["ccol"] = np.ascontiguousarray(np.asarray(inputs["c"][b], dtype=np.float32).reshape(-1, 128).T)
        in_maps.append(m)
    res = run_bass_kernel_spmd(nc, in_maps, core_ids=list(range(8)))
    return np.stack([np.asarray(r["out"]) for r in res.results], axis=0).astype(np.float32)
```

```python
import numpy as np
from contextlib import ExitStack
import concourse.bass as bass
import concourse.mybir as mybir
from concourse.bass_utils import run_bass_kernel_spmd

F32 = mybir.dt.float32
BF16 = mybir.dt.bfloat16
AF = mybir.ActivationFunctionType
ALU = mybir.AluOpType

S = 2048
D = 2048
DEPTH = 4
EPS = 1e-6
NEG = -30000.0
CE = ("pe", "act", "dve", "pool")


class Tok:
    __slots__ = ("name", "w", "r", "closed", "pr", "psum")

    ALL = []

    def __init__(self, name, psum=False):
        self.name = name
        self.psum = psum
        self.reset()
        Tok.ALL.append(self)

    def reset(self):
        self.w = {}
        self.r = {}
        self.pr = {}
        self.closed = False


def _merge(d, s):
    for k, v in s.items():
        if d.get(k, 0) < v:
            d[k] = v


class Prog:
    NS = 8

    def __init__(self, nc, es):
        self.nc = nc
        self.sem = {e: es.enter_context(nc.semaphore("s_" + e)) for e in CE}
        self.cnt = {e: 0 for e in CE}
        self.dq = ("sp", "pool", "act")
        self.dsem = {q: [es.enter_context(nc.semaphore("d_%s%d" % (q, i))) for i in range(self.NS)] for q in self.dq}
        self.reset()

    def reset(self):
        self.cnt = {e: 0 for e in CE}
        self.dval = {q: [0] * self.NS for q in self.dq}
        self.dnext = {q: 0 for q in self.dq}
        self.stream = {e: [] for e in ("pe", "act", "dve", "pool", "sp")}
        self.pend = {e: ([], [], []) for e in CE}
        self.pendset = {e: set() for e in CE}
        self.nops = 0

    def semh(self, key):
        if isinstance(key, tuple):
            return self.dsem[key[0]][key[1]]
        return self.sem[key]

    def _publish(self, r, w, wp, key, val):
        for t in r:
            if t.r.get(key, 0) < val:
                t.r[key] = val
            t.closed = True
        for t in w:
            t.w = {key: val}
            t.r = {}
            t.pr = {key: val}
            t.closed = False
        for t in wp:
            if t.closed:
                t.pr = t.r
                t.w = {key: val}
                t.r = {}
                t.closed = False
            else:
                if t.w.get(key, 0) < val:
                    t.w[key] = val

    def op(self, eng, fn, r=(), w=(), wp=(), sig=True, dma=False):
        self.nops += 1
        waits = {}
        if eng != "pe":
            xs = [t for t in (*r, *w, *wp) if t.psum]
            if xs:
                r = [t for t in r if not t.psum]
                wp = [t for t in wp if not t.psum]
                w = [t for t in w if not t.psum] + xs
                for t in xs:
                    for d_ in (t.w, t.r, t.pr):
                        for k, v in d_.items():
                            if k != eng and waits.get(k, 0) < v:
                                waits[k] = v
                own = waits.get(eng)
            else:
                own = None
        else:
            xs = ()
            own = None
        for t in (*r, *w, *wp):
            for e2, ps in self.pendset.items():
                if e2 != eng and t in ps:
                    raise RuntimeError("token %s pending on %s touched by %s" % (t.name, e2, eng))
        for t in r:
            _merge(waits, t.w)
        for t in w:
            _merge(waits, t.w)
            _merge(waits, t.r)
        for t in wp:
            _merge(waits, t.r)
            if not t.closed:
                _merge(waits, t.pr)
        if eng == "pe":
            waits.pop("pe", None)
        elif xs:
            ownv = 0
            for t in r:
                ownv = max(ownv, t.w.get(eng, 0))
            for t in w:
                if not t.psum:
                    ownv = max(ownv, t.w.get(eng, 0), t.r.get(eng, 0))
            for t in wp:
                ownv = max(ownv, t.r.get(eng, 0))
                if not t.closed:
                    ownv = max(ownv, t.pr.get(eng, 0))
            if ownv:
                waits[eng] = ownv
            else:
                waits.pop(eng, None)
        if dma:
            q = eng
            slot = self.dnext[q] % self.NS
            self.dnext[q] += 1
            key = (q, slot)
            if self.dval[q][slot] > 0:
                waits[key] = max(waits.get(key, 0), self.dval[q][slot])
            self.dval[q][slot] += 16
            self._publish(r, w, wp, key, self.dval[q][slot])
            inc = (self.dsem[q][slot], 16)
        else:
            pr, pw, pwp = self.pend[eng]
            pr.extend(r)
            pw.extend(w)
            pwp.extend(wp)
            if sig:
                self.cnt[eng] += 1
                self._publish(pr, pw, pwp, eng, self.cnt[eng])
                self.pend[eng] = ([], [], [])
                self.pendset[eng] = set()
                inc = (self.sem[eng], 1)
            else:
                assert eng == "pe"
                self.pendset[eng].update(r)
                self.pendset[eng].update(w)
                self.pendset[eng].update(wp)
                inc = None
        self.stream[eng].append((waits, fn, inc))

    def emit(self, eng, e):
        seen = {}
        for waits, fn, inc in self.stream[eng]:
            for k, v in waits.items():
                if seen.get(k, 0) >= v:
                    continue
                seen[k] = v
                e.wait_ge(self.semh(k), v)
            if fn is not None:
                ins = fn(e)
                if inc is not None:
                    ins.then_inc(inc[0], inc[1])


class _Stop(Exception):
    pass


def build(depth=DEPTH, stop=None):
    Tok.ALL = []
    nc = bass.Bass("TRN2", target_bir_lowering=False)

    def din(name, shape, dt=F32):
        return nc.dram_tensor(name, list(shape), dt, kind="ExternalInput").ap()

    def dscr(name, shape, dt):
        return nc.dram_tensor(name, list(shape), dt).ap()

    x_in = din("x", [S, D])
    ccol = din("ccol", [128, 16])
    w_mod = din("w_mod", [4, D, 6144])
    b_mod = din("b_mod", [4, 6144])
    gpre_col = din("gpre_col", [128, 64])
    g_post = din("g_post", [4, D])
    ab_w_in = din("ab_w_in", [2, D, 5952])
    gq_col = din("gq_col", [128, 8])
    a_w_uq = din("a_w_uq", [2, 512, 1536])
    gkv_col = din("gkv_col", [128, 4])
    a_w_ukv = din("a_w_ukv", [2, 256, 2048])
    tb4 = din("tb4", [2, 8, 128, 256])
    tb3 = din("tb3", [2, 8, 128, 128])
    crep = din("crep", [128, 16])
    ab_w_out = din("ab_w_out", [2, D, D])
    sg_w_in = din("sg_w_in", [2, D, 6144])
    lng_col = din("lng_col", [128, 32])
    lnb_col = din("lnb_col", [128, 32])
    sg_w_s = din("sg_w_s", [2, 8, 128, 128])
    sg_b_s = din("sg_b_s", [2, 1024])
    sg_w_out = din("sg_w_out", [2, D, D])
    ident_d = din("ident", [128, 128])
    ropecs = din("ropecs", [128, S])
    out = nc.dram_tensor("out", [S, D], F32, kind="ExternalOutput").ap()

    XS = [dscr("XA", [S, D], F32), dscr("XB", [S, D], F32)]
    Y2 = dscr("Y2", [S, D], F32)
    G2ROW = dscr("G2ROW", [4, D], F32)
    QN = dscr("QN", [8, 128, S], BF16)
    QR = dscr("QR", [8, 64, S], BF16)
    KN = dscr("KN", [8, 128, S], BF16)
    VA = dscr("VA", [S, 1024], BF16)
    BQ = dscr("BQ", [8, 128, S], BF16)
    BK = dscr("BK", [8, 128, S], BF16)
    BV = dscr("BV", [S, 1024], BF16)
    GT = dscr("GT", [16, 128, S], BF16)
    UG = dscr("UG", [16, 128, S], BF16)
    VR = dscr("VR", [S, D], F32)

    es = ExitStack()
    with es:
        def sb(name, shape, dt):
            return es.enter_context(nc.sbuf_tensor(name, list(shape), dt))

        A = sb("A", [128, 16 * 2048], BF16)
        A3 = A[:].rearrange("p (k t) -> p k t", t=2048)
        WB = [sb("WB%d" % i, [128, 8192], BF16) for i in range(3)]
        SL = [sb("SL%d" % i, [128, 4096], BF16) for i in range(8)]
        PS = es.enter_context(nc.psum_tensor("PS", [128, 4096], F32))
        identb = sb("identb", [128, 128], BF16)
        onesb = sb("onesb", [128, 128], BF16)
        onesf = sb("onesf", [128, 128], F32)
        csT = sb("csT", [128, 16], BF16)
        ccol_t = sb("ccol_t", [128, 16], F32)
        gpre_t = sb("gpre_t", [128, 64], F32)
        gq_t = sb("gq_t", [128, 8], F32)
        gkv_t = sb("gkv_t", [128, 4], F32)
        crep_t = sb("crep_t", [128, 16], F32)
        lng_t = sb("lng_t", [128, 32], F32)
        lnb_t = sb("lnb_t", [128, 32], F32)
        Acol = sb("Acol", [128, 64], F32)
        Bcol = sb("Bcol", [128, 64], F32)
        st = sb("st", [128, 64], F32)
        rows = [sb("row%d" % i, [1, 512], F32) for i in range(2)]
        foldb = sb("foldb", [128, 64], BF16)
        brow = [sb("brow%d" % i, [1, 512], F32) for i in range(2)]
        grow = [sb("grow%d" % i, [1, 512], F32) for i in range(2)]
        identf = sb("identf", [128, 128], F32)
        diag = [sb("diag%d" % i, [128, 128], F32) for i in range(2)]
        junk = sb("junk", [128, 2048], BF16)
        bt4 = [sb("bt4_%d" % i, [128, 256], F32) for i in range(2)]
        bt3 = [sb("bt3_%d" % i, [128, 128], F32) for i in range(2)]
        bsrow = sb("bsrow", [1, 1024], F32)

        P = Prog(nc, es)

        A_t = [[Tok("A%d_%d" % (k, g)) for g in range(4)] for k in range(16)]
        WB_t = [Tok("WB%d" % i) for i in range(3)]
        H_t = [Tok("H%d" % i) for i in range(16)]
        PS_t = [Tok("PS%d" % i, psum=True) for i in range(8)]
        c_t = Tok("consts")
        st_t = [Tok("st%d" % i) for i in range(8)]
        row_t = [Tok("row%d" % i) for i in range(2)]
        brow_t = [Tok("brow%d" % i) for i in range(2)]
        grow_t = [Tok("grow%d" % i) for i in range(2)]
        diag_t = [Tok("diag%d" % i) for i in range(2)]
        bt_t = [Tok("bt%d" % i) for i in range(2)]
        AB_t = [Tok("AB%d" % l) for l in range(4)]
        X_t = {id(XS[0]): [Tok("XA%d" % i) for i in range(16)], id(XS[1]): [Tok("XB%d" % i) for i in range(16)],
               id(out): [Tok("out%d" % i) for i in range(16)], id(x_in): [Tok("xin%d" % i) for i in range(16)]}
        Y2_t = [Tok("Y2_%d" % i) for i in range(16)]
        G2_t = [Tok("G2_%d" % i) for i in range(4)]
        QN_t = [Tok("QN%d" % i) for i in range(8)]
        QR_t = [Tok("QR%d" % i) for i in range(8)]
        KN_t = [Tok("KN%d" % i) for i in range(8)]
        VA_t = Tok("VA")
        BQ_t = [Tok("BQ%d" % i) for i in range(8)]
        BK_t = [Tok("BK%d" % i) for i in range(8)]
        BV_t = Tok("BV")
        GT_t = [Tok("GT%d" % i) for i in range(16)]
        UG_t = [Tok("UG%d" % i) for i in range(16)]
        VR_t = [Tok("VR%d" % i) for i in range(16)]
        bs_t = Tok("bsrow")

        def H(i):
            return SL[i // 2][:, (i % 2) * 2048:(i % 2) * 2048 + 2048]

        def H32(i):
            return H(i).bitcast(F32)

        def Fs(j):
            return SL[j][:].bitcast(F32)

        def F_t(j):
            return [H_t[2 * j], H_t[2 * j + 1]]

        def bank(b):
            return PS[:, b * 512:(b + 1) * 512]

        def mm(o, lhsT, rhs, start, stop, r, w, sig):
            P.op("pe", lambda e: e.matmul(o, lhsT, rhs, start=start, stop=stop), r=r, w=w, sig=sig)

        def act(o, i, func, r, w=(), wp=(), bias=None, scale=None, accum=None):
            kw = {}
            if bias is not None:
                kw["bias"] = bias
            if scale is not None:
                kw["scale"] = scale
            if accum is not None:
                kw["accum_out"] = accum
            P.op("act", lambda e: e.activation(out=o, in_=i, func=func, **kw), r=r, w=w, wp=wp)

        def ts(eng, o, i, s1, s2, op0, op1, r, w=(), wp=()):
            if s2 is None:
                P.op(eng, lambda e: e.tensor_scalar(o, i, s1, None, op0), r=r, w=w, wp=wp)
            else:
                P.op(eng, lambda e: e.tensor_scalar(o, i, s1, s2, op0, op1), r=r, w=w, wp=wp)

        def stt(eng, o, i0, sc, i1, op0, op1, r, w=(), wp=()):
            P.op(eng, lambda e: e.scalar_tensor_tensor(o, i0, sc, i1, op0, op1), r=r, w=w, wp=wp)

        def tt(eng, o, i0, i1, op, r, w=(), wp=()):
            P.op(eng, lambda e: e.tensor_tensor(o, i0, i1, op), r=r, w=w, wp=wp)

        def cp(eng, o, i, r, w=(), wp=()):
            P.op(eng, lambda e: e.tensor_copy(o, i), r=r, w=w, wp=wp)

        def recip(o, i, r, w=(), wp=()):
            P.op("dve", lambda e: e.reciprocal(o, i), r=r, w=w, wp=wp)

        def mset(eng, o, val, w=(), wp=()):
            P.op(eng, lambda e: e.memset(o, val), w=w, wp=wp)

        def dma(q, o, i, r=(), w=(), wp=(), slow=False):
            if slow:
                P.op(q, lambda e: e.dma_start(out=o, in_=i, allow_slow_non_contiguous=True), r=r, w=w, wp=wp, dma=True)
            else:
                P.op(q, lambda e: e.dma_start(out=o, in_=i), r=r, w=w, wp=wp, dma=True)

        def chk(name):
            if stop == name:
                raise _Stop()

        rr = {"bank": 0, "fm": 0, "w": 0, "row": 0, "st": 0}

        def nb():
            b = rr["bank"] % 8
            rr["bank"] += 1
            return b

        def nfm():
            b = 4 * (rr["fm"] % 2)
            rr["fm"] += 1
            return b

        plan = {"specs": [], "replay": None, "k": 0, "issued": 0}

        def wload_generic(issue_fn):
            k = plan["k"]
            plan["k"] += 1
            if plan["replay"] is None:
                plan["specs"].append(issue_fn)
                issue_fn(k % 3)
            else:
                specs = plan["replay"]
                while plan["issued"] <= min(k + 1, len(specs) - 1):
                    j = plan["issued"]
                    tb_ = WB_t[j % 3]
                    assert tb_.closed or not tb_.w, "weight buffer %d reloaded before its consumers were recorded (load %d)" % (j % 3, j)
                    specs[j](j % 3)
                    plan["issued"] += 1
            return k % 3

        def wload(src, nkc, ncols, bufcols=None, col_off=0):
            bufcols = bufcols or ncols

            def view(i):
                return WB[i][:, 0:nkc * bufcols].rearrange("p (k c) -> p k c", c=bufcols)

            def issue(i):
                dma("pool", view(i)[:, :, col_off:col_off + ncols], src.rearrange("(k p) c -> p k c", p=128), w=[WB_t[i]])

            i = wload_generic(issue)
            return view(i), WB_t[i]

        def consts():
            dma("pool", identb[:], ident_d[:, :], w=[c_t])
            dma("sp", identf[:], ident_d[:, :], wp=[c_t])
            mset("dve", onesb[:], 1.0, wp=[c_t])
            mset("dve", onesf[:], 1.0, wp=[c_t])
            dma("sp", ccol_t[:], ccol[:, :], wp=[c_t])
            dma("sp", gpre_t[:], gpre_col[:, :], wp=[c_t])
            dma("sp", gq_t[:], gq_col[:, :], wp=[c_t])
            dma("sp", gkv_t[:], gkv_col[:, :], wp=[c_t])
            dma("sp", crep_t[:], crep[:, :], wp=[c_t])
            dma("sp", lng_t[:], lng_col[:, :], wp=[c_t])
            dma("sp", lnb_t[:], lnb_col[:, :], wp=[c_t])
            act(csT[:], ccol_t[:], AF.Silu, r=[c_t], wp=[c_t])
            tt("dve", foldb[:], identb[:, 0:64], identb[:, 64:128], ALU.add, r=[c_t], wp=[c_t])

        modq = []

        def mod_rows_prefetch(l, cg):
            bi = cg % 2
            dma("sp", brow[bi][:], b_mod[l:l + 1, cg * 512:(cg + 1) * 512], w=[brow_t[bi]])
            if cg >= 8:
                g0 = (cg - 8) * 512
                dma("sp", grow[bi][:], g_post[l:l + 1, g0:g0 + 512], w=[grow_t[bi]])

        def mod_flush():
            while modq:
                modq.pop(0)()

        def mod_group(l, cg):
            wv, wt = wload(w_mod[l, :, cg * 512:(cg + 1) * 512], 16, 512)
            mod_flush()
            if cg == 0:
                mod_rows_prefetch(l, 0)
            if cg + 1 < 12:
                mod_rows_prefetch(l, cg + 1)
            b = nb()
            for kc in range(16):
                mm(bank(b)[0:1, :], csT[:, kc:kc + 1], wv[:, kc, :], kc == 0, kc == 15, r=[wt, c_t], w=[PS_t[b]], sig=(kc == 15))
            bi = cg % 2
            ri = cg % 2
            tt("dve", rows[ri][:], bank(b)[0:1, :], brow[bi][:], ALU.add, r=[PS_t[b], brow_t[bi]], w=[row_t[ri]])
            if cg < 8:
                def fin():
                    b2 = nb()
                    for j in range(4):
                        mm(bank(b2)[:, j:j + 1], rows[ri][0:1, j * 128:(j + 1) * 128], onesf[0:1, 0:1], True, True,
                           r=[row_t[ri], c_t], w=[PS_t[b2]], sig=(j == 3))
                    if cg < 4:
                        c0 = l * 16 + cg * 4
                        cp("dve", Bcol[:, c0:c0 + 4], bank(b2)[:, 0:4], r=[PS_t[b2]], wp=[AB_t[l]])
                    else:
                        c0 = l * 16 + (cg - 4) * 4
                        stt("dve", Acol[:, c0:c0 + 4], bank(b2)[:, 0:4], 1.0, gpre_t[:, c0:c0 + 4], ALU.add, ALU.mult,
                            r=[PS_t[b2], c_t], wp=[AB_t[l]])
                modq.append(fin)
            else:
                g0 = (cg - 8) * 512
                tt("dve", rows[ri][:], rows[ri][:], grow[bi][:], ALU.mult, r=[row_t[ri], grow_t[bi]], w=[row_t[ri]])
                dma("sp", G2ROW[l:l + 1, g0:g0 + 512], rows[ri][:], r=[row_t[ri]], wp=[G2_t[l]])

        def norm_gen(l, Xprev, Xnext, has_y2, make_h):
            G2rep = Fs(5)
            if has_y2:
                dma("sp", G2rep, G2ROW[l - 1:l, :].broadcast_to([128, D]), r=[G2_t[l - 1]], w=F_t(5))

            def ysl(t_):
                return t_ % 2

            def xsl(t_):
                return 2 + t_ % 3

            def loads(t_):
                if has_y2:
                    dma("sp", Fs(ysl(t_)), Y2[t_ * 128:(t_ + 1) * 128, :], r=[Y2_t[t_]], w=F_t(ysl(t_)))
                dma("sp", Fs(xsl(t_)), Xprev[t_ * 128:(t_ + 1) * 128, :], r=[X_t[id(Xprev)][t_]], w=F_t(xsl(t_)))

            cols = {}

            def stage_a(t_):
                y2 = Fs(ysl(t_))
                xt = Fs(xsl(t_))
                yt_, xt_ = F_t(ysl(t_)), F_t(xsl(t_))
                si = rr["st"] % 8
                rr["st"] += 1
                c = si * 8
                cols[t_] = (si, c)
                stt_ = [st_t[si]]
                if has_y2:
                    act(junk[:], y2, AF.Square, r=yt_, w=stt_, accum=st[:, c:c + 1])
                    act(st[:, c + 1:c + 2], st[:, c:c + 1], AF.Sqrt, r=stt_, w=stt_, bias=EPS, scale=1.0 / D)
                    recip(st[:, c + 2:c + 3], st[:, c + 1:c + 2], r=stt_, w=stt_)
                    tt("dve", y2, y2, G2rep, ALU.mult, r=yt_ + F_t(5), w=yt_)
                    stt("dve", xt, y2, st[:, c + 2:c + 3], xt, ALU.mult, ALU.add, r=yt_ + xt_ + stt_, w=xt_)
                    dma("sp", Xnext[t_ * 128:(t_ + 1) * 128, :], xt, r=xt_, w=[X_t[id(Xnext)][t_]])

            def stage_b(t_):
                xt = Fs(xsl(t_))
                xt_ = F_t(xsl(t_))
                si, c = cols[t_]
                stt_ = [st_t[si]]
                act(junk[:], xt, AF.Square, r=xt_, w=stt_, accum=st[:, c + 3:c + 4])
                act(st[:, c + 4:c + 5], st[:, c + 3:c + 4], AF.Sqrt, r=stt_, w=stt_, bias=EPS, scale=1.0 / D)
                recip(st[:, c + 5:c + 6], st[:, c + 4:c + 5], r=stt_, w=stt_)
                dg = diag[t_ % 2]
                ts("dve", dg[:], identf[:], st[:, c + 5:c + 6], None, ALU.mult, None, r=stt_ + [c_t], w=[diag_t[t_ % 2]])

            def stage_b2(t_):
                xt = Fs(xsl(t_))
                xt_ = F_t(xsl(t_))
                dg = diag[t_ % 2]
                for q in range(4):
                    b = nb()
                    for j in range(4):
                        fc = 4 * q + j
                        mm(bank(b)[:, j * 128:(j + 1) * 128], xt[:, fc * 128:(fc + 1) * 128], dg[:], True, True,
                           r=xt_ + [diag_t[t_ % 2]], w=[PS_t[b]], sig=(j == 3))
                    for j in range(4):
                        fc = 4 * q + j
                        o = A3[:, fc, t_ * 128:(t_ + 1) * 128]
                        i_ = bank(b)[:, j * 128:(j + 1) * 128]
                        ac = Acol[:, l * 16 + fc:l * 16 + fc + 1]
                        bc = Bcol[:, l * 16 + fc:l * 16 + fc + 1]
                        if q % 2 == 0:
                            act(o, i_, AF.Identity, r=[PS_t[b], AB_t[l]], wp=[A_t[fc][t_ // 4]], bias=bc, scale=ac)
                        else:
                            ts("dve", o, i_, ac, bc, ALU.mult, ALU.add, r=[PS_t[b], AB_t[l]], wp=[A_t[fc][t_ // 4]])

            loads(0)
            loads(1)
            stage_a(0)
            for t_ in range(16):
                if make_h and t_ >= 1:
                    stage_b2(t_ - 1)
                if t_ + 2 < 16:
                    loads(t_ + 2)
                if t_ + 1 < 16:
                    stage_a(t_ + 1)
                if make_h:
                    stage_b(t_)
                yield t_
            if make_h:
                stage_b2(15)

        def fm_block(wv, wt, cols, nkc, rhs_fn, b0, M=128, banks=None):
            for kc in range(nkc):
                for tg in range(4):
                    rhs, rt = rhs_fn(kc, tg)
                    b = b0 + tg
                    mm(bank(b)[0:M, :], wv[:, kc, cols], rhs, kc == 0, kc == nkc - 1, r=[wt] + rt, w=[PS_t[b]], sig=(kc == nkc - 1))

        def rhs_A(kc, tg):
            return A3[:, kc, tg * 512:(tg + 1) * 512], [A_t[kc][tg]]

        def tm_tile(t_, rhs_fn, nkc, lhs_fn):
            b = nb()
            for kc in range(nkc):
                lhsT, lt = lhs_fn(kc, t_)
                rhs, rt = rhs_fn(kc)
                mm(bank(b), lhsT, rhs, kc == 0, kc == nkc - 1, r=lt + rt, w=[PS_t[b]], sig=(kc == nkc - 1))
            return b

        def lhs_A(kc, t_):
            return A3[:, kc, t_ * 128:(t_ + 1) * 128], [A_t[kc][t_ // 4]]

        ev = {"i": 0}

        def evac_copy(o, i_, r, w=(), wp=()):
            ev["i"] += 1
            if ev["i"] % 2 == 0:
                act(o, i_, AF.Copy, r=r, w=w, wp=wp)
            else:
                cp("dve", o, i_, r=r, w=w, wp=wp)

        def fm_to_dram(wv, wt, cols, dst, dst_t, hslot, silu=False):
            b0 = nfm()
            fm_block(wv, wt, cols, 16, rhs_A, b0)
            o = H(hslot)
            src = PS[:, b0 * 512:(b0 + 4) * 512]
            rt = [PS_t[b0 + i] for i in range(4)]
            if silu:
                act(o, src, AF.Silu, r=rt, w=[H_t[hslot]])
            else:
                for tg in range(4):
                    evac_copy(o[:, tg * 512:(tg + 1) * 512], bank(b0 + tg), r=[PS_t[b0 + tg]], wp=[H_t[hslot]])
            dma("sp", dst, o, r=[H_t[hslot]], w=[dst_t])

        def out_phase(w_out_l, mods, ngen):
            hs = [12, 13, 14, 15]
            k = 0
            adv = 0
            cnt2 = 0
            for half in range(2):
                if half == 1:
                    while len(mods) > 4:
                        mods.pop(0)()
                    mod_flush()
                for cg in range(4):
                    if mods:
                        mods.pop(0)()
                    wv, wt = wload(w_out_l[:, cg * 512:(cg + 1) * 512], 16, 512)
                    for t_ in (range(0, 10) if half == 0 else range(10, 16)):
                        b = tm_tile(t_, lambda kc: (wv[:, kc, :], [wt]), 16, lhs_A)
                        hi = hs[k % 4]
                        k += 1
                        o = H32(hi)[:, 0:512]
                        evac_copy(o, bank(b), r=[PS_t[b]], w=[H_t[hi]])
                        dma("sp", Y2[t_ * 128:(t_ + 1) * 128, cg * 512:(cg + 1) * 512], o, r=[H_t[hi]], wp=[Y2_t[t_]])
                        if half == 1:
                            cnt2 += 1
                            if cnt2 % 3 == 0 and adv < 8:
                                next(ngen)
                                adv += 1

        def even_layer(i, mods):
            w_in = ab_w_in[i]
            CS = Fs(3)
            dma("sp", CS, ropecs[:, :], w=F_t(3))
            cqn = [H(0), H(1), H(2), H(3)]
            ckvn = [H(4), H(5)]
            KRs = 10
            KR = H(KRs)
            sq_h = [H32(8), H32(9)]
            rst_h = H32(11)
            Tt = H(11)[:, 1024:1536]

            def rope_fold(src_bank, tg, dst, dst_tok):
                tt("dve", Tt, bank(src_bank), CS[:, tg * 512:(tg + 1) * 512], ALU.mult, r=[PS_t[src_bank]] + F_t(3), w=[H_t[11]])
                bf = nb()
                mm(bank(bf)[0:64, :], foldb[:], Tt, True, True, r=[H_t[11], c_t], w=[PS_t[bf]], sig=True)
                evac_copy(dst[0:64, tg * 512:(tg + 1) * 512], bank(bf)[0:64, :], r=[PS_t[bf]], wp=[dst_tok])

            def lowrank_group(wv, wt, ncb, g_t, gcol0, outs, with_kr):
                for tg in range(4):
                    for cb in range(ncb):
                        for kc in range(16):
                            mm(bank(cb), wv[:, kc, cb * 128:(cb + 1) * 128], A3[:, kc, tg * 512:(tg + 1) * 512],
                               kc == 0, kc == 15, r=[wt, A_t[kc][tg]], w=[PS_t[cb]], sig=(kc == 15))
                    if with_kr:
                        for kc in range(16):
                            mm(bank(2), wv[:, kc, 256:384], A3[:, kc, tg * 512:(tg + 1) * 512],
                               kc == 0, kc == 15, r=[wt, A_t[kc][tg]], w=[PS_t[2]], sig=(kc == 15))
                    sbk = 4 + (tg % 2)
                    for cb in range(ncb):
                        sq = sq_h[cb % 2][:, (cb // 2 % 2) * 512:(cb // 2 % 2) * 512 + 512]
                        sq_tok = H_t[8 + cb % 2]
                        act(sq, bank(cb), AF.Square, r=[PS_t[cb]], w=[sq_tok])
                        mm(bank(sbk), onesf[:], sq, cb == 0, cb == ncb - 1, r=[sq_tok, c_t], w=[PS_t[sbk]], sig=True)
                    rst = rst_h[:, 0:512]
                    act(rst, bank(sbk), AF.Sqrt, r=[PS_t[sbk]], w=[H_t[11]], bias=EPS, scale=1.0 / (ncb * 128))
                    recip(rst, rst, r=[H_t[11]], w=[H_t[11]])
                    for cb in range(ncb):
                        o, ot = outs[cb]
                        stt("dve", o[:, tg * 512:(tg + 1) * 512], bank(cb), g_t[:, gcol0 + cb:gcol0 + cb + 1], rst, ALU.mult, ALU.mult,
                            r=[PS_t[cb], H_t[11], c_t], wp=[ot])
                    if with_kr:
                        rope_fold(2, tg, KR, H_t[KRs])

            wv, wt = wload(w_in[:, 0:512], 16, 512)
            wv1, wt1 = wload(w_in[:, 512:832], 16, 320, bufcols=384)
            lowrank_group(wv, wt, 4, gq_t, i * 4, [(cqn[k], H_t[k]) for k in range(4)], False)
            chk("g0")
            ts("dve", wv1[:, :, 320:352], wv1[:, :, 288:320], -1.0, None, ALU.mult, None, r=[wt1], wp=[wt1])
            cp("dve", wv1[:, :, 352:384], wv1[:, :, 256:288], r=[wt1], wp=[wt1])
            lowrank_group(wv1, wt1, 2, gkv_t, i * 2, [(ckvn[k], H_t[4 + k]) for k in range(2)], True)

            chk("g1")
            if mods:
                mods.pop(0)()
            wkv, wkt = wload(a_w_ukv[i], 2, 2048)
            wkv4 = wkv.rearrange("p k (h c) -> p k h c", c=256)
            def wq_view(i_):
                return WB[i_][:, 0:8192].rearrange("p (k h c) -> p k h c", k=4, c=256)

            def wq_issue(i_):
                for kc in range(4):
                    dma("pool", wq_view(i_)[:, kc, :, 0:192], a_w_uq[i, kc * 128:(kc + 1) * 128, :].rearrange("p (h c) -> p h c", c=192),
                        w=[WB_t[i_]] if kc == 0 else [], wp=[] if kc == 0 else [WB_t[i_]])

            iq = wload_generic(wq_issue)
            wqt = WB_t[iq]
            wq4 = wq_view(iq)
            ts("dve", wq4[:, :, :, 192:224], wq4[:, :, :, 160:192], -1.0, None, ALU.mult, None, r=[wqt], wp=[wqt])
            cp("dve", wq4[:, :, :, 224:256], wq4[:, :, :, 128:160], r=[wqt], wp=[wqt])

            def rhs_ckv(kc, tg):
                return ckvn[kc][:, tg * 512:(tg + 1) * 512], [H_t[4 + kc]]

            eh = [12, 13, 14, 15]
            ek = 0
            for h in range(8):
                b0 = nfm()
                fm_block(wkv, wkt, slice(h * 256, h * 256 + 128), 2, rhs_ckv, b0)
                hi = eh[ek % 4]
                ek += 1
                for tg in range(4):
                    evac_copy(H(hi)[:, tg * 512:(tg + 1) * 512], bank(b0 + tg), r=[PS_t[b0 + tg]], wp=[H_t[hi]])
                dma("sp", KN[h], H(hi), r=[H_t[hi]], w=[KN_t[h]])
            for t_ in range(16):
                for half in range(2):
                    b = tm_tile(t_, lambda kc: (wkv4[:, kc, 4 * half:4 * half + 4, 128:256], [wkt]), 2,
                                lambda kc, t2: (ckvn[kc][:, t2 * 128:(t2 + 1) * 128], [H_t[4 + kc]]))
                    hi = eh[ek % 4]
                    ek += 1
                    o = H(hi)[:, 0:512]
                    evac_copy(o, bank(b), r=[PS_t[b]], w=[H_t[hi]])
                    dma("sp", VA[t_ * 128:(t_ + 1) * 128, half * 512:(half + 1) * 512], o, r=[H_t[hi]], wp=[VA_t])
            chk("kv")
            for h in range(8):
                b0 = nfm()
                for kc in range(4):
                    for tg in range(4):
                        mm(bank(b0 + tg), wq4[:, kc, h, 0:128], cqn[kc][:, tg * 512:(tg + 1) * 512], kc == 0, kc == 3,
                           r=[wqt, H_t[kc]], w=[PS_t[b0 + tg]], sig=(kc == 3))
                hi = eh[ek % 4]
                ek += 1
                for tg in range(4):
                    evac_copy(H(hi)[:, tg * 512:(tg + 1) * 512], bank(b0 + tg), r=[PS_t[b0 + tg]], wp=[H_t[hi]])
                dma("sp", QN[h], H(hi), r=[H_t[hi]], w=[QN_t[h]])
                hi = eh[ek % 4]
                ek += 1
                for tg in range(4):
                    br = nb()
                    for kc in range(4):
                        mm(bank(br), wq4[:, kc, h, 128:256], cqn[kc][:, tg * 512:(tg + 1) * 512],
                           kc == 0, kc == 3, r=[wqt, H_t[kc]], w=[PS_t[br]], sig=(kc == 3))
                    rope_fold(br, tg, H(hi), H_t[hi])
                dma("sp", QR[h], H(hi)[0:64, :], r=[H_t[hi]], w=[QR_t[h]])

            chk("q")
            def fm_cols(c0, nblk, dst, dst_t, blk0, silu):
                nonlocal ek
                done = 0
                while done < nblk:
                    n = min(4, nblk - done)
                    if mods:
                        mods.pop(0)()
                    wv_, wt_ = wload(w_in[:, c0 + done * 128:c0 + (done + n) * 128], 16, n * 128)
                    for cb in range(n):
                        hi = eh[ek % 4]
                        ek += 1
                        blk = blk0 + done + cb
                        fm_to_dram(wv_, wt_, slice(cb * 128, (cb + 1) * 128), dst[blk], dst_t[blk], hi, silu=silu)
                    done += n

            fm_cols(832, 8, BQ, BQ_t, 0, False)
            fm_cols(1856, 8, BK, BK_t, 0, False)
            for cg in range(2):
                if mods:
                    mods.pop(0)()
                wv_, wt_ = wload(w_in[:, 2880 + cg * 512:2880 + (cg + 1) * 512], 16, 512)
                for t_ in range(16):
                    b = tm_tile(t_, lambda kc: (wv_[:, kc, :], [wt_]), 16, lhs_A)
                    hi = eh[ek % 4]
                    ek += 1
                    o = H(hi)[:, 0:512]
                    evac_copy(o, bank(b), r=[PS_t[b]], w=[H_t[hi]])
                    dma("sp", BV[t_ * 128:(t_ + 1) * 128, cg * 512:(cg + 1) * 512], o, r=[H_t[hi]], wp=[BV_t])
            fm_cols(3904, 16, GT, GT_t, 0, True)

            chk("inproj")
            attn(i)

        def attn_head(kind, i, h, par, loads_only=False, compute_only=False):
            base = par * 5
            qs, qrs, ks, vs, gs = base, base + 1, base + 2, base + 3, base + 4
            chunk = h if kind == "a" else 8 + h
            if not compute_only:
                if kind == "a":
                    dma("sp", H(qs), QN[h], r=[QN_t[h]], w=[H_t[qs]])
                    dma("sp", H(qrs)[0:64, :], QR[h], r=[QR_t[h]], w=[H_t[qrs]])
                    dma("sp", H(ks), KN[h], r=[KN_t[h]], w=[H_t[ks]])
                    dma("sp", H(vs).rearrange("p (t d) -> p t d", d=128), VA[:, h * 128:(h + 1) * 128].rearrange("(t p) d -> p t d", p=128),
                        r=[VA_t], w=[H_t[vs]])
                else:
                    dma("sp", H(qs), BQ[h], r=[BQ_t[h]], w=[H_t[qs]])
                    dma("sp", H(ks), BK[h], r=[BK_t[h]], w=[H_t[ks]])
                    dma("sp", H(vs).rearrange("p (t d) -> p t d", d=128), BV[:, h * 128:(h + 1) * 128].rearrange("(t p) d -> p t d", p=128),
                        r=[BV_t], w=[H_t[vs]])
                    dma("sp", bt4[par][:], tb4[i, h], w=[bt_t[par]])
                    dma("sp", bt3[par][:], tb3[i, h], wp=[bt_t[par]])
                dma("sp", H(gs), GT[chunk], r=[GT_t[chunk]], w=[H_t[gs]])
            if loads_only:
                return
            q_, k_, v_, g_ = H(qs), H(ks), H(vs).rearrange("p (t d) -> p t d", d=128), H(gs)
            qr_ = H(qrs)
            KR = H(10)
            PT_s = [11, 12]
            tmp_s = [13, 14]
            scale = (192.0 ** -0.5) if kind == "a" else (128.0 ** -0.5)
            cb_ = crep_t[:, i * 8 + h:i * 8 + h + 1]
            def group_tiles(g):
                tiles = []
                if kind == "a":
                    for kt in range(4 * g + 4):
                        c0 = 0 if kt < 4 * g else (kt - 4 * g) * 128
                        tiles.append((kt, c0, 512 - c0, "diag" if kt >= 4 * g else "plain"))
                else:
                    for u in [4, 5, 6, 7, 0, 1, 2, 3]:
                        t_ = 4 * g - 4 + u
                        if t_ < 0:
                            continue
                        if u < 4:
                            tiles.append((t_, 0, 128 * (u + 1), "u%d" % u))
                        else:
                            tiles.append((t_, 128 * (u - 4), 512 - 128 * (u - 4), "u%d" % u))
                return tiles

            jobs = []
            for g in range(4):
                tl = group_tiles(g)
                for idx, tile in enumerate(tl):
                    jobs.append((g, idx, len(tl), tile))
            ptl = {}

            def s_stage(n):
                g, idx, nt, (kt, c0, N, kindt) = jobs[n]
                k_i = att["pt"] % 8
                att["pt"] += 1
                sbk = 4 + att["s"] % 4
                att["s"] += 1
                pt_tok = PT_t[k_i]
                pt = H(PT_s[k_i // 4])[:, (k_i % 4) * 512:(k_i % 4) * 512 + 512]
                q0 = g * 512 + c0
                if kind == "a":
                    mm(bank(sbk)[:, 0:N], k_[:, kt * 128:(kt + 1) * 128], q_[:, q0:q0 + N], True, False,
                       r=[H_t[ks], H_t[qs]], w=[PS_t[sbk]], sig=False)
                    mm(bank(sbk)[:, 0:N], KR[0:64, kt * 128:(kt + 1) * 128], qr_[0:64, q0:q0 + N], False, True,
                       r=[H_t[10], H_t[qrs]], w=[PS_t[sbk]], sig=True)
                    act(pt[:, 0:N], bank(sbk)[:, 0:N], AF.Exp, r=[PS_t[sbk]], w=[pt_tok], scale=scale)
                    if kindt == "diag":
                        mset("dve", pt[64:128, 0:64], 0.0, w=[pt_tok])
                else:
                    mm(bank(sbk)[:, 0:N], k_[:, kt * 128:(kt + 1) * 128], q_[:, q0:q0 + N], True, True,
                       r=[H_t[ks], H_t[qs]], w=[PS_t[sbk]], sig=True)
                    u = int(kindt[1:])
                    if u >= 3:
                        nbias = min(256, N) if u >= 4 else 128
                        btile = bt4[par] if u >= 4 else bt3[par]
                        ti = att["tmp"] % 2
                        att["tmp"] += 1
                        tmp = H32(tmp_s[ti])[:, 0:nbias]
                        stt("dve", tmp, bank(sbk)[:, 0:nbias], scale, btile[:, 0:nbias], ALU.mult, ALU.add,
                            r=[PS_t[sbk], bt_t[par]], w=[H_t[tmp_s[ti]]])
                        act(pt[:, 0:nbias], tmp, AF.Exp, r=[H_t[tmp_s[ti]]], w=[pt_tok])
                        if N > nbias:
                            act(pt[:, nbias:N], bank(sbk)[:, nbias:N], AF.Exp, r=[PS_t[sbk], c_t], wp=[pt_tok], scale=scale, bias=cb_)
                    else:
                        act(pt[:, 0:N], bank(sbk)[:, 0:N], AF.Exp, r=[PS_t[sbk], c_t], w=[pt_tok], scale=scale, bias=cb_)
                    if u < 4:
                        mset("dve", pt[0:64, N - 64:N], 0.0, w=[pt_tok])
                ptl[n] = (pt, pt_tok)

            def pv_stage(n):
                g, idx, nt, (kt, c0, N, kindt) = jobs[n]
                ob = g % 2
                sb_ = 2 + g % 2
                pt, pt_tok = ptl.pop(n)
                mm(bank(ob)[:, c0:c0 + N], v_[:, kt, :], pt[:, 0:N], idx == 0, idx == nt - 1,
                   r=[H_t[vs], pt_tok], w=[PS_t[ob]], sig=False)
                mm(bank(sb_)[:, c0:c0 + N], onesb[:], pt[:, 0:N], idx == 0, idx == nt - 1,
                   r=[c_t, pt_tok], w=[PS_t[sb_]], sig=True)
                if idx == nt - 1:
                    ti = att["tmp"] % 2
                    att["tmp"] += 1
                    rec = H32(tmp_s[ti])[:, 0:512]
                    o32 = H32(tmp_s[ti])[:, 512:1024]
                    act(rec, bank(sb_), AF.Ln, r=[PS_t[sb_]], w=[H_t[tmp_s[ti]]])
                    act(rec, rec, AF.Exp, r=[H_t[tmp_s[ti]]], w=[H_t[tmp_s[ti]]], scale=-1.0)
                    tt("dve", o32, bank(ob), rec, ALU.mult, r=[PS_t[ob], H_t[tmp_s[ti]]], w=[H_t[tmp_s[ti]]])
                    tt("pool", A3[:, chunk, g * 512:(g + 1) * 512], o32, g_[:, g * 512:(g + 1) * 512], ALU.mult,
                       r=[H_t[tmp_s[ti]], H_t[gs]], w=[A_t[chunk][g]])

            LA = 3
            for n in range(len(jobs) + LA):
                if n < len(jobs):
                    s_stage(n)
                if n >= LA:
                    pv_stage(n - LA)

        att = {"pt": 0, "s": 0, "tmp": 0}
        PT_t = [Tok("PT%d" % k) for k in range(8)]

        def tok_split(parent, children):
            for ch in children:
                ch.w = dict(parent.w)
                ch.r = dict(parent.r)
                ch.pr = dict(parent.pr)
                ch.closed = parent.closed

        def tok_join(parent, children):
            w, r, pr = {}, {}, {}
            for ch in children:
                _merge(w, ch.w)
                _merge(r, ch.r)
                _merge(r, ch.w)
                _merge(pr, ch.pr)
            parent.w, parent.r, parent.pr, parent.closed = w, r, pr, True

        def attn(i):
            tok_split(H_t[11], PT_t[0:4])
            tok_split(H_t[12], PT_t[4:8])
            attn_inner(i)
            tok_join(H_t[11], PT_t[0:4])
            tok_join(H_t[12], PT_t[4:8])

        def attn_inner(i):
            heads = [("a", h) for h in range(8)] + [("b", h) for h in range(8)]
            attn_head(heads[0][0], i, heads[0][1], 0, loads_only=True)
            for n, (kind, h) in enumerate(heads):
                if n + 1 < len(heads):
                    attn_head(heads[n + 1][0], i, heads[n + 1][1], (n + 1) % 2, loads_only=True)
                attn_head(kind, i, h, n % 2, compute_only=True)

        def odd_layer(i, mods):
            w_in = sg_w_in[i]
            eh = [12, 13, 14, 15]
            ek = 0
            for cg in range(4):
                if mods:
                    mods.pop(0)()
                wu, wut = wload(w_in[:, cg * 512:(cg + 1) * 512], 16, 512)
                wg, wgt = wload(w_in[:, 4096 + cg * 512:4096 + (cg + 1) * 512], 16, 512)
                for cb in range(4):
                    fc = cg * 4 + cb
                    hu = eh[ek % 4]
                    hg = eh[(ek + 1) % 4]
                    ek += 2
                    b0 = nfm()
                    fm_block(wu, wut, slice(cb * 128, (cb + 1) * 128), 16, rhs_A, b0)
                    for tg in range(4):
                        cp("dve", H(hu)[:, tg * 512:(tg + 1) * 512], bank(b0 + tg), r=[PS_t[b0 + tg]], wp=[H_t[hu]])
                    b1 = nfm()
                    fm_block(wg, wgt, slice(cb * 128, (cb + 1) * 128), 16, rhs_A, b1)
                    act(H(hg), PS[:, b1 * 512:(b1 + 4) * 512], AF.Silu, r=[PS_t[b1 + k] for k in range(4)], w=[H_t[hg]])
                    tt("pool", H(hu), H(hu), H(hg), ALU.mult, r=[H_t[hu], H_t[hg]], w=[H_t[hu]])
                    dma("sp", UG[fc], H(hu), r=[H_t[hu]], w=[UG_t[fc]])
            for cg in range(4):
                if mods:
                    mods.pop(0)()
                wv_, wt_ = wload(w_in[:, 2048 + cg * 512:2048 + (cg + 1) * 512], 16, 512)
                for t_ in range(16):
                    b = tm_tile(t_, lambda kc: (wv_[:, kc, :], [wt_]), 16, lhs_A)
                    hi = eh[ek % 4]
                    ek += 1
                    o = H32(hi)[:, 0:512]
                    evac_copy(o, bank(b), r=[PS_t[b]], w=[H_t[hi]])
                    dma("sp", VR[t_ * 128:(t_ + 1) * 128, cg * 512:(cg + 1) * 512], o, r=[H_t[hi]], wp=[VR_t[t_]])
            wsb = H(0)[:, 0:1024].rearrange("p (g j) -> p g j", j=128)
            wsT = H(1)[:, 0:1024].rearrange("p (g i) -> p g i", i=128)
            E = Fs(1)
            Ev = E.rearrange("p (c i) -> p c i", i=128)
            dma("pool", wsb, sg_w_s[i].rearrange("g i j -> i g j"), w=[H_t[0]])
            mset("dve", wsb[0:64, :, 64:128], 0.0, wp=[H_t[0]])
            for half in range(2):
                b = nb()
                for j in range(4):
                    g = half * 4 + j
                    mm(bank(b)[:, j * 128:(j + 1) * 128], wsb[:, g, :], identb[:], True, True, r=[H_t[0], c_t], w=[PS_t[b]], sig=(j == 3))
                evac_copy(H(1)[:, half * 512:(half + 1) * 512], bank(b), r=[PS_t[b]], wp=[H_t[1]])
            dma("sp", bsrow[:], sg_b_s[i:i + 1, :], w=[bs_t])
            rs_b = [nb(), nb()]
            bs_b = [nb(), nb()]
            for half in range(2):
                for j in range(4):
                    g = half * 4 + j
                    mm(bank(rs_b[half])[:, j * 128:(j + 1) * 128], onesb[:], wsT[:, g, :], True, True, r=[H_t[1], c_t],
                       w=[PS_t[rs_b[half]]], sig=(j == 3))
                mm(bank(bs_b[half]), onesf[0:1, :], bsrow[0:1, half * 512:(half + 1) * 512], True, True, r=[bs_t, c_t],
                   w=[PS_t[bs_b[half]]], sig=True)
            bsr = H32(4)
            for half in range(2):
                cp("dve", bsr[:, half * 512:(half + 1) * 512], bank(bs_b[half]), r=[PS_t[bs_b[half]]], wp=[H_t[4]])
            for fc in range(16):
                g = fc // 2
                stt("dve", Ev[:, fc, :], bank(rs_b[g // 4])[:, (g % 4) * 128:(g % 4 + 1) * 128], lnb_t[:, i * 16 + fc:i * 16 + fc + 1],
                    bsr[:, g * 128:(g + 1) * 128], ALU.mult, ALU.add, r=[PS_t[rs_b[g // 4]], H_t[4], c_t], wp=F_t(1))

            def load_v(n):
                p = n % 2
                dma("sp", Fs(2 + p), VR[n * 128:(n + 1) * 128, :], r=[VR_t[n]], w=F_t(2 + p))

            def load_ug(n):
                p = n % 2
                dma("sp", H(8 + p).rearrange("p (c t) -> p c t", t=128), UG[:, :, n * 128:(n + 1) * 128].rearrange("c p t -> p c t"),
                    r=UG_t, w=[H_t[8 + p]])

            def s1(n):
                p = n % 2
                vt = Fs(2 + p)
                si = rr["st"] % 8
                rr["st"] += 1
                c = si * 8
                s_ = [st_t[si]]
                act(junk[:], vt, AF.Identity, r=F_t(2 + p), w=s_, accum=st[:, c:c + 1])
                act(junk[:], vt, AF.Square, r=F_t(2 + p), wp=s_, accum=st[:, c + 1:c + 2])
                ts("dve", st[:, c + 2:c + 3], st[:, c:c + 1], 1.0 / D, None, ALU.mult, None, r=s_, w=s_)
                tt("dve", st[:, c + 3:c + 4], st[:, c + 2:c + 3], st[:, c + 2:c + 3], ALU.mult, r=s_, w=s_)
                stt("dve", st[:, c + 4:c + 5], st[:, c + 1:c + 2], 1.0 / D, st[:, c + 3:c + 4], ALU.mult, ALU.subtract, r=s_, w=s_)
                act(st[:, c + 5:c + 6], st[:, c + 4:c + 5], AF.Sqrt, r=s_, w=s_, bias=EPS, scale=1.0)
                recip(st[:, c + 6:c + 7], st[:, c + 5:c + 6], r=s_, w=s_)
                stt("dve", st[:, c + 7:c + 8], st[:, c + 2:c + 3], -1.0, st[:, c + 6:c + 7], ALU.mult, ALU.mult, r=s_, w=s_)
                vn = H(10 + p)
                act(vn, vt, AF.Identity, r=F_t(2 + p) + s_, w=[H_t[10 + p]], bias=st[:, c + 7:c + 8], scale=st[:, c + 6:c + 7])

            def s2(n):
                p = n % 2
                vn = H(10 + p)
                ug = H(8 + p).rearrange("p (c t) -> p c t", t=128)
                for q in range(4):
                    b = nb()
                    for j in range(4):
                        fc = 4 * q + j
                        mm(bank(b)[:, j * 128:(j + 1) * 128], vn[:, fc * 128:(fc + 1) * 128], wsT[:, fc // 2, :], True, True,
                           r=[H_t[10 + p], H_t[1]], w=[PS_t[b]], sig=(j == 3))
                    ti = att["tmp"] % 2
                    att["tmp"] += 1
                    tmp = H32(13 + ti)[:, 0:512]
                    for j in range(4):
                        fc = 4 * q + j
                        stt("dve", tmp[:, j * 128:(j + 1) * 128], bank(b)[:, j * 128:(j + 1) * 128], lng_t[:, i * 16 + fc:i * 16 + fc + 1],
                            Ev[:, fc, :], ALU.mult, ALU.add, r=[PS_t[b], c_t] + F_t(1), wp=[H_t[13 + ti]])
                    tt("pool", A3[:, 4 * q:4 * q + 4, n * 128:(n + 1) * 128], tmp.rearrange("p (c t) -> p c t", t=128), ug[:, 4 * q:4 * q + 4, :], ALU.mult,
                       r=[H_t[13 + ti], H_t[8 + p]], wp=[A_t[4 * q + j][n // 4] for j in range(4)])

            load_v(0)
            load_v(1)
            load_ug(0)
            load_ug(1)
            s1(0)
            for n in range(16):
                if n + 1 < 16:
                    s1(n + 1)
                s2(n)
                if n + 2 < 16:
                    load_v(n + 2)
                    load_ug(n + 2)

        def whole():
            consts()
            chk("const")
            for cg in range(8):
                mod_group(0, cg)
                chk("mod%d" % cg)
            mod_flush()
            chk("mod")
            for _ in norm_gen(0, x_in, XS[0], False, True):
                pass
            chk("norm0")
            Xcur = x_in
            for l in range(depth):
                mods = []
                if l == 0:
                    mods = [(lambda cg=cg: mod_group(0, cg)) for cg in range(8, 12)]
                if l + 1 < depth:
                    mods += [(lambda l1=l + 1, cg=cg: mod_group(l1, cg)) for cg in range(12)]
                i = l // 2
                if l % 2 == 0:
                    even_layer(i, mods)
                    w_out_l = ab_w_out[i]
                else:
                    odd_layer(i, mods)
                    w_out_l = sg_w_out[i]
                chk("mix%d" % l)
                Xnext = out if l + 1 == depth else XS[(l + 1) % 2]
                ngen = norm_gen(l + 1, Xcur, Xnext, True, l + 1 < depth)
                out_phase(w_out_l, mods, ngen)
                chk("out%d" % l)
                while mods:
                    mods.pop(0)()
                mod_flush()
                for _ in ngen:
                    pass
                Xcur = Xnext

        def reset_all():
            P.reset()
            for t in Tok.ALL:
                t.reset()
            for d_ in (rr, ev, att):
                for k_ in d_:
                    d_[k_] = 0
            plan["k"] = 0
            plan["issued"] = 0

        for pass_ in range(2):
            try:
                whole()
            except _Stop:
                pass
            if pass_ == 0:
                plan["replay"] = plan["specs"]
                reset_all()
        fin = {}
        for t in X_t[id(out)]:
            _merge(fin, t.w)
        P.stream["sp"].append((fin, None, None))

        with nc.Block() as block:
            @block.tensor
            def _(e):
                P.emit("pe", e)

            @block.scalar
            def _(e):
                P.emit("act", e)

            @block.vector
            def _(e):
                P.emit("dve", e)

            @block.gpsimd
            def _(e):
                P.emit("pool", e)

            @block.sync
            def _(e):
                P.emit("sp", e)
        build.stats = {e: len(s) for e, s in P.stream.items()}
    return nc


def _host_inputs(inputs, b):
    f = np.float32
    x = np.ascontiguousarray(inputs["x"][b], dtype=f)
    c = np.asarray(inputs["c"][b], dtype=f)

    def col(v):
        return np.ascontiguousarray(np.asarray(v, dtype=f).reshape(-1, 128).T)

    def cols(m):
        return np.ascontiguousarray(np.concatenate([col(m[l]) for l in range(m.shape[0])], axis=1))

    tab = np.asarray(inputs["b_rel_bias"], dtype=f)
    kk = np.arange(64)[:, None]
    cc = np.arange(320)[None, :]
    idx = np.minimum(cc - kk + 128, 256)
    TB = tab[:, :, idx]
    tb4 = np.full((2, 8, 128, 256), NEG, dtype=f)
    tb4[:, :, 0:64, :] = TB[:, :, :, 0:256]
    tb4[:, :, 64:128, 64:256] = TB[:, :, :, 0:192]
    tb3 = np.empty((2, 8, 128, 128), dtype=f)
    tb3[:, :, 0:64, :] = TB[:, :, :, 128:256]
    tb3[:, :, 64:128, :] = TB[:, :, :, 64:192]
    crep = np.ascontiguousarray(np.broadcast_to(tab[:, :, 256].reshape(1, 16), (128, 16)), dtype=f)
    half = 32
    freqs = (10000.0 ** (-np.arange(half, dtype=f) / f(half))).astype(f)
    ang = (np.arange(S, dtype=f)[None, :] * freqs[:, None]).astype(f)
    cs = np.concatenate([np.cos(ang), np.cos(ang), np.sin(ang), np.sin(ang)], 0).astype(f)
    d = {
        "x": x, "ccol": col(c),
        "w_mod": inputs["w_mod"], "b_mod": inputs["b_mod"],
        "gpre_col": cols(np.asarray(inputs["g_pre"])), "g_post": inputs["g_post"],
        "ab_w_in": inputs["ab_w_in"], "gq_col": cols(np.asarray(inputs["a_g_q"])), "a_w_uq": inputs["a_w_uq"],
        "gkv_col": cols(np.asarray(inputs["a_g_kv"])), "a_w_ukv": inputs["a_w_ukv"],
        "tb4": tb4, "tb3": tb3, "crep": crep, "ab_w_out": inputs["ab_w_out"],
        "sg_w_in": inputs["sg_w_in"], "lng_col": cols(np.asarray(inputs["sg_ln_g"])), "lnb_col": cols(np.asarray(inputs["sg_ln_b"])),
        "sg_w_s": inputs["sg_w_s"], "sg_b_s": np.asarray(inputs["sg_b_s"], dtype=f).reshape(2, 1024), "sg_w_out": inputs["sg_w_out"],
        "ident": np.eye(128, dtype=f), "ropecs": cs,
    }
    return {k: np.ascontiguousarray(np.asarray(v, dtype=f)) for k, v in d.items()}


_NC = {}


def kernel(**inputs):
    inputs = {k: np.asarray(v) for k, v in inputs.items()}
    if DEPTH not in _NC:
        _NC[DEPTH] = build(DEPTH)
    nc = _NC[DEPTH]
    shared = None
    in_maps = []
    for b in range(8):
        m = _host_inputs(inputs, b) if shared is None else dict(shared)
        if shared is None:
            shared = m
        else:
            m["x"] = np.ascontiguousarray(inputs["x"][b], dtype=np.float32)
            m["ccol"] = np.ascontiguousarray(np.asarray(inputs["c"][b], dtype=np.float32).reshape(-1, 128).T)
        in_maps.append(m)
    res = run_bass_kernel_spmd(nc, in_maps, core_ids=list(range(8)))
    return np.stack([np.asarray(r["out"]) for r in res.results], axis=0).astype(np.float32)
```
